# Optimizing a Trainium2 kernel written in Bass

```python
import jax, jax.numpy as jnp
from jax import lax
import numpy as np


D_MODEL = 4096
BATCH = 8
SEQ = 2048
DEPTH = 1
DEC_BATCH = 8
DEC_SEQ = 64
PAST_LEN = 1024

CHUNK = 64
Q_BLOCK = 128
PLE_DIM = 256
EPS = 1e-6
SB_HEADS = 16
SB_HEAD_DIM = 128
D_SB = SB_HEADS * SB_HEAD_DIM
SSD_HEADS = 32
SSD_HEAD_DIM = 64
D_SSD = SSD_HEADS * SSD_HEAD_DIM
SSD_GROUPS = 8
SSD_STATE = 128
SSD_HPG = SSD_HEADS // SSD_GROUPS
CONV_WIDTH = 4
CONV_CH = D_SSD + 2 * SSD_GROUPS * SSD_STATE
D_MIX = D_SB + D_SSD
N_IN = 4 * D_SB + CONV_CH + D_SSD + SSD_HEADS
_SPLITS = (D_SB, 2 * D_SB, 3 * D_SB, 4 * D_SB, 4 * D_SB + CONV_CH, 4 * D_SB + CONV_CH + D_SSD)

kernel_name = 'hymba_stickbreak_ssd_stream_step'


def rmsnorm(x, w):
    xf = x.astype(jnp.float32)
    y = xf * lax.rsqrt(jnp.mean(xf * xf, axis=-1, keepdims=True) + EPS)
    return (y * w.astype(jnp.float32)).astype(x.dtype)


def stick_breaking(q, k_all, v_all, q_pos, k_pos):
    z = jnp.einsum('bqhd,bkhd->bhqk', q.astype(jnp.float32), k_all.astype(jnp.float32)) * (SB_HEAD_DIM ** -0.5)
    mask = k_pos[None, :] < q_pos[:, None]
    log_stay = jnp.where(mask, jax.nn.log_sigmoid(-z), 0.0)
    later = lax.cumsum(log_stay, axis=3, reverse=True) - log_stay
    att = jnp.where(mask, jnp.exp(jax.nn.log_sigmoid(z) + later), 0.0)
    return jnp.einsum('bhqk,bkhd->bqhd', att, v_all.astype(jnp.float32))


def ssd_chunked(x, dt, a, bm, cm, h0):
    bsz, t = x.shape[0], x.shape[1]
    nc = -(-t // CHUNK)
    pad = nc * CHUNK - t
    padt = lambda arr: jnp.pad(arr, [(0, 0), (0, pad)] + [(0, 0)] * (arr.ndim - 2))
    x, dt, bm, cm = padt(x), padt(dt), padt(bm), padt(cm)
    L = CHUNK
    xc = x.reshape(bsz, nc, L, SSD_GROUPS, SSD_HPG, SSD_HEAD_DIM)
    dtc = dt.reshape(bsz, nc, L, SSD_GROUPS, SSD_HPG)
    bc = bm.reshape(bsz, nc, L, SSD_GROUPS, SSD_STATE)
    cc = cm.reshape(bsz, nc, L, SSD_GROUPS, SSD_STATE)
    acum = jnp.cumsum(dtc * a.reshape(SSD_GROUPS, SSD_HPG), axis=2)
    diff = acum[:, :, :, None] - acum[:, :, None, :]
    causal = jnp.tril(jnp.ones((L, L), dtype=bool))[:, :, None, None]
    decay = jnp.exp(jnp.where(causal, diff, -jnp.inf))
    cb = jnp.einsum('bclgn,bcsgn->bclsg', cc, bc)
    wts = cb[..., None] * decay * dtc[:, :, None]
    y_intra = jnp.einsum('bclsgh,bcsghp->bclghp', wts, xc)
    decay_end = jnp.exp(acum[:, :, -1:] - acum) * dtc
    states = jnp.einsum('bclgh,bclgn,bclghp->bcghpn', decay_end, bc, xc)
    block_decay = jnp.exp(acum[:, :, -1])

    def step(h, inp):
        st, bd, c_blk, ac = inp
        y_in = jnp.einsum('blgn,bghpn->blghp', c_blk, h) * jnp.exp(ac)[..., None]
        h = bd[..., None, None] * h + st
        return h, y_in

    h_init = h0.reshape(bsz, SSD_GROUPS, SSD_HPG, SSD_HEAD_DIM, SSD_STATE)
    h_fin, y_inter = lax.scan(step, h_init, (jnp.moveaxis(states, 1, 0), jnp.moveaxis(block_decay, 1, 0),
                                             jnp.moveaxis(cc, 1, 0), jnp.moveaxis(acum, 1, 0)))
    y = (y_intra + jnp.moveaxis(y_inter, 0, 1)).reshape(bsz, nc * L, SSD_HEADS, SSD_HEAD_DIM)[:, :t]
    return y, h_fin.reshape(bsz, SSD_HEADS, SSD_HEAD_DIM, SSD_STATE)


def hybrid_mixer(h, k_past, v_past, conv_past, ssd_past, w_in, conv_w, conv_b, dt_bias, a_log, d_skip,
                 sb_norm, ssd_norm, w_out):
    bsz, t, _ = h.shape
    n_past = k_past.shape[1]
    proj = h @ w_in
    q, k, v, g_sb, xbc, z_ssd, dt_raw = jnp.split(proj, _SPLITS, axis=-1)
    q = q.reshape(bsz, t, SB_HEADS, SB_HEAD_DIM)
    k = k.reshape(bsz, t, SB_HEADS, SB_HEAD_DIM)
    v = v.reshape(bsz, t, SB_HEADS, SB_HEAD_DIM)
    k_all = jnp.concatenate([k_past.astype(k.dtype), k], axis=1)
    v_all = jnp.concatenate([v_past.astype(v.dtype), v], axis=1)
    k_pos = jnp.arange(n_past + t)
    qb = min(Q_BLOCK, t)
    nblk = -(-t // qb)
    qpad = nblk * qb - t
    q_blocks = jnp.pad(q, ((0, 0), (0, qpad), (0, 0), (0, 0))).reshape(
        bsz, nblk, qb, SB_HEADS, SB_HEAD_DIM).swapaxes(0, 1)
    q_pos = (n_past + jnp.arange(nblk * qb)).reshape(nblk, qb)
    o = lax.map(lambda blk: stick_breaking(blk[0], k_all, v_all, blk[1], k_pos), (q_blocks, q_pos))
    o = o.swapaxes(0, 1).reshape(bsz, nblk * qb, D_SB)[:, :t].astype(h.dtype)
    o_sb = rmsnorm(o * jax.nn.silu(g_sb), sb_norm)
    xbc_ext = jnp.concatenate([conv_past.astype(xbc.dtype), xbc], axis=1)
    conv = conv_b
    for j in range(CONV_WIDTH):
        conv = conv + conv_w[j] * xbc_ext[:, j:j + t]
    xbc_act = jax.nn.silu(conv)
    x_s, b_s, c_s = jnp.split(xbc_act, [D_SSD, D_SSD + SSD_GROUPS * SSD_STATE], axis=-1)
    dt = jax.nn.softplus(dt_raw.astype(jnp.float32) + dt_bias.astype(jnp.float32))
    a = -jnp.exp(a_log.astype(jnp.float32))
    x_h = x_s.astype(jnp.float32).reshape(bsz, t, SSD_HEADS, SSD_HEAD_DIM)
    y, ssd_new = ssd_chunked(x_h, dt, a,
                             b_s.astype(jnp.float32).reshape(bsz, t, SSD_GROUPS, SSD_STATE),
                             c_s.astype(jnp.float32).reshape(bsz, t, SSD_GROUPS, SSD_STATE),
                             ssd_past.astype(jnp.float32))
    y = (y + d_skip.astype(jnp.float32)[:, None] * x_h).reshape(bsz, t, D_SSD).astype(h.dtype)
    o_ssd = rmsnorm(y * jax.nn.silu(z_ssd), ssd_norm)
    out = jnp.concatenate([o_sb, o_ssd], axis=-1) @ w_out
    new_conv = xbc_ext[:, -(CONV_WIDTH - 1):]
    return out, k, v, new_conv, ssd_new.astype(h.dtype)


def trunk_layer(x, p, k_past, v_past, conv_past, ssd_past, norm_pre, norm_post, w_in, conv_w, conv_b,
                dt_bias, a_log, d_skip, sb_norm, ssd_norm, w_out, w_ple_gate, w_ple_proj, ple_norm):
    h = rmsnorm(x, norm_pre)
    m, k_new, v_new, conv_new, ssd_new = hybrid_mixer(h, k_past, v_past, conv_past, ssd_past, w_in, conv_w,
                                                      conv_b, dt_bias, a_log, d_skip, sb_norm, ssd_norm, w_out)
    x = x + rmsnorm(m.astype(x.dtype), norm_post)
    e = p.astype(x.dtype) @ w_ple_proj
    gate = jax.nn.sigmoid(x @ w_ple_gate)
    x = x + rmsnorm(gate * e, ple_norm)
    return x, k_new, v_new, conv_new, ssd_new


def setup_inputs(seed: int = 0) -> dict:
    key = jax.random.key(seed)
    ks = jax.random.split(key, 24)
    f32 = jnp.float32
    nrm = lambda k, shape, s=1.0: jax.random.normal(k, shape, f32) * s
    gain = lambda k, n: 1.0 + 0.01 * jax.random.normal(k, (DEPTH, n), f32)
    dt0 = jnp.exp(jax.random.uniform(ks[13], (DEPTH, SSD_HEADS), f32) * (jnp.log(0.1) - jnp.log(0.001)) + jnp.log(0.001))
    return {
        'x_prompt': nrm(ks[0], (BATCH, SEQ, D_MODEL)),
        'x_sample': nrm(ks[1], (DEC_BATCH, DEC_SEQ, D_MODEL)),
        'cache_k': nrm(ks[2], (DEPTH, DEC_BATCH, PAST_LEN, SB_HEADS, SB_HEAD_DIM)),
        'cache_v': nrm(ks[3], (DEPTH, DEC_BATCH, PAST_LEN, SB_HEADS, SB_HEAD_DIM)),
        'state_conv': nrm(ks[4], (DEPTH, DEC_BATCH, CONV_WIDTH - 1, CONV_CH)),
        'state_ssd': nrm(ks[5], (DEPTH, DEC_BATCH, SSD_HEADS, SSD_HEAD_DIM, SSD_STATE), 0.5),
        'p_prompt': nrm(ks[6], (DEPTH, BATCH, SEQ, PLE_DIM)),
        'p_sample': nrm(ks[7], (DEPTH, DEC_BATCH, DEC_SEQ, PLE_DIM)),
        'norm_pre': gain(ks[8], D_MODEL),
        'norm_post': gain(ks[9], D_MODEL),
        'w_in': nrm(ks[10], (DEPTH, D_MODEL, N_IN), D_MODEL ** -0.5),
        'conv_w': nrm(ks[11], (DEPTH, CONV_WIDTH, CONV_CH), CONV_WIDTH ** -0.5),
        'conv_b': nrm(ks[12], (DEPTH, CONV_CH), 0.01),
        'dt_bias': dt0 + jnp.log(-jnp.expm1(-dt0)),
        'a_log': jnp.log(jax.random.uniform(ks[14], (DEPTH, SSD_HEADS), f32, 1.0, 16.0)),
        'd_skip': gain(ks[15], SSD_HEADS),
        'sb_norm': gain(ks[16], D_SB),
        'ssd_norm': gain(ks[17], D_SSD),
        'w_out': nrm(ks[18], (DEPTH, D_MIX, D_MODEL), D_MIX ** -0.5),
        'w_ple_gate': nrm(ks[19], (DEPTH, D_MODEL, D_MODEL), D_MODEL ** -0.5),
        'w_ple_proj': nrm(ks[20], (DEPTH, PLE_DIM, D_MODEL), PLE_DIM ** -0.5),
        'ple_norm': gain(ks[21], D_MODEL),
    }


def reference(x_prompt, x_sample, cache_k, cache_v, state_conv, state_ssd, p_prompt, p_sample,
              norm_pre, norm_post, w_in, conv_w, conv_b, dt_bias, a_log, d_skip, sb_norm, ssd_norm,
              w_out, w_ple_gate, w_ple_proj, ple_norm):
    yp, ys = x_prompt, x_sample
    kp_l, vp_l, cp_l, sp_l, ks_l, vs_l, cs_l, ss_l = [], [], [], [], [], [], [], []
    for i in range(DEPTH):
        lw = (norm_pre[i], norm_post[i], w_in[i], conv_w[i], conv_b[i], dt_bias[i], a_log[i], d_skip[i],
              sb_norm[i], ssd_norm[i], w_out[i], w_ple_gate[i], w_ple_proj[i], ple_norm[i])
        dt_p = x_prompt.dtype
        yp, kp, vp, cp, sp = trunk_layer(
            yp, p_prompt[i],
            jnp.zeros((BATCH, 0, SB_HEADS, SB_HEAD_DIM), dt_p),
            jnp.zeros((BATCH, 0, SB_HEADS, SB_HEAD_DIM), dt_p),
            jnp.zeros((BATCH, CONV_WIDTH - 1, CONV_CH), dt_p),
            jnp.zeros((BATCH, SSD_HEADS, SSD_HEAD_DIM, SSD_STATE), dt_p), *lw)
        ys, ks, vs, cs, ss = trunk_layer(ys, p_sample[i], cache_k[i], cache_v[i], state_conv[i], state_ssd[i], *lw)
        kp_l.append(kp); vp_l.append(vp); cp_l.append(cp); sp_l.append(sp)
        ks_l.append(ks); vs_l.append(vs); cs_l.append(cs); ss_l.append(ss)
    k_prompt, v_prompt = jnp.stack(kp_l), jnp.stack(vp_l)
    conv_prompt, ssd_prompt = jnp.stack(cp_l), jnp.stack(sp_l)
    k_sample, v_sample = jnp.stack(ks_l), jnp.stack(vs_l)
    conv_sample, ssd_sample = jnp.stack(cs_l), jnp.stack(ss_l)
    return (yp, ys, k_prompt, v_prompt, conv_prompt, ssd_prompt, k_sample, v_sample, conv_sample, ssd_sample)
```

```python
import contextlib
import numpy as np
import concourse.bass as bass
import concourse.mybir as mybir
from concourse.bass_utils import run_bass_kernel_spmd

F32 = mybir.dt.float32
BF16 = mybir.dt.bfloat16
AF = mybir.ActivationFunctionType
ALU = mybir.AluOpType

N_CORES = 8
TP, TS = 2048, 64
T = TP + TS
NT = 17
D = 4096
NIN = 14368
PAST = 1024
EPS = 1e-6
Q0, K0, V0, G0, XBC0, Z0, DT0 = 0, 2048, 4096, 6144, 8192, 12288, 14336
TBS = [(0, 512), (512, 512), (1024, 512), (1536, 512), (2048, 64)]
NEG = -30000.0

ENGINES = ["pe", "act", "dve", "pool", "sp"]
SEM_CHUNK = 30000


class _Op:
    __slots__ = ("eng", "seq", "fn", "waits", "slot", "use", "sig")

    def __init__(self):
        self.sig = None


class Sched:
    def __init__(self, nc, slots_per_queue=8):
        self.nc = nc
        self.streams = {e: [] for e in ENGINES}
        self.nseq = {e: 0 for e in ENGINES}
        self.known = {e: {} for e in ENGINES}
        self.lastw = {}
        self.readers = {}
        self.clocks = {}
        self.waited = set()
        self.nslots = slots_per_queue
        self.slot_rr = {q: 0 for q in ("sp", "act", "pool")}
        self.slot_use = {}
        self.opmap = {}
        self.tag = ""
        self.pe_tags = []

    def _deps(self, eng, reads, writes):
        deps = set()
        for k in reads:
            ev = self.lastw.get(k)
            if ev is not None:
                deps.add(ev)
            if isinstance(k, tuple) and k[0] == "ps":
                for src, seq in self.readers.get(k, {}).items():
                    if src != eng:
                        deps.add((src, seq))
        for k in writes:
            ev = self.lastw.get(k)
            if ev is not None:
                deps.add(ev)
            for src, seq in self.readers.get(k, {}).items():
                deps.add((src, seq))
        if eng == "pe":
            deps = {d for d in deps if d[0] != "pe"}
        return deps

    def _wait_list(self, eng, deps):
        kn = self.known[eng]
        waits = []
        for ev in sorted(deps, key=lambda d: (str(d[0]), d[1])):
            src, seq = ev
            if kn.get(src, 0) >= seq:
                continue
            waits.append(ev)
            kn[src] = seq
            for s2, v2 in self.clocks.get(ev, {}).items():
                if kn.get(s2, 0) < v2:
                    kn[s2] = v2
            if not isinstance(src, tuple):
                self.waited.add(ev)
        return waits

    def _mark(self, ev, reads, writes):
        src, seq = ev
        for k in reads:
            self.readers.setdefault(k, {})[src] = seq
        for k in writes:
            self.lastw[k] = ev
            self.readers[k] = {}

    def op(self, eng, fn, reads=(), writes=()):
        deps = self._deps(eng, reads, writes)
        o = _Op()
        o.eng = eng
        o.waits = self._wait_list(eng, deps)
        self.nseq[eng] += 1
        o.seq = self.nseq[eng]
        o.fn = fn
        o.slot = None
        if eng == "pe":
            self.pe_tags.append(self.tag)
        ev = (eng, o.seq)
        self.clocks[ev] = dict(self.known[eng])
        self.streams[eng].append(o)
        self.opmap[ev] = o
        self._mark(ev, reads, writes)
        return ev

    def dma(self, q, fn, reads=(), writes=()):
        i = self.slot_rr[q]
        self.slot_rr[q] = (i + 1) % self.nslots
        slot = (q, i)
        use = self.slot_use.get(slot, 0)
        deps = self._deps(slot, reads, writes)
        if use > 0:
            deps.add((slot, use))
        o = _Op()
        o.eng = q
        o.waits = self._wait_list(q, deps)
        o.seq = None
        o.fn = fn
        o.slot = slot
        o.use = use + 1
        self.slot_use[slot] = use + 1
        ev = (slot, use + 1)
        self.clocks[ev] = dict(self.known[q])
        self.streams[q].append(o)
        self._mark(ev, reads, writes)
        return ev

    def fence(self, src_keys, dst_keys):
        evs = {}
        for k in src_keys:
            ev = self.lastw.get(k)
            if ev is not None:
                evs[ev[0]] = max(evs.get(ev[0], 0), ev[1])
            for s_, q_ in self.readers.get(k, {}).items():
                evs[s_] = max(evs.get(s_, 0), q_)
        for k in dst_keys:
            r = self.readers.setdefault(k, {})
            for s_, q_ in evs.items():
                r[s_] = max(r.get(s_, 0), q_)

    def barrier(self):
        deps = {(slot, use) for slot, use in self.slot_use.items()}
        for e in ENGINES:
            if self.nseq[e] > 0:
                deps.add((e, self.nseq[e]))
        for e in ENGINES:
            o = _Op()
            o.eng = e
            o.waits = self._wait_list(e, {d for d in deps if d[0] != e})
            o.seq = None
            o.fn = None
            o.slot = None
            self.streams[e].append(o)
        self.lastw = {}
        self.readers = {}

    def emit(self):
        nc = self.nc
        nsig = {}
        for e in ENGINES:
            c = 0
            for o in self.streams[e]:
                if o.seq is not None and (e, o.seq) in self.waited:
                    c += 1
                    o.sig = c
            nsig[e] = c
        with contextlib.ExitStack() as st:
            csem = {}
            for e in ENGINES:
                n = (nsig[e] + SEM_CHUNK - 1) // SEM_CHUNK
                csem[e] = [st.enter_context(nc.semaphore(f"c_{e}_{j}")) for j in range(n)]
            dsem = {}
            for slot in self.slot_use:
                dsem[slot] = st.enter_context(nc.semaphore(f"d_{slot[0]}_{slot[1]}"))

            def resolve(ev):
                src, seq = ev
                if isinstance(src, tuple):
                    return dsem[src], 16 * seq
                r = self.opmap[ev].sig
                return csem[src][(r - 1) // SEM_CHUNK], (r - 1) % SEM_CHUNK + 1

            def run(e, h):
                for o in self.streams[e]:
                    for ev in o.waits:
                        s, v = resolve(ev)
                        h.wait_ge(s, v)
                    if o.fn is None:
                        continue
                    ins = o.fn(h)
                    if o.slot is not None:
                        ins.then_inc(dsem[o.slot], 16)
                    elif o.sig is not None:
                        ins.then_inc(csem[e][(o.sig - 1) // SEM_CHUNK], 1)

            with nc.Block() as block:
                @block.tensor
                def _(h):
                    run("pe", h)

                @block.scalar
                def _(h):
                    run("act", h)

                @block.vector
                def _(h):
                    run("dve", h)

                @block.gpsimd
                def _(h):
                    run("pool", h)

                @block.sync
                def _(h):
                    run("sp", h)
        return nsig


class Arena:
    def __init__(self, tile_f32, nwords):
        self.t = tile_f32
        self.n = nwords * 4
        self.off = 0

    def mark(self):
        return self.off

    def reset(self, m):
        self.off = m

    def alloc(self, shape_free, dtype):
        esz = 4 if dtype == F32 else 2
        n = 1
        for s in shape_free:
            n *= s
        nbytes = (n * esz + 31) // 32 * 32
        assert self.off + nbytes <= self.n, f"arena overflow: {self.off}+{nbytes} > {self.n}"
        w0 = self.off // 4
        v = self.t[:, w0:w0 + nbytes // 4]
        self.off += nbytes
        if dtype != F32:
            v = v.bitcast(dtype)
        v = v[:, 0:n]
        if len(shape_free) == 2:
            v = v.rearrange("p (a b) -> p a b", a=shape_free[0])
        elif len(shape_free) == 3:
            v = v.rearrange("p (a b c) -> p a b c", a=shape_free[0], b=shape_free[1])
        return v


def build_program(phases=("p1", "p2a", "p2b", "p3"), dbg=False):
    nc = bass.Bass("TRN2", target_bir_lowering=False)

    def din(name, shape):
        return nc.dram_tensor(name, shape, F32, kind="ExternalInput").ap()

    def dout(name, shape):
        return nc.dram_tensor(name, shape, F32, kind="ExternalOutput").ap()

    x = din("x", [T, D])
    p_in = din("p", [T, 256])
    ck = din("ck", [PAST, 2048])
    cv = din("cv", [PAST, 2048])
    sconv = din("sconv", [3, D])
    sssd = din("sssd", [2048, 128])
    w_in = din("w_in", [D, NIN])
    w_out = din("w_out", [D, D])
    w_gate = din("w_gate", [D, D])
    w_proj = din("w_proj", [256, D])
    norm_pre = din("norm_pre", [D])
    norm_post = din("norm_post", [D])
    ple_norm = din("ple_norm", [D])
    conv_w = din("conv_w", [4, D])
    conv_b = din("conv_b", [D])
    sb_norm = din("sb_norm", [2048])
    ssd_norm = din("ssd_norm", [2048])
    dt_bias = din("dt_bias", [32])
    a_log = din("a_log", [32])
    d_skip = din("d_skip", [32])

    y = dout("y", [T, D])
    k_out = dout("k_out", [T, 2048])
    v_out = dout("v_out", [T, 2048])
    conv_out = dout("conv_out", [6, D])
    ssd_out = dout("ssd_out", [4096, 128])

    p3only = bool(dbg) and dbg.get("p3only", False)
    omix_d = nc.dram_tensor("omix_scr", [32, 128, T], BF16,
                            kind=("ExternalInput" if p3only else ("ExternalOutput" if dbg else "Internal"))).ap()
    ssq_in = din("ssq_in", [128, 2 * NT]) if p3only else None
    x1_d = nc.dram_tensor("x1_scr", [T, D], F32, kind="Internal").ap()
    x1T_d = nc.dram_tensor("x1T_scr", [32, 128, T], BF16, kind="Internal").ap()
    dbg_d = dout("dbg", [128, 4096]) if dbg else None

    S = Sched(nc)

    def row(ap):
        return ap.rearrange("(o n) -> o n", o=1)

    def I(eng, method, reads, writes, *args, **kw):
        return S.op(eng, lambda e: getattr(e, method)(*args, **kw), reads=reads, writes=writes)

    def DMA(q, out, in_, reads, writes):
        return S.dma(q, lambda e: e.dma_start(out=out, in_=in_), reads=reads, writes=writes)

    rr = {"ev": 0}

    def evac_eng():
        rr["ev"] += 1
        return "act" if rr["ev"] % 2 else "dve"

    def copy_on(eng, out, in_, reads, writes):
        if eng == "act":
            return I("act", "activation", reads, writes, out=out, in_=in_, func=AF.Copy)
        return I(eng, "tensor_copy", reads, writes, out=out, in_=in_)

    with contextlib.ExitStack() as st:
        NW = 52224
        arena_t = st.enter_context(nc.sbuf_tensor("arena", [128, NW], F32))
        A = Arena(arena_t, NW)
        PS = [st.enter_context(nc.psum_tensor(f"ps{i}", [128, 512], F32)) for i in range(8)]

        def psb(i):
            return PS[i][:].bitcast(BF16)

        ident_bf = A.alloc([128], BF16)
        ident_f = A.alloc([128], F32)
        negtri = A.alloc([128], BF16)
        negones = A.alloc([128], BF16)
        ones_bf = A.alloc([128], BF16)
        ones_f = A.alloc([128], F32)
        maskM = A.alloc([896], BF16)
        tri_f = A.alloc([128], F32)
        negmask2 = A.alloc([128], BF16)
        FV = A.alloc([32, 16], F32)
        ssq_sb = A.alloc([NT], F32)
        ssq_ssd = A.alloc([NT], F32)
        dtb_b = A.alloc([32], F32)
        alog_b = A.alloc([32], F32)
        dskp = A.alloc([16], F32)

        m0 = A.mark()
        zero_bf = A.alloc([896], BF16)
        I("pool", "memset", [], ["negones"], negones, -1.0)
        I("pool", "memset", [], ["ones_bf"], ones_bf, 1.0)
        I("pool", "memset", [], ["ones_f"], ones_f, 1.0)
        I("pool", "memset", [], ["zero_bf"], zero_bf, 0.0)
        I("pool", "memset", [], ["ssq_sb"], ssq_sb, 0.0)
        I("pool", "memset", [], ["ssq_ssd"], ssq_ssd, 0.0)
        I("pool", "affine_select", ["ones_bf"], ["ident_bf"], out=ident_bf, in_=ones_bf, pattern=[[-1, 128]],
          compare_op=ALU.is_equal, fill=0.0, base=0, channel_multiplier=1)
        I("pool", "affine_select", ["ones_f"], ["ident_f"], out=ident_f, in_=ones_f, pattern=[[-1, 128]],
          compare_op=ALU.is_equal, fill=0.0, base=0, channel_multiplier=1)
        I("pool", "affine_select", ["negones"], ["negtri"], out=negtri, in_=negones, pattern=[[-1, 128]],
          compare_op=ALU.is_ge, fill=0.0, base=0, channel_multiplier=1)
        I("pool", "affine_select", ["zero_bf"], ["maskM"], out=maskM, in_=zero_bf, pattern=[[1, 896]],
          compare_op=ALU.is_gt, fill=NEG, base=-384, channel_multiplier=-1)
        I("pool", "affine_select", ["ones_f"], ["tri_f"], out=tri_f, in_=ones_f, pattern=[[1, 128]],
          compare_op=ALU.is_ge, fill=0.0, base=0, channel_multiplier=-1)
        I("pool", "affine_select", ["zero_bf"], ["negmask2"], out=negmask2, in_=zero_bf[:, 0:128], pattern=[[1, 128]],
          compare_op=ALU.is_ge, fill=NEG, base=0, channel_multiplier=-1)
        CONSTS = ["ident_bf", "ident_f", "negtri", "negones", "ones_bf", "ones_f", "maskM", "tri_f", "negmask2"]

        fvs = A.alloc([D], F32)
        I("pool", "memset", [], ["fvs"], fvs[0:32, :], 0.0)
        DMA("sp", fvs[0:4, :], conv_w, [], ["fvs"])
        DMA("sp", fvs[4:5, :], row(conv_b), [], ["fvs"])
        DMA("sp", fvs[5:6, 0:2048], row(sb_norm), [], ["fvs"])
        DMA("sp", fvs[5:6, 2048:4096], row(ssd_norm), [], ["fvs"])
        DMA("sp", fvs[6:9, :], sconv, [], ["fvs"])
        DMA("sp", dtb_b, dt_bias.partition_broadcast(128), [], ["dtb_b"])
        DMA("sp", alog_b, a_log.partition_broadcast(128), [], ["alog_b"])
        for c in range(32):
            I("pe", "transpose", ["fvs", "ident_f"], [("ps", 7)], out=PS[7][:, c * 16:c * 16 + 16],
              in_=fvs[0:16, c * 128:(c + 1) * 128], identity=ident_f[0:16, 0:16])
        I("dve", "tensor_copy", [("ps", 7)], ["FV"], out=FV, in_=PS[7][:].rearrange("p (c r) -> p c r", c=32))
        S.barrier()
        A.reset(m0)

        mC = A.mark()
        hT = A.alloc([32, T], BF16)
        mP = A.mark()

        if "p1" in phases:
            npre_b = A.alloc([D], F32)
            xt = [A.alloc([D], F32) for _ in range(2)]
            hns = [A.alloc([D], BF16) for _ in range(2)]
            st1 = A.alloc([NT, 4], F32)
            DMA("sp", npre_b, norm_pre.partition_broadcast(128), [], ["npre_b"])
            def p1_load(tt):
                nt = 128 if tt < 16 else 64
                b = tt % 2
                DMA("sp", xt[b][0:nt, :], x[tt * 128:tt * 128 + nt, :], [], [("xt", b)])

            def p1_square(tt):
                nt = 128 if tt < 16 else 64
                b = tt % 2
                I("act", "activation", [("xt", b)], [("hn", b), ("st1", tt)], out=hns[b][0:nt, :], in_=xt[b][0:nt, :],
                  func=AF.Square, accum_out=st1[0:nt, tt, 0:1])

            def p1_stats(tt):
                nt = 128 if tt < 16 else 64
                I("dve", "tensor_scalar", [("st1", tt)], [("st1b", tt)], out=st1[0:nt, tt, 1:2], in0=st1[0:nt, tt, 0:1],
                  scalar1=1.0 / D, scalar2=EPS, op0=ALU.mult, op1=ALU.add)
                I("act", "activation", [("st1b", tt)], [("st1c", tt)], out=st1[0:nt, tt, 2:3], in_=st1[0:nt, tt, 1:2], func=AF.Ln)
                I("act", "activation", [("st1c", tt)], [("st1d", tt)], out=st1[0:nt, tt, 3:4], in_=st1[0:nt, tt, 2:3],
                  func=AF.Exp, scale=-0.5)

            def p1_main(tt):
                nt = 128 if tt < 16 else 64
                r0 = tt * 128
                b = tt % 2
                hn = hns[b]
                hk = ("hn", b)
                I("dve", "scalar_tensor_tensor", [("xt", b), ("st1d", tt), "npre_b"], [hk], out=hn[0:nt, :],
                  in0=xt[b][0:nt, :], scalar=st1[0:nt, tt, 3:4], in1=npre_b[0:nt, :], op0=ALU.mult, op1=ALU.mult)
                for c4 in range(4):
                    bk = (tt * 4 + c4) % 4
                    pv = psb(bk).rearrange("p (j n) -> p j n", j=8)
                    for j in range(8):
                        fc = c4 * 8 + j
                        I("pe", "transpose", [hk, "ident_bf"], [("ps", bk)], out=pv[:, j, 0:nt],
                          in_=hn[0:nt, fc * 128:(fc + 1) * 128], identity=ident_bf[0:nt, 0:nt])
                    copy_on(evac_eng(), hT[:, c4 * 8:(c4 + 1) * 8, r0:r0 + nt], pv[:, :, 0:nt], [("ps", bk)], [("hT", tt)])

            p1_load(0)
            p1_square(0)
            for tt in range(NT):
                if tt + 1 < NT:
                    p1_load(tt + 1)
                p1_stats(tt)
                if tt + 1 < NT:
                    p1_square(tt + 1)
                p1_main(tt)
            S.barrier()
            A.reset(mP)

        def hkeys(t0, n):
            return [("hT", tt) for tt in range(t0 // 128, (t0 + n + 127) // 128)]

        if "p2a" in phases:
            WSL = 2
            wt = [A.alloc([32, 128], BF16) for _ in range(WSL)]
            qTs = [A.alloc([T], BF16) for _ in range(2)]
            kT = A.alloc([T], BF16)
            gs = A.alloc([T], BF16)
            v_tok = A.alloc([NT, 128], BF16)
            kst = [A.alloc([4, 128], F32) for _ in range(2)]
            kb16 = [A.alloc([4, 128], BF16) for _ in range(2)]
            e_t = A.alloc([512], F32)
            og_t = e_t
            sp_t = [A.alloc([512], BF16) for _ in range(3)]
            racc2 = A.alloc([2, 512], BF16)
            racc = [racc2[:, 0, :], racc2[:, 1, :]]
            ckb = racc2.rearrange("p a b -> p (a b)").rearrange("p (j d) -> p j d", j=8)
            att_t = [A.alloc([512], BF16) for _ in range(2)]
            sq_t = A.alloc([512], BF16)
            om_t = [A.alloc([512], BF16) for _ in range(2)]
            vcb = A.alloc([8, 128], BF16)
            kTc = A.alloc([PAST], BF16)
            sp_s = A.alloc([64], BF16)
            I("pool", "memset", [], ["sp_s"], sp_s, 0.0)
            SCALE = 128.0 ** -0.5
            cnt = {"w": 0, "bank": 0, "st": 0}

            def load_w(col0):
                slot = cnt["w"] % WSL
                cnt["w"] += 1
                for part in range(4):
                    src = w_in[part * 1024:(part + 1) * 1024, col0:col0 + 128].rearrange("(c p) n -> p c n", p=128)
                    DMA("pool", wt[slot][:, part * 8:(part + 1) * 8, :], src, [], [("wt", slot, part)])
                return slot

            def proj_fm(col0, evac):
                slot = load_w(col0)
                for tbi, (t0, n) in enumerate(TBS):
                    bk = cnt["bank"] % 4
                    cnt["bank"] += 1
                    for fc in range(32):
                        I("pe", "matmul", [("wt", slot, fc // 8)] + hkeys(t0, n), [("ps", bk)], PS[bk][:, 0:n],
                          wt[slot][:, fc, :], hT[:, fc, t0:t0 + n], start=(fc == 0), stop=(fc == 31))
                    evac(bk, tbi, t0, n)

            def proj_tm(col0, hd, dst_out, is_k):
                slot = load_w(col0)
                kpend = []
                for j4 in range(5):
                    tiles = list(range(j4 * 4, min(j4 * 4 + 4, NT)))
                    bk = cnt["bank"] % 4
                    cnt["bank"] += 1
                    for j, tt in enumerate(tiles):
                        nt = 128 if tt < 16 else 64
                        for fc in range(32):
                            I("pe", "matmul", [("wt", slot, fc // 8), ("hT", tt)], [("ps", bk)],
                              PS[bk][0:nt, j * 128:(j + 1) * 128], hT[:, fc, tt * 128:tt * 128 + nt], wt[slot][:, fc, :],
                              start=(fc == 0), stop=(fc == 31))
                    while kpend:
                        kpend.pop(0)()
                    s = cnt["st"] % 2
                    cnt["st"] += 1
                    nj = len(tiles)
                    npart = 128 if tiles[0] < 16 else 64
                    copy_on(evac_eng(), kst[s][0:npart, 0:nj, :], PS[bk][0:npart, 0:nj * 128].rearrange("p (j d) -> p j d", j=nj),
                            [("ps", bk)], [("kst", s)])
                    r0 = tiles[0] * 128
                    if npart == 128:
                        dst = dst_out[r0:r0 + nj * 128, hd * 128:(hd + 1) * 128].rearrange("(j p) d -> p j d", p=128)
                        DMA("sp", dst, kst[s][:, 0:nj, :], [("kst", s)], [])
                    else:
                        DMA("sp", dst_out[r0:r0 + 64, hd * 128:(hd + 1) * 128], kst[s][0:64, 0, :], [("kst", s)], [])
                    if is_k:
                        I("pool", "tensor_copy", [("kst", s)], [("kb16", s)], out=kb16[s][0:npart, 0:nj, :], in_=kst[s][0:npart, 0:nj, :])

                        def tr(s=s, tiles=tiles, npart=npart, nj=nj, r0=r0, j4=j4):
                            pv = psb(4 + s).rearrange("p (j n) -> p j n", j=8)
                            for j, tt in enumerate(tiles):
                                I("pe", "transpose", [("kb16", s), "ident_bf"], [("ps", 4 + s)], out=pv[:, j, 0:npart],
                                  in_=kb16[s][0:npart, j, :], identity=ident_bf[0:npart, 0:npart])
                            if npart == 128:
                                copy_on(evac_eng(), kT[:, r0:r0 + nj * 128].rearrange("p (j n) -> p j n", j=nj), pv[:, 0:nj, :],
                                        [("ps", 4 + s)], [("kT", j4)])
                            else:
                                copy_on(evac_eng(), kT[:, r0:r0 + 64], pv[:, 0, 0:64], [("ps", 4 + s)], [("kT", j4)])
                        kpend.append(tr)
                    else:
                        I("pool", "tensor_copy", [("kst", s)], [("v_tok", j4)], out=v_tok[0:npart, tiles[0]:tiles[0] + nj, :],
                          in_=kst[s][0:npart, 0:nj, :])
                while kpend:
                    kpend.pop(0)()

            def attention(hd, qT, qp, bg=None):
                blocks = []
                for qb in range(4):
                    kbs = list(range(4 * qb + 3, -1, -1))
                    for i, kb in enumerate(kbs):
                        m = kb - 4 * qb
                        blocks.append(dict(q0=qb * 512, nq=512, k_ap=kT[:, kb * 128:(kb + 1) * 128], nk=128,
                                           v_ap=v_tok[:, kb, :], mask=(maskM[:, 384 - 128 * m:384 - 128 * m + 512] if m >= 0 else None),
                                           first=(i == 0), last=(i == len(kbs) - 1), qkey=("qT", qp, qb), kkey=("kT", kb // 4),
                                           vkey=("v_tok", kb // 4), ckeys=[], sps=None, obank=6, qbi=qb))
                blocks.append(dict(q0=TP, nq=64, k_ap=kT[:, TP:T], nk=64, v_ap=v_tok[0:64, 16, :], mask=maskM[0:64, 384:448],
                                   first=True, last=False, qkey=("qT", qp, 4), kkey=("kT", 4), vkey=("v_tok", 4), ckeys=[],
                                   sps=sp_s, obank=6, qbi=4))
                for i, kb in enumerate(range(7, -1, -1)):
                    blocks.append(dict(q0=TP, nq=64, k_ap=kTc[:, kb * 128:(kb + 1) * 128], nk=128, v_ap=vcb[:, kb, :], mask=None,
                                       first=False, last=(i == 7), qkey=("qT", qp, 4), kkey="kTc", vkey="vcb", ckeys=[], sps=None,
                                       obank=6, qbi=4))
                n = len(blocks)
                state = {"R": None, "Rkey": None, "ra": 0}

                def Zm(i):
                    b = blocks[i]
                    zb = i % 2
                    nk, nq = b["nk"], b["nq"]
                    I("pe", "matmul", [b["kkey"], b["qkey"]], [("ps", zb)], PS[zb][0:nk, 0:nq], b["k_ap"], qT[:, b["q0"]:b["q0"] + nq],
                      start=True, stop=(b["mask"] is None))
                    if b["mask"] is not None:
                        I("pe", "matmul", ["ident_bf", "maskM"], [("ps", zb)], PS[zb][0:nk, 0:nq], ident_bf[0:nk, 0:nk], b["mask"],
                          start=False, stop=True)
                    I("act", "activation", [("ps", zb)], ["e_t"], out=e_t[0:nk, 0:nq], in_=PS[zb][0:nk, 0:nq], func=AF.Exp)
                    if b["sps"] is not None:
                        sp_ap, spk = b["sps"], "sp_s"
                        b["sp_full"] = sp_ap[:, 0:nq]
                    else:
                        si = i % 3
                        sp_ap, spk = sp_t[si], ("sp", si)
                        b["sp_full"] = sp_ap[:, 0:nq]
                    b["sp"], b["spk"] = sp_ap, spk
                    I("act", "activation", ["e_t"], [spk], out=sp_ap[0:nk, 0:nq], in_=e_t[0:nk, 0:nq], func=AF.Ln, bias=1.0)
                    if b["first"]:
                        b["R"], b["Rk"] = None, None
                    else:
                        pb = blocks[i - 1]
                        if pb["first"]:
                            b["R"], b["Rk"] = pb["sp_full"], pb["spk"]
                        else:
                            ra = state["ra"] % 2
                            state["ra"] += 1
                            I("dve", "tensor_tensor", [pb["Rk"], pb["spk"]], [("racc", ra)], out=racc[ra][:, 0:nq],
                              in0=pb["R"], in1=pb["sp_full"], op=ALU.add)
                            b["R"], b["Rk"] = racc[ra][:, 0:nq], ("racc", ra)

                def Am(i):
                    b = blocks[i]
                    ab = 2 + i % 2
                    nk, nq = b["nk"], b["nq"]
                    I("pe", "matmul", [b["kkey"], b["qkey"]], [("ps", ab)], PS[ab][0:nk, 0:nq], b["k_ap"], qT[:, b["q0"]:b["q0"] + nq],
                      start=True, stop=False)
                    if b["mask"] is not None:
                        I("pe", "matmul", ["ident_bf", "maskM"], [("ps", ab)], PS[ab][0:nk, 0:nq], ident_bf[0:nk, 0:nk], b["mask"],
                          start=False, stop=False)
                    I("pe", "matmul", ["negtri", b["spk"]], [("ps", ab)], PS[ab][0:nk, 0:nq], negtri[0:nk, 0:nk], b["sp"][0:nk, 0:nq],
                      start=False, stop=(b["R"] is None))
                    if b["R"] is not None:
                        I("pe", "matmul", ["negones", b["Rk"]], [("ps", ab)], PS[ab][0:nk, 0:nq], negones[:, 0:nk], b["R"],
                          start=False, stop=True)
                    ai = i % 2
                    I("act", "activation", [("ps", ab)], [("att", ai)], out=att_t[ai][0:nk, 0:nq], in_=PS[ab][0:nk, 0:nq], func=AF.Exp)

                def AVm(i):
                    b = blocks[i]
                    nk, nq = b["nk"], b["nq"]
                    ob = b["obank"]
                    ai = i % 2
                    I("pe", "matmul", [b["vkey"], ("att", ai)], [("ps", ob)], PS[ob][:, 0:nq], b["v_ap"][0:nk, :], att_t[ai][0:nk, 0:nq],
                      start=b["first"], stop=b["last"])
                    if b["last"]:
                        q0, qbi = b["q0"], b["qbi"]
                        I("dve", "tensor_tensor", [("ps", ob), ("gs", qbi)], ["e_t"], out=og_t[:, 0:nq], in0=PS[ob][:, 0:nq],
                          in1=gs[:, q0:q0 + nq], op=ALU.mult)
                        oi = qbi % 2
                        I("dve", "tensor_scalar", ["e_t", "FV"], [("om", oi)], out=om_t[oi][:, 0:nq], in0=og_t[:, 0:nq],
                          scalar1=FV[:, hd, 5:6], scalar2=None, op0=ALU.mult)
                        DMA("sp", omix_d[hd, :, q0:q0 + nq], om_t[oi][:, 0:nq], [("om", oi)], [("omix", hd, qbi)])
                        I("act", "activation", ["e_t"], ["sq_t"], out=sq_t[:, 0:nq], in_=og_t[:, 0:nq], func=AF.Square)
                        def ssq_mm(q0=q0, nq=nq):
                            for j in range((nq + 127) // 128):
                                tt = q0 // 128 + j
                                nt = min(128, nq - j * 128)
                                I("pe", "matmul", ["sq_t", "ones_bf"], [("ps", 7)], PS[7][0:nt, tt:tt + 1], sq_t[:, j * 128:j * 128 + nt],
                                  ones_bf[:, 0:1], start=True, stop=True)
                        pending.append(ssq_mm)

                pending = []
                for s in range(n + 2):
                    if pending and s % 2 == 0:
                        pending.pop(0)()
                    if s < n:
                        Zm(s)
                    if 0 <= s - 1 < n:
                        Am(s - 1)
                    if 0 <= s - 2 < n:
                        AVm(s - 2)
                    if bg is not None:
                        for _ in range(4):
                            next(bg, None)
                if bg is not None:
                    for _ in bg:
                        pass
                while pending:
                    pending.pop(0)()
                I("dve", "tensor_tensor", [("ps", 7), "ssq_sb"], ["ssq_sb"], out=ssq_sb[:, 0:16], in0=PS[7][:, 0:16], in1=ssq_sb[:, 0:16], op=ALU.add)
                I("dve", "tensor_tensor", [("ps", 7), "ssq_sb"], ["ssq_sb"], out=ssq_sb[0:64, 16:17], in0=PS[7][0:64, 16:17],
                  in1=ssq_sb[0:64, 16:17], op=ALU.add)

            nheads = 16 if not dbg else dbg.get("nheads", 16)

            def q_proj_gen(hd):
                qp = hd % 2
                slot = load_w(Q0 + hd * 128)
                for tbi, (t0, n) in enumerate(TBS):
                    bk = 4 + tbi % 2
                    for fc in range(32):
                        I("pe", "matmul", [("wt", slot, fc // 8)] + hkeys(t0, n), [("ps", bk)], PS[bk][:, 0:n],
                          wt[slot][:, fc, :], hT[:, fc, t0:t0 + n], start=(fc == 0), stop=(fc == 31))
                        yield
                    I("dve", "tensor_scalar", [("ps", bk)], [("qT", qp, tbi)], out=qTs[qp][:, t0:t0 + n], in0=PS[bk][:, 0:n],
                      scalar1=SCALE, scalar2=None, op0=ALU.mult)

            for _ in q_proj_gen(0):
                pass
            for hd in range(nheads):
                proj_tm(K0 + hd * 128, hd, k_out, True)
                proj_tm(V0 + hd * 128, hd, v_out, False)
                DMA("pool", ckb, ck[:, hd * 128:(hd + 1) * 128].rearrange("(j p) d -> p j d", p=128), [], [("racc", 0), ("racc", 1)])
                DMA("pool", vcb, cv[:, hd * 128:(hd + 1) * 128].rearrange("(j p) d -> p j d", p=128), [], ["vcb"])
                proj_fm(G0 + hd * 128, lambda bk, tbi, t0, n: I("act", "activation", [("ps", bk)], [("gs", tbi)], out=gs[:, t0:t0 + n],
                                                               in_=PS[bk][:, 0:n], func=AF.Silu))
                pv = psb(7).rearrange("p (j n) -> p j n", j=8)
                for j in range(8):
                    I("pe", "transpose", [("racc", 0), ("racc", 1), "ident_bf"], [("ps", 7)], out=pv[:, j, :], in_=ckb[:, j, :], identity=ident_bf)
                copy_on(evac_eng(), kTc.rearrange("p (j n) -> p j n", j=8), pv, [("ps", 7)], ["kTc"])
                attention(hd, qTs[hd % 2], hd % 2, q_proj_gen(hd + 1) if hd + 1 < nheads else None)
            if dbg:
                I("dve", "tensor_copy", ["ssq_sb"], ["dbgt"], out=og_t[:, 0:NT], in_=ssq_sb)
                DMA("sp", dbg_d[:, 0:NT], og_t[:, 0:NT], ["dbgt"], [])
            S.barrier()
            A.reset(mP)


        if "p2b" in phases:
            A.reset(mP)
            wt = [A.alloc([32, 128], BF16) for _ in range(2)]
            rA = A.alloc([2120], F32)
            wdt = rA[:, 0:512].bitcast(BF16).rearrange("p (c n) -> p c n", c=32)
            rB = A.alloc([2120], F32)
            xact = [A.alloc([T], BF16) for _ in range(2)]
            Bact = A.alloc([T], BF16)
            Cact = A.alloc([T], BF16)
            dt_tok = A.alloc([NT, 32], F32)
            a_b = A.alloc([32], F32)
            dsk_b = A.alloc([32], F32)
            hlast = A.alloc([32, 8], BF16)
            cst = [A.alloc([128], F32) for _ in range(1)]
            dAb4 = A.alloc([4, 128], F32)
            D4 = A.alloc([4, 128], F32)
            Wt4 = A.alloc([4, 128], BF16)
            xB_tok = A.alloc([3, 128], BF16)
            w4 = A.alloc([4], F32)
            xw = A.alloc([4, 64], BF16)
            dAx = A.alloc([2, 128], F32)
            t1 = A.alloc([128], F32)
            Hs = A.alloc([256], F32)
            Hbf = A.alloc([256], BF16)
            hst = [A.alloc([128], F32) for _ in range(1)] * 2
            zs_t = [A.alloc([512], BF16) for _ in range(1)]
            om_b = [A.alloc([512], BF16) for _ in range(1)]
            cntb = {"w": 0, "bank": 0, "z": 0, "c": 0, "h": 0}
            zpend = []

            def load_wb(col0):
                slot = cntb["w"] % 2
                cntb["w"] += 1
                for part in range(4):
                    src = w_in[part * 1024:(part + 1) * 1024, col0:col0 + 128].rearrange("(c p) n -> p c n", p=128)
                    DMA("pool", wt[slot][:, part * 8:(part + 1) * 8, :], src, [], [("wt", slot, part)])
                return slot

            def proj_fmb(col0, evac, extra=None):
                slot = load_wb(col0)
                for tbi, (t0, n) in enumerate(TBS):
                    bk = cntb["bank"] % 3
                    cntb["bank"] += 1
                    for fc in range(32):
                        I("pe", "matmul", [("wt", slot, fc // 8)] + hkeys(t0, n), [("ps", bk)], PS[bk][:, 0:n],
                          wt[slot][:, fc, :], hT[:, fc, t0:t0 + n], start=(fc == 0), stop=(fc == 31))
                    evac(bk, tbi, t0, n)
                if extra is not None:
                    extra(slot)

            DMA("pool", wdt, w_in[:, DT0:DT0 + 32].rearrange("(c p) n -> p c n", p=128), [], ["wdt", "rA"])
            DMA("sp", dsk_b, d_skip.partition_broadcast(128), [], ["dsk_b"])
            I("dve", "tensor_copy", ["dsk_b"], ["dskp"], out=dskp[0:64, :], in_=dsk_b[0:64, :].rearrange("p (j t) -> p j t", t=2)[:, :, 0])
            I("dve", "tensor_copy", ["dsk_b"], ["dskp"], out=dskp[64:128, :], in_=dsk_b[64:128, :].rearrange("p (j t) -> p j t", t=2)[:, :, 1])
            I("act", "activation", ["alog_b"], ["a_e"], out=a_b, in_=alog_b, func=AF.Exp)
            I("dve", "tensor_scalar", ["a_e"], ["a_b"], out=a_b, in0=a_b, scalar1=-1.0, scalar2=None, op0=ALU.mult)
            for tt in range(NT):
                nt = 128 if tt < 16 else 64
                bk, cc = (3, tt * 32) if tt < 16 else (4, 0)
                for fc in range(32):
                    I("pe", "matmul", ["wdt", "rA", ("hT", tt)], [("ps", bk)], PS[bk][0:nt, cc:cc + 32], hT[:, fc, tt * 128:tt * 128 + nt],
                      wdt[:, fc, :], start=(fc == 0), stop=(fc == 31))
            I("dve", "tensor_tensor", [("ps", 3), "dtb_b"], ["dt_tok"], out=dt_tok[:, 0:16, :],
              in0=PS[3][:].rearrange("p (t h) -> p t h", t=16),
              in1=dtb_b.rearrange("p (o h) -> p o h", o=1).broadcast_to([128, 16, 32]), op=ALU.add)
            I("dve", "tensor_tensor", [("ps", 4), "dtb_b"], ["dt_tok"], out=dt_tok[0:64, 16, :], in0=PS[4][0:64, 0:32], in1=dtb_b[0:64, :], op=ALU.add)
            I("act", "activation", ["dt_tok"], ["dt_e"], out=dt_tok[:, 0:16, :], in_=dt_tok[:, 0:16, :], func=AF.Exp)
            I("act", "activation", ["dt_tok", "dt_e"], ["dt_e"], out=dt_tok[0:64, 16, :], in_=dt_tok[0:64, 16, :], func=AF.Exp)
            I("act", "activation", ["dt_e"], ["dt_f"], out=dt_tok[:, 0:16, :], in_=dt_tok[:, 0:16, :], func=AF.Ln, bias=1.0)
            I("act", "activation", ["dt_e", "dt_f"], ["dt_f"], out=dt_tok[0:64, 16, :], in_=dt_tok[0:64, 16, :], func=AF.Ln, bias=1.0)
            I("dve", "tensor_copy", [("hT", 15)], ["hlast"], out=hlast[:, :, 0:3], in_=hT[:, :, TP - 3:TP])
            I("dve", "tensor_copy", [("hT", 16), "hlast"], ["hlast"], out=hlast[:, :, 3:6], in_=hT[:, :, T - 3:T])

            def proj_conv(col0, dst, dkey):
                ch = (col0 - XBC0) // 128
                raw, cvb = rA, rB

                def ev(bk, tbi, t0, n):
                    o0 = 3 + t0 if tbi < 4 else 2054
                    copy_on(evac_eng(), raw[:, o0:o0 + n], PS[bk][:, 0:n], [("ps", bk)], ["rA"])

                def extra(slot):
                    cs = 0
                    cntb["c"] += 1
                    for fc in range(32):
                        I("pe", "matmul", [("wt", slot, fc // 8), "hlast"], [("ps", 4)], PS[4][0:6, 128:256], hlast[:, fc, 0:6], wt[slot][:, fc, :],
                          start=(fc == 0), stop=(fc == 31))
                    copy_on("act", cst[cs][0:6, :], PS[4][0:6, 128:256], [("ps", 4)], [("cst", cs)])
                    DMA("sp", conv_out[0:6, ch * 128:(ch + 1) * 128], cst[cs][0:6, :], [("cst", cs)], [])

                proj_fmb(col0, ev, extra)
                I("dve", "memset", [], ["rA"], raw[:, 0:3], 0.0)
                I("dve", "tensor_copy", ["FV"], ["rA"], out=raw[:, 2051:2054], in_=FV[:, ch, 6:9])
                L = 2115
                I("dve", "tensor_scalar", ["rA", "FV"], ["rB"], out=cvb[:, 0:L], in0=raw[:, 0:L], scalar1=FV[:, ch, 0:1], scalar2=FV[:, ch, 4:5],
                  op0=ALU.mult, op1=ALU.add)
                for j in range(1, 4):
                    I("dve", "scalar_tensor_tensor", ["rA", "rB", "FV"], ["rB"], out=cvb[:, 0:L], in0=raw[:, j:j + L], scalar=FV[:, ch, j:j + 1],
                      in1=cvb[:, 0:L], op0=ALU.mult, op1=ALU.add)
                I("act", "activation", ["rB"], [dkey], out=dst[:, 0:TP], in_=cvb[:, 0:TP], func=AF.Silu)
                I("act", "activation", ["rB", dkey], [dkey], out=dst[:, TP:T], in_=cvb[:, 2051:2115], func=AF.Silu)

            w1f = wt[1].rearrange("p a b -> p (a b)")

            def carve(off, n, dtype):
                if dtype == F32:
                    return w1f[:, off // 2:off // 2 + 2 * n].bitcast(F32)
                return w1f[:, off // 2:off // 2 + n]
            TS2 = [dict(dAb4=dAb4, D4=D4, dAx=dAx, Wt4=Wt4, xB_tok=xB_tok, xw=xw, w4=w4),
                   dict(dAb4=carve(0, 512, F32).rearrange("p (a b) -> p a b", a=4), D4=carve(2048, 512, F32).rearrange("p (a b) -> p a b", a=4),
                        dAx=carve(4096, 256, F32).rearrange("p (a b) -> p a b", a=2), Wt4=carve(5120, 512, BF16).rearrange("p (a b) -> p a b", a=4),
                        xB_tok=carve(6144, 384, BF16).rearrange("p (a b) -> p a b", a=3), xw=carve(6912, 256, BF16).rearrange("p (a b) -> p a b", a=4),
                        w4=carve(7424, 4, F32))]
            dA_g = carve(7456, 68, F32).rearrange("p (c h) -> p c h", c=NT)
            negcum_g = carve(7744, 68, F32).rearrange("p (c h) -> p c h", c=NT)
            etot_g = A.alloc([NT, 4], F32)
            T1KEYS = [("dAb4", 1), ("dAx", 1), ("xB_tok", 1), ("w4", 1), ("xw", 1), "dA_g", "negcum_g"] + \
                     [("D4", 1, hq) for hq in range(4)] + [("Wt4", 1, hq) for hq in range(4)]
            W1KEYS = [("wt", 1, part) for part in range(4)]
            zflat = zs_t[0]
            oflat = om_b[0]
            ecxs = [[zflat[:, 0:256].bitcast(F32), zflat[:, 256:512].bitcast(F32)], [oflat[:, 0:256].bitcast(F32), oflat[:, 256:512].bitcast(F32)]]
            EKEYS = [("ecx", p_, r_) for p_ in range(2) for r_ in range(2)]
            ZKEYS = [("zs", 0), ("omb", 0)]

            def group_pre(g):
                hs = slice(4 * g, 4 * g + 4)
                I("dve", "tensor_tensor", ["dt_f", "a_b"], ["dA_g"], out=dA_g[:, 0:16, :], in0=dt_tok[:, 0:16, hs],
                  in1=a_b[:, hs].rearrange("p (o h) -> p o h", o=1).broadcast_to([128, 16, 4]), op=ALU.mult)
                I("dve", "tensor_tensor", ["dt_f", "a_b", "dA_g"], ["dA_g"], out=dA_g[0:64, 16, :], in0=dt_tok[0:64, 16, hs], in1=a_b[0:64, hs], op=ALU.mult)
                for c in range(NT):
                    nt = 128 if c < 16 else 64
                    I("pe", "matmul", ["dA_g", "tri_f"], [("ps", 3)], PS[3][0:nt, c * 4:c * 4 + 4], tri_f[0:nt, 0:nt], dA_g[0:nt, c, :], start=True, stop=True)
                    I("pe", "matmul", ["dA_g", "ones_f"], [("ps", 3)], PS[3][:, 128 + c * 4:128 + c * 4 + 4], ones_f[0:nt, :], dA_g[0:nt, c, :],
                      start=True, stop=True)
                I("dve", "tensor_scalar", [("ps", 3)], ["negcum_g"], out=negcum_g[:, 0:16, :], in0=PS[3][:, 0:64].rearrange("p (c h) -> p c h", c=16),
                  scalar1=-1.0, scalar2=None, op0=ALU.mult)
                I("dve", "tensor_scalar", [("ps", 3), "negcum_g"], ["negcum_g"], out=negcum_g[0:64, 16, :], in0=PS[3][0:64, 64:68], scalar1=-1.0, scalar2=None,
                  op0=ALU.mult)
                I("act", "activation", [("ps", 3)], ["etot_g"], out=etot_g, in_=PS[3][:, 128:196].rearrange("p (c h) -> p c h", c=NT), func=AF.Exp)

            def geo(c):
                nt = 128 if c < 16 else 64
                par = c % 2
                banks = (5, 4, 7, 3) if par == 0 else (6, 0, 1, 2)
                return nt, c * 128, par, TS2[par], banks

            def u1a(g, c, inter):
                nt, t0, par, tt_, _ = geo(c)
                dA4 = dA_g[0:nt, c, :]
                I("dve", "tensor_tensor", ["dA_g", "tri_f"], [("dAb4", par)], out=tt_["dAb4"][0:nt, :, 0:nt],
                  in0=tri_f[0:nt, 0:nt].rearrange("p (o l) -> p o l", o=1).broadcast_to([nt, 4, nt]),
                  in1=dA4.rearrange("p (h o) -> p h o", o=1).broadcast_to([nt, 4, nt]), op=ALU.mult)
                if inter:
                    I("dve", "tensor_copy", ["dA_g"], [("dAx", par)], out=tt_["dAx"][0:nt, :, :].rearrange("p a (b q) -> p (a b) q", q=64),
                      in_=dA4.rearrange("p (h o) -> p h o", o=1).broadcast_to([nt, 4, 64]))

            def u1b(g, c, inter):
                nt, t0, par, tt_, (pb, b1, yb, tb) = geo(c)
                dAb4, D4, dAx, Wt4, xB_tok, xw, w4 = tt_["dAb4"], tt_["D4"], tt_["dAx"], tt_["Wt4"], tt_["xB_tok"], tt_["xw"], tt_["w4"]
                kdAb, kdAx, kxB, kw4, kxw = ("dAb4", par), ("dAx", par), ("xB_tok", par), ("w4", par), ("xw", par)
                hs = slice(4 * g, 4 * g + 4)
                I("pe", "matmul", ["Bact", "Cact"], [("ps", b1)], PS[b1][0:nt, 0:nt], Bact[:, t0:t0 + nt], Cact[:, t0:t0 + nt], start=True, stop=True)
                if nt == 128:
                    I("pe", "matmul", [kdAb, "ones_f"], [("ps", pb)], PS[pb][:, :], ones_f[:, :], dAb4[:, :, :].rearrange("p h l -> p (h l)"),
                      start=True, stop=False)
                    for hq in range(4):
                        I("pe", "matmul", ["ident_bf", "negmask2"], [("ps", pb)], PS[pb][:, hq * 128:(hq + 1) * 128], ident_bf[:, :], negmask2[:, :],
                          start=False, stop=(hq == 3))
                else:
                    for hq in range(4):
                        I("pe", "matmul", [kdAb, "ones_f"], [("ps", pb)], PS[pb][0:nt, hq * 128:hq * 128 + nt], ones_f[0:nt, 0:nt], dAb4[0:nt, hq, 0:nt],
                          start=True, stop=False)
                        I("pe", "matmul", ["ident_bf", "negmask2"], [("ps", pb)], PS[pb][0:nt, hq * 128:hq * 128 + nt], ident_bf[0:nt, 0:nt],
                          negmask2[0:nt, 0:nt], start=False, stop=True)
                pv2 = psb(tb).rearrange("p (j n) -> p j n", j=8)
                for pr in range(2):
                    I("pe", "transpose", [("xact", pr), "ident_bf"], [("ps", tb)], out=pv2[0:nt, pr, :], in_=xact[pr][:, t0:t0 + nt], identity=ident_bf)
                I("pe", "transpose", ["Bact", "ident_bf"], [("ps", tb)], out=pv2[0:nt, 2, :], in_=Bact[:, t0:t0 + nt], identity=ident_bf)
                if inter:
                    for pr in range(2):
                        I("pe", "matmul", [kdAx, "tri_f"], [("ps", b1)], PS[b1][:, 128 + pr * 128:128 + pr * 128 + nt],
                          dAx[0:nt, pr, :], tri_f[0:nt, 0:nt], start=True, stop=True)
                for hq in range(4):
                    I("act", "activation", [("ps", pb), "negcum_g"], [("D4", par, hq)], out=D4[0:nt, hq, 0:nt], in_=PS[pb][0:nt, hq * 128:hq * 128 + nt],
                      func=AF.Exp, bias=negcum_g[0:nt, c, hq:hq + 1], scale=1.0)
                    I("dve", "scalar_tensor_tensor", [("D4", par, hq), "dt_f", ("ps", b1)], [("Wt4", par, hq)], out=Wt4[0:nt, hq, 0:nt],
                      in0=D4[0:nt, hq, 0:nt], scalar=dt_tok[0:nt, c, 4 * g + hq:4 * g + hq + 1], in1=PS[b1][0:nt, 0:nt], op0=ALU.mult, op1=ALU.mult)
                copy_on("act", xB_tok[0:nt, :, :], pv2[0:nt, 0:3, :], [("ps", tb)], [kxB])
                if inter:
                    for pr in range(2):
                        I("act", "activation", [("ps", b1)], [("ecx", par, pr)], out=ecxs[par][pr][:, 0:nt],
                          in_=PS[b1][:, 128 + pr * 128:128 + pr * 128 + nt], func=AF.Exp)
                D4k = [("D4", par, hq) for hq in range(4)]
                I("dve", "tensor_tensor", D4k + ["dt_f"], [kw4], out=w4[0:nt, :], in0=D4[0:nt, :, nt - 1], in1=dt_tok[0:nt, c, hs], op=ALU.mult)
                I("dve", "tensor_tensor", [kxB, kw4], [kxw], out=xw[0:nt, :, :],
                  in0=xB_tok[0:nt, 0:2, :].rearrange("p a (b q) -> p (a b) q", q=64),
                  in1=w4[0:nt, :].rearrange("p (h o) -> p h o", o=1).broadcast_to([nt, 4, 64]), op=ALU.mult)

            def u2(g, c, first, inter):
                nt, t0, par, tt_, (pb, b1, yb, tb) = geo(c)
                Wt4, xB_tok, xw = tt_["Wt4"], tt_["xB_tok"], tt_["xw"]
                kxB, kxw = ("xB_tok", par), ("xw", par)
                yall = [rA, rB]
                ykey = ["rA", "rB"]
                for pr in range(2):
                    for hh in range(2):
                        I("pe", "matmul", [kxB, ("Wt4", par, 2 * pr + hh)], [("ps", yb)], PS[yb][hh * 64:(hh + 1) * 64, pr * 128:pr * 128 + nt],
                          xB_tok[0:nt, pr, hh * 64:(hh + 1) * 64], Wt4[0:nt, 2 * pr + hh, 0:nt], start=True, stop=True)
                    if inter:
                        I("pe", "matmul", ["Hbf", "Cact"], [("ps", yb)], PS[yb][:, 256 + pr * 128:256 + pr * 128 + nt], Hbf[:, pr * 128:(pr + 1) * 128],
                          Cact[:, t0:t0 + nt], start=True, stop=True)
                I("pe", "matmul", [kxB, kxw], [("ps", tb)], PS[tb][:, 256:512], xB_tok[0:nt, 2, :], xw[0:nt, :, :].rearrange("p h q -> p (h q)"),
                  start=True, stop=True)
                if first:
                    I("dve", "tensor_copy", [("ps", tb)], ["Hs"], out=Hs, in_=PS[tb][:, 256:512])
                else:
                    I("dve", "tensor_tensor", ["Hs", "etot_g"], ["Hs"], out=Hs.rearrange("p (h q) -> p h q", q=64), in0=Hs.rearrange("p (h q) -> p h q", q=64),
                      in1=etot_g[:, c, :].rearrange("p (h o) -> p h o", o=1).broadcast_to([128, 4, 64]), op=ALU.mult)
                    I("dve", "tensor_tensor", ["Hs", ("ps", tb)], ["Hs"], out=Hs, in0=PS[tb][:, 256:512], in1=Hs, op=ALU.add)
                for pr in range(2):
                    if inter:
                        I("dve", "tensor_tensor", [("ps", yb), ("ecx", par, pr)], ["t1"], out=t1[:, 0:nt], in0=PS[yb][:, 256 + pr * 128:256 + pr * 128 + nt],
                          in1=ecxs[par][pr][:, 0:nt], op=ALU.mult)
                        I("dve", "tensor_tensor", [("ps", yb), "t1"], [ykey[pr]], out=yall[pr][:, t0:t0 + nt], in0=PS[yb][:, pr * 128:pr * 128 + nt],
                          in1=t1[:, 0:nt], op=ALU.add)
                    else:
                        I("dve", "tensor_copy", [("ps", yb)], [ykey[pr]], out=yall[pr][:, t0:t0 + nt], in_=PS[yb][:, pr * 128:pr * 128 + nt])

            def hbf_update():
                I("act", "activation", ["Hs"], ["Hbf"], out=Hbf, in_=Hs, func=AF.Copy)

            def state_out(g, row0):
                for pr in range(2):
                    hsx = cntb["h"] % 2
                    cntb["h"] += 1
                    I("pe", "transpose", ["Hs", "ident_f"], [("ps", 3)], out=PS[3][:, 0:128], in_=Hs[:, pr * 128:(pr + 1) * 128], identity=ident_f)
                    copy_on("act", hst[hsx], PS[3][:, 0:128], [("ps", 3)], ["hst"])
                    DMA("sp", ssd_out[row0 + g * 256 + pr * 128:row0 + g * 256 + (pr + 1) * 128, :], hst[hsx], ["hst"], [])

            def state_in(g):
                for pr in range(2):
                    hsx = cntb["h"] % 2
                    cntb["h"] += 1
                    DMA("sp", hst[hsx], sssd[g * 256 + pr * 128:g * 256 + (pr + 1) * 128, :], [], ["hst"])
                    I("pe", "transpose", ["hst", "ident_f"], [("ps", 3)], out=PS[3][:, 0:128], in_=hst[hsx], identity=ident_f)
                    I("dve", "tensor_copy", [("ps", 3)], ["Hs"], out=Hs[:, pr * 128:(pr + 1) * 128], in_=PS[3][:, 0:128])
                I("act", "activation", ["Hs"], ["Hbf"], out=Hbf, in_=Hs, func=AF.Copy)

            ngroups = 8 if not dbg else dbg.get("ngroups", 8)
            for g in range(ngroups):
                proj_conv(XBC0 + g * 256, xact[0], ("xact", 0))
                proj_conv(XBC0 + g * 256 + 128, xact[1], ("xact", 1))
                proj_conv(XBC0 + 2048 + g * 128, Bact, "Bact")
                proj_conv(XBC0 + 3072 + g * 128, Cact, "Cact")
                S.fence(W1KEYS, T1KEYS)
                S.fence(ZKEYS, EKEYS)
                group_pre(g)
                u1a(g, 0, False)
                for st_ in range(18):
                    if st_ + 1 < 17:
                        u1a(g, st_ + 1, True)
                    if st_ < 17:
                        u1b(g, st_, inter=(st_ > 0))
                    if st_ >= 1:
                        c = st_ - 1
                        if c == 16:
                            state_out(g, 0)
                            state_in(g)
                        u2(g, c, first=(c == 0), inter=(c > 0))
                        hbf_update()
                state_out(g, 2048)
                S.fence(T1KEYS, W1KEYS)
                S.fence(EKEYS, ZKEYS)
                yall = [rA, rB]
                ykey = ["rA", "rB"]
                for pr in range(2):
                    fcx = 16 + 2 * g + pr
                    I("dve", "scalar_tensor_tensor", [("xact", pr), "dskp", ykey[pr]], [ykey[pr]], out=yall[pr][:, 0:T], in0=xact[pr][:, 0:T],
                      scalar=dskp[:, 2 * g + pr:2 * g + pr + 1], in1=yall[pr][:, 0:T], op0=ALU.mult, op1=ALU.add)

                    def evz(bk, tbi, t0, n, pr=pr, fcx=fcx):
                        zi = 0
                        while zpend:
                            zpend.pop(0)()
                        cntb["z"] += 1
                        I("act", "activation", [("ps", bk)], [("zs", zi)], out=zs_t[zi][:, 0:n], in_=PS[bk][:, 0:n], func=AF.Silu)
                        I("dve", "tensor_tensor", [ykey[pr], ("zs", zi)], [ykey[pr]], out=yall[pr][:, t0:t0 + n], in0=yall[pr][:, t0:t0 + n],
                          in1=zs_t[zi][:, 0:n], op=ALU.mult)
                        I("dve", "tensor_scalar", [ykey[pr], "FV"], [("omb", zi)], out=om_b[zi][:, 0:n], in0=yall[pr][:, t0:t0 + n],
                          scalar1=FV[:, fcx, 5:6], scalar2=None, op0=ALU.mult)
                        DMA("sp", omix_d[fcx, :, t0:t0 + n], om_b[zi][:, 0:n], [("omb", zi)], [])
                        I("act", "activation", [ykey[pr]], [("zs", zi)], out=zs_t[zi][:, 0:n], in_=yall[pr][:, t0:t0 + n], func=AF.Square)
                        def ssq_mm(t0=t0, n=n, zi=zi):
                            for j in range((n + 127) // 128):
                                tt = t0 // 128 + j
                                nt = min(128, n - j * 128)
                                I("pe", "matmul", [("zs", zi), "ones_bf"], [("ps", 5)], PS[5][0:nt, tt:tt + 1], zs_t[zi][:, j * 128:j * 128 + nt],
                                  ones_bf[:, 0:1], start=True, stop=True)
                        zpend.append(ssq_mm)

                    proj_fmb(Z0 + g * 256 + pr * 128, evz)
                    while zpend:
                        zpend.pop(0)()
                    I("dve", "tensor_tensor", [("ps", 5), "ssq_ssd"], ["ssq_ssd"], out=ssq_ssd[:, 0:16], in0=PS[5][:, 0:16], in1=ssq_ssd[:, 0:16], op=ALU.add)
                    I("dve", "tensor_tensor", [("ps", 5), "ssq_ssd"], ["ssq_ssd"], out=ssq_ssd[0:64, 16:17], in0=PS[5][0:64, 16:17],
                      in1=ssq_ssd[0:64, 16:17], op=ALU.add)
            if dbg:
                I("dve", "tensor_copy", ["ssq_ssd"], ["dbgt"], out=t1[:, 0:NT], in_=ssq_ssd)
                DMA("sp", dbg_d[:, 32:32 + NT], t1[:, 0:NT], ["dbgt"], [])
            S.barrier()
            A.reset(mP)

        if "p3" in phases:
            A.reset(mC)
            if p3only:
                DMA("sp", ssq_sb, ssq_in[:, 0:NT], [], ["ssq_sb"])
                DMA("sp", ssq_ssd, ssq_in[:, NT:2 * NT], [], ["ssq_ssd"])
            GROUPS = [list(range(0, 6)), list(range(6, 12)), list(range(12, 17))]
            rs = A.alloc([4, NT], F32)
            for (src, key, a, b) in ((ssq_sb, "ssq_sb", 0, 1), (ssq_ssd, "ssq_ssd", 2, 3)):
                I("dve", "tensor_scalar", [key], [("rs", a)], out=rs[:, a, :], in0=src, scalar1=1.0 / 2048, scalar2=EPS,
                  op0=ALU.mult, op1=ALU.add)
                I("act", "activation", [("rs", a)], [("rsl", a)], out=rs[:, a, :], in_=rs[:, a, :], func=AF.Ln)
                I("act", "activation", [("rsl", a)], [("rs", b)], out=rs[:, b, :], in_=rs[:, a, :], func=AF.Exp, scale=-0.5)
            m3 = A.mark()

            def stage3(first):
                A.reset(m3)
                src_d = omix_d if first else x1T_d
                W = w_out if first else w_gate
                nrm = norm_post if first else ple_norm
                res_src = x if first else x1_d
                res_dst = x1_d if first else y
                oT = A.alloc([32, 768], BF16)
                msb = A.alloc([6, D], F32)
                NW3 = 4 if first else 3
                w3 = [A.alloc([4, 512], BF16) for _ in range(NW3)]
                nrmr = [A.alloc([512], F32) for _ in range(2)]
                xr = [A.alloc([1024], F32) for _ in range(2)]
                junk = A.alloc([512], F32) if first else None
                stt = A.alloc([6, 12], F32)
                rst = A.alloc([6], F32)
                if first:
                    rB = [A.alloc([768], F32) for _ in range(2)]
                    dg = A.alloc([128], F32)
                    x1b = [A.alloc([1024], BF16) for _ in range(4)]
                    x1Ts = [A.alloc([8, 128], BF16) for _ in range(2)]
                else:
                    pT = A.alloc([2, 768], BF16)
                    pst = A.alloc([256], F32)
                    pbf = A.alloc([256], BF16)
                    wp = [A.alloc([2, 512], BF16) for _ in range(2)]
                    sg = [A.alloc([512], F32) for _ in range(2)]
                    tmp = [A.alloc([512], F32) for _ in range(1)] * 2
                    e_sb = A.alloc([6, 512], F32)
                c3 = {"w": 0, "n": 0, "x": 0, "e": 0, "t": 0}

                def geom(tiles):
                    nts = [128 if t < 16 else 64 for t in tiles]
                    return tiles[0] * 128, nts, sum(nts)

                def load_group(tiles):
                    t0, nts, ng = geom(tiles)
                    S.tag = "load_group"
                    for c8 in range(4):
                        DMA("sp", oT[:, c8 * 8:(c8 + 1) * 8, 0:ng],
                            src_d[c8 * 8:(c8 + 1) * 8, :, t0:t0 + ng].rearrange("c p t -> p c t"), [], [("oT", c8)])
                    if first:
                        for which, col in ((0, 1), (1, 3)):
                            for j, tt in enumerate(tiles):
                                nt = nts[j]
                                bank = 6 + (j * 128) // 512
                                cc = (j * 128) % 512
                                I("dve", "tensor_scalar", ["ident_f", ("rs", col)], ["dg"], out=dg[0:nt, 0:nt], in0=ident_f[0:nt, 0:nt],
                                  scalar1=rs[0:nt, col, tt:tt + 1], scalar2=None, op0=ALU.mult)
                                I("pe", "matmul", ["dg", "ones_f"], [("ps", bank)], PS[bank][:, cc:cc + nt], ones_f[0:nt, :], dg[0:nt, 0:nt],
                                  start=True, stop=True)
                            n6 = min(ng, 512)
                            copy_on("act", rB[which][:, 0:n6], PS[6][:, 0:n6], [("ps", 6)], [("rB", which)])
                            if ng > 512:
                                copy_on("act", rB[which][:, 512:ng], PS[7][:, 0:ng - 512], [("ps", 7)], [("rB", which)])
                        for c8 in range(4):
                            which = 0 if c8 < 2 else 1
                            bc = rB[which][:, 0:ng].rearrange("p (o t) -> p o t", o=1).broadcast_to([128, 8, ng])
                            I("dve", "tensor_tensor", [("oT", c8), ("rB", which)], [("oT", c8)], out=oT[:, c8 * 8:(c8 + 1) * 8, 0:ng],
                              in0=oT[:, c8 * 8:(c8 + 1) * 8, 0:ng], in1=bc, op=ALU.mult)
                    else:
                        for j, tt in enumerate(tiles):
                            nt = nts[j]
                            r0 = tt * 128
                            DMA("sp", pst[0:nt, :], p_in[r0:r0 + nt, :], [], ["pst"])
                            I("dve", "tensor_copy", ["pst"], ["pbf"], out=pbf[0:nt, :], in_=pst[0:nt, :])
                            bk = 6 + j % 2
                            pvv = psb(bk).rearrange("p (j n) -> p j n", j=8)
                            for c in range(2):
                                I("pe", "transpose", ["pbf", "ident_bf"], [("ps", bk)], out=pvv[:, c, 0:nt],
                                  in_=pbf[0:nt, c * 128:(c + 1) * 128], identity=ident_bf[0:nt, 0:nt])
                            copy_on(evac_eng(), pT[:, :, j * 128:j * 128 + nt], pvv[:, 0:2, 0:nt], [("ps", bk)], [("pT", j)])

                def e_matmuls(tiles, ns):
                    t0, nts, ng = geom(tiles)
                    S.tag = "e_mm"
                    for j, tt in enumerate(tiles):
                        nt = nts[j]
                        eb = 6 + c3["e"] % 2
                        c3["e"] += 1
                        for c in range(2):
                            I("pe", "matmul", [("pT", j), ("wp", ns)], [("ps", eb)], PS[eb][0:nt, :], pT[:, c, j * 128:j * 128 + nt],
                              wp[ns][:, c, :], start=(c == 0), stop=(c == 1))
                        copy_on("act", e_sb[0:nt, j, :], PS[eb][0:nt, :], [("ps", eb)], [("e_sb", j)])

                def evac(tiles, cb, ns):
                    t0, nts, ng = geom(tiles)
                    if first:
                        for j, tt in enumerate(tiles):
                            copy_on("act" if j % 2 == 0 else "dve", msb[0:nts[j], j, cb * 512:(cb + 1) * 512], PS[j][0:nts[j], :],
                                    [("ps", j)], [("msb", j, cb)])
                    for j, tt in enumerate(tiles):
                        nt = nts[j]
                        cs = slice(cb * 512, (cb + 1) * 512)
                        if first:
                            I("dve", "scalar_tensor_tensor", [("msb", j, cb)], ["junk", ("stt", j, cb)], out=junk[0:nt, :], in0=msb[0:nt, j, cs],
                              scalar=1.0, in1=msb[0:nt, j, cs], op0=ALU.mult, op1=ALU.mult, accum_out=stt[0:nt, j, cb:cb + 1])
                            I("dve", "tensor_tensor", [("msb", j, cb), ("nrm", ns)], [("msb", j, cb)], out=msb[0:nt, j, cs],
                              in0=msb[0:nt, j, cs], in1=nrmr[ns][0:nt, :], op=ALU.mult)
                        else:
                            es = c3["t"] % 2
                            c3["t"] += 1
                            I("act", "activation", [("ps", j)], [("sg", es)], out=sg[es][0:nt, :], in_=PS[j][0:nt, :], func=AF.Sigmoid)
                            I("dve", "tensor_tensor", [("e_sb", j), ("sg", es)], ["tmp"], out=tmp[es][0:nt, :], in0=e_sb[0:nt, j, :],
                              in1=sg[es][0:nt, :], op=ALU.mult)
                            I("dve", "scalar_tensor_tensor", ["tmp"], [("sg", es), ("stt", j, cb)], out=sg[es][0:nt, :], in0=tmp[es][0:nt, :],
                              scalar=1.0, in1=tmp[es][0:nt, :], op0=ALU.mult, op1=ALU.mult, accum_out=stt[0:nt, j, cb:cb + 1])
                            I("dve", "tensor_tensor", ["tmp", ("nrm", ns)], [("msb", j, cb)], out=msb[0:nt, j, cs],
                              in0=tmp[es][0:nt, :], in1=nrmr[ns][0:nt, :], op=ALU.mult)

                def stats(tiles):
                    t0, nts, ng = geom(tiles)
                    for j, tt in enumerate(tiles):
                        nt = nts[j]
                        sk = [("stt", j, cb) for cb in range(8)]
                        I("dve", "tensor_reduce", sk, [("stt8", j)], out=stt[0:nt, j, 8:9], in_=stt[0:nt, j, 0:8], axis=mybir.AxisListType.X,
                          op=ALU.add)
                        I("dve", "tensor_scalar", [("stt8", j)], [("stt9", j)], out=stt[0:nt, j, 9:10], in0=stt[0:nt, j, 8:9],
                          scalar1=1.0 / D, scalar2=EPS, op0=ALU.mult, op1=ALU.add)
                        I("act", "activation", [("stt9", j)], [("stt10", j)], out=stt[0:nt, j, 10:11], in_=stt[0:nt, j, 9:10], func=AF.Ln)
                        I("act", "activation", [("stt10", j)], [("rst", j)], out=rst[0:nt, j:j + 1], in_=stt[0:nt, j, 10:11],
                          func=AF.Exp, scale=-0.5)

                def epiA(tiles, pc):
                    t0, nts, ng = geom(tiles)
                    n = len(tiles)
                    base = c3["x"]
                    c3["x"] += n
                    pcs = slice(pc * 1024, (pc + 1) * 1024)

                    def load(j):
                        xs = (base + j) % 2
                        r0 = tiles[j] * 128
                        DMA("sp", xr[xs][0:nts[j], :], res_src[r0:r0 + nts[j], pcs], [], [("xr", xs)])

                    load(0)
                    if n > 1:
                        load(1)
                    for j, tt in enumerate(tiles):
                        nt = nts[j]
                        r0 = tt * 128
                        xs = (base + j) % 2
                        mk = [("msb", j, 2 * pc), ("msb", j, 2 * pc + 1)]
                        I("dve", "scalar_tensor_tensor", mk + [("rst", j), ("xr", xs)], mk, out=msb[0:nt, j, pcs], in0=msb[0:nt, j, pcs],
                          scalar=rst[0:nt, j:j + 1], in1=xr[xs][0:nt, :], op0=ALU.mult, op1=ALU.add)
                        DMA("sp", res_dst[r0:r0 + nt, pcs], msb[0:nt, j, pcs], mk, [])
                        if j + 2 < n:
                            load(j + 2)

                def epiB(tiles, pc):
                    if not first:
                        return
                    S.tag = "epiB"
                    t0, nts, ng = geom(tiles)
                    for j, tt in enumerate(tiles):
                        nt = nts[j]
                        r0 = tt * 128
                        xs = c3["t"] % 4
                        c3["t"] += 1
                        pcs = slice(pc * 1024, (pc + 1) * 1024)
                        mk = [("msb", j, 2 * pc), ("msb", j, 2 * pc + 1)]
                        copy_on("act", x1b[xs][0:nt, :], msb[0:nt, j, pcs], mk, [("x1b", xs)])
                        bk = 6 + xs % 2
                        pvv = psb(bk).rearrange("p (j n) -> p j n", j=8)
                        for c in range(8):
                            I("pe", "transpose", [("x1b", xs), "ident_bf"], [("ps", bk)], out=pvv[:, c, 0:nt],
                              in_=x1b[xs][0:nt, c * 128:(c + 1) * 128], identity=ident_bf[0:nt, 0:nt])
                        copy_on("act", x1Ts[xs % 2][:, :, 0:nt], pvv[:, :, 0:nt], [("ps", bk)], [("x1Ts", xs % 2)])
                        DMA("sp", x1T_d[pc * 8:(pc + 1) * 8, :, r0:r0 + nt].rearrange("c p t -> p c t"), x1Ts[xs % 2][:, :, 0:nt],
                            [("x1Ts", xs % 2)], [])

                prev = None
                load_group(GROUPS[0])
                for gi, tiles in enumerate(GROUPS):
                    t0, nts, ng = geom(tiles)
                    for cb in range(8):
                        ns = c3["n"] % 2
                        c3["n"] += 1
                        DMA("sp", nrmr[ns], nrm[cb * 512:(cb + 1) * 512].partition_broadcast(128), [], [("nrm", ns)])
                        if not first:
                            DMA("pool", wp[ns], w_proj[:, cb * 512:(cb + 1) * 512].rearrange("(c p) n -> p c n", p=128), [], [("wp", ns)])
                        if prev is not None:
                            if cb % 2 == 0:
                                epiB(prev, cb // 2)
                            elif cb // 2 + 1 < 4:
                                epiA(prev, cb // 2 + 1)
                        for fq in range(8):
                            slot = c3["w"] % NW3
                            c3["w"] += 1
                            DMA("pool", w3[slot], W[fq * 512:(fq + 1) * 512, cb * 512:(cb + 1) * 512].rearrange("(c p) n -> p c n", p=128),
                                [], [("w3", slot)])
                            S.tag = "mm.fq%d.%s" % (fq, "a" if first else "b")
                            order = [(f4, j) for f4 in range(4) for j in range(len(tiles))]
                            if fq == 0:
                                order = [(f4, j) for j in range(len(tiles)) for f4 in range(4)]
                            for f4, j in order:
                                fc = fq * 4 + f4
                                nt = nts[j]
                                I("pe", "matmul", [("w3", slot), ("oT", fc // 8)], [("ps", j)], PS[j][0:nt, :],
                                  oT[:, fc, j * 128:j * 128 + nt], w3[slot][:, f4, :], start=(fc == 0), stop=(fc == 31))
                            if (not first) and fq == 5:
                                e_matmuls(tiles, ns)
                        evac(tiles, cb, ns)
                    if gi + 1 < len(GROUPS):
                        load_group(GROUPS[gi + 1])
                    stats(tiles)
                    epiA(tiles, 0)
                    prev = tiles
                for pc in range(4):
                    epiB(prev, pc)
                    if pc + 1 < 4:
                        epiA(prev, pc + 1)
                S.barrier()

            stage3(True)
            stage3(False)

        S.barrier()
        nsig = S.emit()
    nc._pe_tags = S.pe_tags
    return nc, nsig


_CACHE = {}


def _shard_inputs(inputs, c):
    f = np.float32
    g = lambda k: np.asarray(inputs[k])
    m = {
        "x": np.concatenate([g("x_prompt")[c], g("x_sample")[c]], axis=0).astype(f, copy=False),
        "p": np.concatenate([g("p_prompt")[0, c], g("p_sample")[0, c]], axis=0).astype(f, copy=False),
        "ck": np.ascontiguousarray(g("cache_k")[0, c].reshape(PAST, 2048)),
        "cv": np.ascontiguousarray(g("cache_v")[0, c].reshape(PAST, 2048)),
        "sconv": np.ascontiguousarray(g("state_conv")[0, c]),
        "sssd": np.ascontiguousarray(g("state_ssd")[0, c].reshape(2048, 128)),
        "w_in": np.ascontiguousarray(g("w_in")[0]),
        "w_out": np.ascontiguousarray(g("w_out")[0]),
        "w_gate": np.ascontiguousarray(g("w_ple_gate")[0]),
        "w_proj": np.ascontiguousarray(g("w_ple_proj")[0]),
    }
    for k in ("norm_pre", "norm_post", "ple_norm", "conv_b", "sb_norm", "ssd_norm", "dt_bias", "a_log", "d_skip"):
        m[k] = np.ascontiguousarray(g(k)[0].reshape(-1))
    m["conv_w"] = np.ascontiguousarray(g("conv_w")[0])
    return m


def kernel(**inputs):
    if "nc" not in _CACHE:
        _CACHE["nc"] = build_program()[0]
    nc = _CACHE["nc"]
    in_maps = [_shard_inputs(inputs, c) for c in range(N_CORES)]
    res = run_bass_kernel_spmd(nc, in_maps, core_ids=list(range(N_CORES)))
    R = res.results
    f = np.float32
    y = np.stack([r["y"] for r in R])
    ko = np.stack([r["k_out"] for r in R])
    vo = np.stack([r["v_out"] for r in R])
    co = np.stack([r["conv_out"] for r in R])
    so = np.stack([r["ssd_out"] for r in R])
    y_prompt = np.ascontiguousarray(y[:, :TP]).astype(f, copy=False)
    y_sample = np.ascontiguousarray(y[:, TP:]).astype(f, copy=False)
    k_prompt = np.ascontiguousarray(ko[:, :TP]).reshape(1, N_CORES, TP, 16, 128)
    v_prompt = np.ascontiguousarray(vo[:, :TP]).reshape(1, N_CORES, TP, 16, 128)
    k_sample = np.ascontiguousarray(ko[:, TP:]).reshape(1, N_CORES, TS, 16, 128)
    v_sample = np.ascontiguousarray(vo[:, TP:]).reshape(1, N_CORES, TS, 16, 128)
    conv_prompt = np.ascontiguousarray(co[:, 0:3]).reshape(1, N_CORES, 3, D)
    conv_sample = np.ascontiguousarray(co[:, 3:6]).reshape(1, N_CORES, 3, D)
    ssd_prompt = np.ascontiguousarray(so[:, 0:2048]).reshape(1, N_CORES, 32, 64, 128)
    ssd_sample = np.ascontiguousarray(so[:, 2048:4096]).reshape(1, N_CORES, 32, 64, 128)
    return (y_prompt, y_sample, k_prompt, v_prompt, conv_prompt, ssd_prompt, k_sample, v_sample, conv_sample, ssd_sample)
```

```python
import contextlib
import numpy as np
import concourse.bass as bass
import concourse.mybir as mybir
from concourse.bass_utils import run_bass_kernel_spmd

F32 = mybir.dt.float32
BF16 = mybir.dt.bfloat16
AF = mybir.ActivationFunctionType
ALU = mybir.AluOpType

N_CORES = 8
TP, TS = 2048, 64
T = TP + TS
NT = 17
D = 4096
NIN = 14368
PAST = 1024
EPS = 1e-6
Q0, K0, V0, G0, XBC0, Z0, DT0 = 0, 2048, 4096, 6144, 8192, 12288, 14336
TBS = [(0, 512), (512, 512), (1024, 512), (1536, 512), (2048, 64)]
NEG = -30000.0

ENGINES = ["pe", "act", "dve", "pool", "sp"]
SEM_CHUNK = 30000


class _Op:
    __slots__ = ("eng", "seq", "fn", "waits", "slot", "use", "sig")

    def __init__(self):
        self.sig = None


class Sched:
    def __init__(self, nc, slots_per_queue=8):
        self.nc = nc
        self.streams = {e: [] for e in ENGINES}
        self.nseq = {e: 0 for e in ENGINES}
        self.known = {e: {} for e in ENGINES}
        self.lastw = {}
        self.readers = {}
        self.clocks = {}
        self.waited = set()
        self.nslots = slots_per_queue
        self.slot_rr = {q: 0 for q in ("sp", "act", "pool")}
        self.slot_use = {}
        self.opmap = {}
        self.tag = ""
        self.pe_tags = []

    def _deps(self, eng, reads, writes):
        deps = set()
        for k in reads:
            ev = self.lastw.get(k)
            if ev is not None:
                deps.add(ev)
            if isinstance(k, tuple) and k[0] == "ps":
                for src, seq in self.readers.get(k, {}).items():
                    if src != eng:
                        deps.add((src, seq))
        for k in writes:
            ev = self.lastw.get(k)
            if ev is not None:
                deps.add(ev)
            for src, seq in self.readers.get(k, {}).items():
                deps.add((src, seq))
        if eng == "pe":
            deps = {d for d in deps if d[0] != "pe"}
        return deps

    def _wait_list(self, eng, deps):
        kn = self.known[eng]
        waits = []
        for ev in sorted(deps, key=lambda d: (str(d[0]), d[1])):
            src, seq = ev
            if kn.get(src, 0) >= seq:
                continue
            waits.append(ev)
            kn[src] = seq
            for s2, v2 in self.clocks.get(ev, {}).items():
                if kn.get(s2, 0) < v2:
                    kn[s2] = v2
            if not isinstance(src, tuple):
                self.waited.add(ev)
        return waits

    def _mark(self, ev, reads, writes):
        src, seq = ev
        for k in reads:
            self.readers.setdefault(k, {})[src] = seq
        for k in writes:
            self.lastw[k] = ev
            self.readers[k] = {}

    def op(self, eng, fn, reads=(), writes=()):
        deps = self._deps(eng, reads, writes)
        o = _Op()
        o.eng = eng
        o.waits = self._wait_list(eng, deps)
        self.nseq[eng] += 1
        o.seq = self.nseq[eng]
        o.fn = fn
        o.slot = None
        if eng == "pe":
            self.pe_tags.append(self.tag)
        ev = (eng, o.seq)
        self.clocks[ev] = dict(self.known[eng])
        self.streams[eng].append(o)
        self.opmap[ev] = o
        self._mark(ev, reads, writes)
        return ev

    def dma(self, q, fn, reads=(), writes=()):
        i = self.slot_rr[q]
        self.slot_rr[q] = (i + 1) % self.nslots
        slot = (q, i)
        use = self.slot_use.get(slot, 0)
        deps = self._deps(slot, reads, writes)
        if use > 0:
            deps.add((slot, use))
        o = _Op()
        o.eng = q
        o.waits = self._wait_list(q, deps)
        o.seq = None
        o.fn = fn
        o.slot = slot
        o.use = use + 1
        self.slot_use[slot] = use + 1
        ev = (slot, use + 1)
        self.clocks[ev] = dict(self.known[q])
        self.streams[q].append(o)
        self._mark(ev, reads, writes)
        return ev

    def fence(self, src_keys, dst_keys):
        evs = {}
        for k in src_keys:
            ev = self.lastw.get(k)
            if ev is not None:
                evs[ev[0]] = max(evs.get(ev[0], 0), ev[1])
            for s_, q_ in self.readers.get(k, {}).items():
                evs[s_] = max(evs.get(s_, 0), q_)
        for k in dst_keys:
            r = self.readers.setdefault(k, {})
            for s_, q_ in evs.items():
                r[s_] = max(r.get(s_, 0), q_)

    def barrier(self):
        deps = {(slot, use) for slot, use in self.slot_use.items()}
        for e in ENGINES:
            if self.nseq[e] > 0:
                deps.add((e, self.nseq[e]))
        for e in ENGINES:
            o = _Op()
            o.eng = e
            o.waits = self._wait_list(e, {d for d in deps if d[0] != e})
            o.seq = None
            o.fn = None
            o.slot = None
            self.streams[e].append(o)
        self.lastw = {}
        self.readers = {}

    def emit(self):
        nc = self.nc
        nsig = {}
        for e in ENGINES:
            c = 0
            for o in self.streams[e]:
                if o.seq is not None and (e, o.seq) in self.waited:
                    c += 1
                    o.sig = c
            nsig[e] = c
        with contextlib.ExitStack() as st:
            csem = {}
            for e in ENGINES:
                n = (nsig[e] + SEM_CHUNK - 1) // SEM_CHUNK
                csem[e] = [st.enter_context(nc.semaphore(f"c_{e}_{j}")) for j in range(n)]
            dsem = {}
            for slot in self.slot_use:
                dsem[slot] = st.enter_context(nc.semaphore(f"d_{slot[0]}_{slot[1]}"))

            def resolve(ev):
                src, seq = ev
                if isinstance(src, tuple):
                    return dsem[src], 16 * seq
                r = self.opmap[ev].sig
                return csem[src][(r - 1) // SEM_CHUNK], (r - 1) % SEM_CHUNK + 1

            def run(e, h):
                for o in self.streams[e]:
                    for ev in o.waits:
                        s, v = resolve(ev)
                        h.wait_ge(s, v)
                    if o.fn is None:
                        continue
                    ins = o.fn(h)
                    if o.slot is not None:
                        ins.then_inc(dsem[o.slot], 16)
                    elif o.sig is not None:
                        ins.then_inc(csem[e][(o.sig - 1) // SEM_CHUNK], 1)

            with nc.Block() as block:
                @block.tensor
                def _(h):
                    run("pe", h)

                @block.scalar
                def _(h):
                    run("act", h)

                @block.vector
                def _(h):
                    run("dve", h)

                @block.gpsimd
                def _(h):
                    run("pool", h)

                @block.sync
                def _(h):
                    run("sp", h)
        return nsig


class Arena:
    def __init__(self, tile_f32, nwords):
        self.t = tile_f32
        self.n = nwords * 4
        self.off = 0

    def mark(self):
        return self.off

    def reset(self, m):
        self.off = m

    def alloc(self, shape_free, dtype):
        esz = 4 if dtype == F32 else 2
        n = 1
        for s in shape_free:
            n *= s
        nbytes = (n * esz + 31) // 32 * 32
        assert self.off + nbytes <= self.n, f"arena overflow: {self.off}+{nbytes} > {self.n}"
        w0 = self.off // 4
        v = self.t[:, w0:w0 + nbytes // 4]
        self.off += nbytes
        if dtype != F32:
            v = v.bitcast(dtype)
        v = v[:, 0:n]
        if len(shape_free) == 2:
            v = v.rearrange("p (a b) -> p a b", a=shape_free[0])
        elif len(shape_free) == 3:
            v = v.rearrange("p (a b c) -> p a b c", a=shape_free[0], b=shape_free[1])
        return v


def build_program(phases=("p1", "p2a", "p2b", "p3"), dbg=False):
    nc = bass.Bass("TRN2", target_bir_lowering=False)

    def din(name, shape):
        return nc.dram_tensor(name, shape, F32, kind="ExternalInput").ap()

    def dout(name, shape):
        return nc.dram_tensor(name, shape, F32, kind="ExternalOutput").ap()

    x = din("x", [T, D])
    p_in = din("p", [T, 256])
    ck = din("ck", [PAST, 2048])
    cv = din("cv", [PAST, 2048])
    sconv = din("sconv", [3, D])
    sssd = din("sssd", [2048, 128])
    w_in = din("w_in", [D, NIN])
    w_out = din("w_out", [D, D])
    w_gate = din("w_gate", [D, D])
    w_proj = din("w_proj", [256, D])
    norm_pre = din("norm_pre", [D])
    norm_post = din("norm_post", [D])
    ple_norm = din("ple_norm", [D])
    conv_w = din("conv_w", [4, D])
    conv_b = din("conv_b", [D])
    sb_norm = din("sb_norm", [2048])
    ssd_norm = din("ssd_norm", [2048])
    dt_bias = din("dt_bias", [32])
    a_log = din("a_log", [32])
    d_skip = din("d_skip", [32])

    y = dout("y", [T, D])
    k_out = dout("k_out", [T, 2048])
    v_out = dout("v_out", [T, 2048])
    conv_out = dout("conv_out", [6, D])
    ssd_out = dout("ssd_out", [4096, 128])

    p3only = bool(dbg) and dbg.get("p3only", False)
    omix_d = nc.dram_tensor("omix_scr", [32, 128, T], BF16,
                            kind=("ExternalInput" if p3only else ("ExternalOutput" if dbg else "Internal"))).ap()
    ssq_in = din("ssq_in", [128, 2 * NT]) if p3only else None
    x1_d = nc.dram_tensor("x1_scr", [T, D], F32, kind="Internal").ap()
    x1T_d = nc.dram_tensor("x1T_scr", [32, 128, T], BF16, kind="Internal").ap()
    dbg_d = dout("dbg", [128, 4096]) if dbg else None

    S = Sched(nc)

    def row(ap):
        return ap.rearrange("(o n) -> o n", o=1)

    def I(eng, method, reads, writes, *args, **kw):
        return S.op(eng, lambda e: getattr(e, method)(*args, **kw), reads=reads, writes=writes)

    def DMA(q, out, in_, reads, writes):
        return S.dma(q, lambda e: e.dma_start(out=out, in_=in_), reads=reads, writes=writes)

    rr = {"ev": 0}

    def evac_eng():
        rr["ev"] += 1
        return "act" if rr["ev"] % 2 else "dve"

    def copy_on(eng, out, in_, reads, writes):
        if eng == "act":
            return I("act", "activation", reads, writes, out=out, in_=in_, func=AF.Copy)
        return I(eng, "tensor_copy", reads, writes, out=out, in_=in_)

    with contextlib.ExitStack() as st:
        NW = 52224
        arena_t = st.enter_context(nc.sbuf_tensor("arena", [128, NW], F32))
        A = Arena(arena_t, NW)
        PS = [st.enter_context(nc.psum_tensor(f"ps{i}", [128, 512], F32)) for i in range(8)]

        def psb(i):
            return PS[i][:].bitcast(BF16)

        ident_bf = A.alloc([128], BF16)
        ident_f = A.alloc([128], F32)
        negtri = A.alloc([128], BF16)
        negones = A.alloc([128], BF16)
        ones_bf = A.alloc([128], BF16)
        ones_f = A.alloc([128], F32)
        maskM = A.alloc([896], BF16)
        tri_f = A.alloc([128], F32)
        negmask2 = A.alloc([128], BF16)
        FV = A.alloc([32, 16], F32)
        ssq_sb = A.alloc([NT], F32)
        ssq_ssd = A.alloc([NT], F32)
        dtb_b = A.alloc([32], F32)
        alog_b = A.alloc([32], F32)
        dskp = A.alloc([16], F32)

        m0 = A.mark()
        zero_bf = A.alloc([896], BF16)
        I("pool", "memset", [], ["negones"], negones, -1.0)
        I("pool", "memset", [], ["ones_bf"], ones_bf, 1.0)
        I("pool", "memset", [], ["ones_f"], ones_f, 1.0)
        I("pool", "memset", [], ["zero_bf"], zero_bf, 0.0)
        I("pool", "memset", [], ["ssq_sb"], ssq_sb, 0.0)
        I("pool", "memset", [], ["ssq_ssd"], ssq_ssd, 0.0)
        I("pool", "affine_select", ["ones_bf"], ["ident_bf"], out=ident_bf, in_=ones_bf, pattern=[[-1, 128]],
          compare_op=ALU.is_equal, fill=0.0, base=0, channel_multiplier=1)
        I("pool", "affine_select", ["ones_f"], ["ident_f"], out=ident_f, in_=ones_f, pattern=[[-1, 128]],
          compare_op=ALU.is_equal, fill=0.0, base=0, channel_multiplier=1)
        I("pool", "affine_select", ["negones"], ["negtri"], out=negtri, in_=negones, pattern=[[-1, 128]],
          compare_op=ALU.is_ge, fill=0.0, base=0, channel_multiplier=1)
        I("pool", "affine_select", ["zero_bf"], ["maskM"], out=maskM, in_=zero_bf, pattern=[[1, 896]],
          compare_op=ALU.is_gt, fill=NEG, base=-384, channel_multiplier=-1)
        I("pool", "affine_select", ["ones_f"], ["tri_f"], out=tri_f, in_=ones_f, pattern=[[1, 128]],
          compare_op=ALU.is_ge, fill=0.0, base=0, channel_multiplier=-1)
        I("pool", "affine_select", ["zero_bf"], ["negmask2"], out=negmask2, in_=zero_bf[:, 0:128], pattern=[[1, 128]],
          compare_op=ALU.is_ge, fill=NEG, base=0, channel_multiplier=-1)
        CONSTS = ["ident_bf", "ident_f", "negtri", "negones", "ones_bf", "ones_f", "maskM", "tri_f", "negmask2"]

        fvs = A.alloc([D], F32)
        I("pool", "memset", [], ["fvs"], fvs[0:32, :], 0.0)
        DMA("sp", fvs[0:4, :], conv_w, [], ["fvs"])
        DMA("sp", fvs[4:5, :], row(conv_b), [], ["fvs"])
        DMA("sp", fvs[5:6, 0:2048], row(sb_norm), [], ["fvs"])
        DMA("sp", fvs[5:6, 2048:4096], row(ssd_norm), [], ["fvs"])
        DMA("sp", fvs[6:9, :], sconv, [], ["fvs"])
        DMA("sp", dtb_b, dt_bias.partition_broadcast(128), [], ["dtb_b"])
        DMA("sp", alog_b, a_log.partition_broadcast(128), [], ["alog_b"])
        for c in range(32):
            I("pe", "transpose", ["fvs", "ident_f"], [("ps", 7)], out=PS[7][:, c * 16:c * 16 + 16],
              in_=fvs[0:16, c * 128:(c + 1) * 128], identity=ident_f[0:16, 0:16])
        I("dve", "tensor_copy", [("ps", 7)], ["FV"], out=FV, in_=PS[7][:].rearrange("p (c r) -> p c r", c=32))
        S.barrier()
        A.reset(m0)

        mC = A.mark()
        hT = A.alloc([32, T], BF16)
        mP = A.mark()

        if "p1" in phases:
            npre_b = A.alloc([D], F32)
            xt = [A.alloc([D], F32) for _ in range(2)]
            hns = [A.alloc([D], BF16) for _ in range(2)]
            st1 = A.alloc([NT, 4], F32)
            DMA("sp", npre_b, norm_pre.partition_broadcast(128), [], ["npre_b"])
            def p1_load(tt):
                nt = 128 if tt < 16 else 64
                b = tt % 2
                DMA("sp", xt[b][0:nt, :], x[tt * 128:tt * 128 + nt, :], [], [("xt", b)])

            def p1_square(tt):
                nt = 128 if tt < 16 else 64
                b = tt % 2
                I("act", "activation", [("xt", b)], [("hn", b), ("st1", tt)], out=hns[b][0:nt, :], in_=xt[b][0:nt, :],
                  func=AF.Square, accum_out=st1[0:nt, tt, 0:1])

            def p1_stats(tt):
                nt = 128 if tt < 16 else 64
                I("dve", "tensor_scalar", [("st1", tt)], [("st1b", tt)], out=st1[0:nt, tt, 1:2], in0=st1[0:nt, tt, 0:1],
                  scalar1=1.0 / D, scalar2=EPS, op0=ALU.mult, op1=ALU.add)
                I("act", "activation", [("st1b", tt)], [("st1c", tt)], out=st1[0:nt, tt, 2:3], in_=st1[0:nt, tt, 1:2], func=AF.Ln)
                I("act", "activation", [("st1c", tt)], [("st1d", tt)], out=st1[0:nt, tt, 3:4], in_=st1[0:nt, tt, 2:3],
                  func=AF.Exp, scale=-0.5)

            def p1_main(tt):
                nt = 128 if tt < 16 else 64
                r0 = tt * 128
                b = tt % 2
                hn = hns[b]
                hk = ("hn", b)
                I("dve", "scalar_tensor_tensor", [("xt", b), ("st1d", tt), "npre_b"], [hk], out=hn[0:nt, :],
                  in0=xt[b][0:nt, :], scalar=st1[0:nt, tt, 3:4], in1=npre_b[0:nt, :], op0=ALU.mult, op1=ALU.mult)
                for c4 in range(4):
                    bk = (tt * 4 + c4) % 4
                    pv = psb(bk).rearrange("p (j n) -> p j n", j=8)
                    for j in range(8):
                        fc = c4 * 8 + j
                        I("pe", "transpose", [hk, "ident_bf"], [("ps", bk)], out=pv[:, j, 0:nt],
                          in_=hn[0:nt, fc * 128:(fc + 1) * 128], identity=ident_bf[0:nt, 0:nt])
                    copy_on(evac_eng(), hT[:, c4 * 8:(c4 + 1) * 8, r0:r0 + nt], pv[:, :, 0:nt], [("ps", bk)], [("hT", tt)])

            p1_load(0)
            p1_square(0)
            for tt in range(NT):
                if tt + 1 < NT:
                    p1_load(tt + 1)
                p1_stats(tt)
                if tt + 1 < NT:
                    p1_square(tt + 1)
                p1_main(tt)
            S.barrier()
            A.reset(mP)

        def hkeys(t0, n):
            return [("hT", tt) for tt in range(t0 // 128, (t0 + n + 127) // 128)]

        if "p2a" in phases:
            WSL = 2
            wt = [A.alloc([32, 128], BF16) for _ in range(WSL)]
            qTs = [A.alloc([T], BF16) for _ in range(2)]
            kT = A.alloc([T], BF16)
            gs = A.alloc([T], BF16)
            v_tok = A.alloc([NT, 128], BF16)
            kst = [A.alloc([4, 128], F32) for _ in range(2)]
            kb16 = [A.alloc([4, 128], BF16) for _ in range(2)]
            e_t = A.alloc([512], F32)
            og_t = e_t
            sp_t = [A.alloc([512], BF16) for _ in range(3)]
            racc2 = A.alloc([2, 512], BF16)
            racc = [racc2[:, 0, :], racc2[:, 1, :]]
            ckb = racc2.rearrange("p a b -> p (a b)").rearrange("p (j d) -> p j d", j=8)
            att_t = [A.alloc([512], BF16) for _ in range(2)]
            sq_t = A.alloc([512], BF16)
            om_t = [A.alloc([512], BF16) for _ in range(2)]
            vcb = A.alloc([8, 128], BF16)
            kTc = A.alloc([PAST], BF16)
            sp_s = A.alloc([64], BF16)
            I("pool", "memset", [], ["sp_s"], sp_s, 0.0)
            SCALE = 128.0 ** -0.5
            cnt = {"w": 0, "bank": 0, "st": 0}

            def load_w(col0):
                slot = cnt["w"] % WSL
                cnt["w"] += 1
                for part in range(4):
                    src = w_in[part * 1024:(part + 1) * 1024, col0:col0 + 128].rearrange("(c p) n -> p c n", p=128)
                    DMA("pool", wt[slot][:, part * 8:(part + 1) * 8, :], src, [], [("wt", slot, part)])
                return slot

            def proj_fm(col0, evac):
                slot = load_w(col0)
                for tbi, (t0, n) in enumerate(TBS):
                    bk = cnt["bank"] % 4
                    cnt["bank"] += 1
                    for fc in range(32):
                        I("pe", "matmul", [("wt", slot, fc // 8)] + hkeys(t0, n), [("ps", bk)], PS[bk][:, 0:n],
                          wt[slot][:, fc, :], hT[:, fc, t0:t0 + n], start=(fc == 0), stop=(fc == 31))
                    evac(bk, tbi, t0, n)

            def proj_tm(col0, hd, dst_out, is_k):
                slot = load_w(col0)
                kpend = []
                for j4 in range(5):
                    tiles = list(range(j4 * 4, min(j4 * 4 + 4, NT)))
                    bk = cnt["bank"] % 4
                    cnt["bank"] += 1
                    for j, tt in enumerate(tiles):
                        nt = 128 if tt < 16 else 64
                        for fc in range(32):
                            I("pe", "matmul", [("wt", slot, fc // 8), ("hT", tt)], [("ps", bk)],
                              PS[bk][0:nt, j * 128:(j + 1) * 128], hT[:, fc, tt * 128:tt * 128 + nt], wt[slot][:, fc, :],
                              start=(fc == 0), stop=(fc == 31))
                    while kpend:
                        kpend.pop(0)()
                    s = cnt["st"] % 2
                    cnt["st"] += 1
                    nj = len(tiles)
                    npart = 128 if tiles[0] < 16 else 64
                    copy_on(evac_eng(), kst[s][0:npart, 0:nj, :], PS[bk][0:npart, 0:nj * 128].rearrange("p (j d) -> p j d", j=nj),
                            [("ps", bk)], [("kst", s)])
                    r0 = tiles[0] * 128
                    if npart == 128:
                        dst = dst_out[r0:r0 + nj * 128, hd * 128:(hd + 1) * 128].rearrange("(j p) d -> p j d", p=128)
                        DMA("sp", dst, kst[s][:, 0:nj, :], [("kst", s)], [])
                    else:
                        DMA("sp", dst_out[r0:r0 + 64, hd * 128:(hd + 1) * 128], kst[s][0:64, 0, :], [("kst", s)], [])
                    if is_k:
                        I("pool", "tensor_copy", [("kst", s)], [("kb16", s)], out=kb16[s][0:npart, 0:nj, :], in_=kst[s][0:npart, 0:nj, :])

                        def tr(s=s, tiles=tiles, npart=npart, nj=nj, r0=r0, j4=j4):
                            pv = psb(4 + s).rearrange("p (j n) -> p j n", j=8)
                            for j, tt in enumerate(tiles):
                                I("pe", "transpose", [("kb16", s), "ident_bf"], [("ps", 4 + s)], out=pv[:, j, 0:npart],
                                  in_=kb16[s][0:npart, j, :], identity=ident_bf[0:npart, 0:npart])
                            if npart == 128:
                                copy_on(evac_eng(), kT[:, r0:r0 + nj * 128].rearrange("p (j n) -> p j n", j=nj), pv[:, 0:nj, :],
                                        [("ps", 4 + s)], [("kT", j4)])
                            else:
                                copy_on(evac_eng(), kT[:, r0:r0 + 64], pv[:, 0, 0:64], [("ps", 4 + s)], [("kT", j4)])
                        kpend.append(tr)
                    else:
                        I("pool", "tensor_copy", [("kst", s)], [("v_tok", j4)], out=v_tok[0:npart, tiles[0]:tiles[0] + nj, :],
                          in_=kst[s][0:npart, 0:nj, :])
                while kpend:
                    kpend.pop(0)()

            def attention(hd, qT, qp, bg=None):
                blocks = []
                for qb in range(4):
                    kbs = list(range(4 * qb + 3, -1, -1))
                    for i, kb in enumerate(kbs):
                        m = kb - 4 * qb
                        blocks.append(dict(q0=qb * 512, nq=512, k_ap=kT[:, kb * 128:(kb + 1) * 128], nk=128,
                                           v_ap=v_tok[:, kb, :], mask=(maskM[:, 384 - 128 * m:384 - 128 * m + 512] if m >= 0 else None),
                                           first=(i == 0), last=(i == len(kbs) - 1), qkey=("qT", qp, qb), kkey=("kT", kb // 4),
                                           vkey=("v_tok", kb // 4), ckeys=[], sps=None, obank=6, qbi=qb))
                blocks.append(dict(q0=TP, nq=64, k_ap=kT[:, TP:T], nk=64, v_ap=v_tok[0:64, 16, :], mask=maskM[0:64, 384:448],
                                   first=True, last=False, qkey=("qT", qp, 4), kkey=("kT", 4), vkey=("v_tok", 4), ckeys=[],
                                   sps=sp_s, obank=6, qbi=4))
                for i, kb in enumerate(range(7, -1, -1)):
                    blocks.append(dict(q0=TP, nq=64, k_ap=kTc[:, kb * 128:(kb + 1) * 128], nk=128, v_ap=vcb[:, kb, :], mask=None,
                                       first=False, last=(i == 7), qkey=("qT", qp, 4), kkey="kTc", vkey="vcb", ckeys=[], sps=None,
                                       obank=6, qbi=4))
                n = len(blocks)
                state = {"R": None, "Rkey": None, "ra": 0}

                def Zm(i):
                    b = blocks[i]
                    zb = i % 2
                    nk, nq = b["nk"], b["nq"]
                    I("pe", "matmul", [b["kkey"], b["qkey"]], [("ps", zb)], PS[zb][0:nk, 0:nq], b["k_ap"], qT[:, b["q0"]:b["q0"] + nq],
                      start=True, stop=(b["mask"] is None))
                    if b["mask"] is not None:
                        I("pe", "matmul", ["ident_bf", "maskM"], [("ps", zb)], PS[zb][0:nk, 0:nq], ident_bf[0:nk, 0:nk], b["mask"],
                          start=False, stop=True)
                    I("act", "activation", [("ps", zb)], ["e_t"], out=e_t[0:nk, 0:nq], in_=PS[zb][0:nk, 0:nq], func=AF.Exp)
                    if b["sps"] is not None:
                        sp_ap, spk = b["sps"], "sp_s"
                        b["sp_full"] = sp_ap[:, 0:nq]
                    else:
                        si = i % 3
                        sp_ap, spk = sp_t[si], ("sp", si)
                        b["sp_full"] = sp_ap[:, 0:nq]
                    b["sp"], b["spk"] = sp_ap, spk
                    I("act", "activation", ["e_t"], [spk], out=sp_ap[0:nk, 0:nq], in_=e_t[0:nk, 0:nq], func=AF.Ln, bias=1.0)
                    if b["first"]:
                        b["R"], b["Rk"] = None, None
                    else:
                        pb = blocks[i - 1]
                        if pb["first"]:
                            b["R"], b["Rk"] = pb["sp_full"], pb["spk"]
                        else:
                            ra = state["ra"] % 2
                            state["ra"] += 1
                            I("dve", "tensor_tensor", [pb["Rk"], pb["spk"]], [("racc", ra)], out=racc[ra][:, 0:nq],
                              in0=pb["R"], in1=pb["sp_full"], op=ALU.add)
                            b["R"], b["Rk"] = racc[ra][:, 0:nq], ("racc", ra)

                def Am(i):
                    b = blocks[i]
                    ab = 2 + i % 2
                    nk, nq = b["nk"], b["nq"]
                    I("pe", "matmul", [b["kkey"], b["qkey"]], [("ps", ab)], PS[ab][0:nk, 0:nq], b["k_ap"], qT[:, b["q0"]:b["q0"] + nq],
                      start=True, stop=False)
                    if b["mask"] is not None:
                        I("pe", "matmul", ["ident_bf", "maskM"], [("ps", ab)], PS[ab][0:nk, 0:nq], ident_bf[0:nk, 0:nk], b["mask"],
                          start=False, stop=False)
                    I("pe", "matmul", ["negtri", b["spk"]], [("ps", ab)], PS[ab][0:nk, 0:nq], negtri[0:nk, 0:nk], b["sp"][0:nk, 0:nq],
                      start=False, stop=(b["R"] is None))
                    if b["R"] is not None:
                        I("pe", "matmul", ["negones", b["Rk"]], [("ps", ab)], PS[ab][0:nk, 0:nq], negones[:, 0:nk], b["R"],
                          start=False, stop=True)
                    ai = i % 2
                    I("act", "activation", [("ps", ab)], [("att", ai)], out=att_t[ai][0:nk, 0:nq], in_=PS[ab][0:nk, 0:nq], func=AF.Exp)

                def AVm(i):
                    b = blocks[i]
                    nk, nq = b["nk"], b["nq"]
                    ob = b["obank"]
                    ai = i % 2
                    I("pe", "matmul", [b["vkey"], ("att", ai)], [("ps", ob)], PS[ob][:, 0:nq], b["v_ap"][0:nk, :], att_t[ai][0:nk, 0:nq],
                      start=b["first"], stop=b["last"])
                    if b["last"]:
                        q0, qbi = b["q0"], b["qbi"]
                        I("dve", "tensor_tensor", [("ps", ob), ("gs", qbi)], ["e_t"], out=og_t[:, 0:nq], in0=PS[ob][:, 0:nq],
                          in1=gs[:, q0:q0 + nq], op=ALU.mult)
                        oi = qbi % 2
                        I("dve", "tensor_scalar", ["e_t", "FV"], [("om", oi)], out=om_t[oi][:, 0:nq], in0=og_t[:, 0:nq],
                          scalar1=FV[:, hd, 5:6], scalar2=None, op0=ALU.mult)
                        DMA("sp", omix_d[hd, :, q0:q0 + nq], om_t[oi][:, 0:nq], [("om", oi)], [("omix", hd, qbi)])
                        I("act", "activation", ["e_t"], ["sq_t"], out=sq_t[:, 0:nq], in_=og_t[:, 0:nq], func=AF.Square)
                        def ssq_mm(q0=q0, nq=nq):
                            for j in range((nq + 127) // 128):
                                tt = q0 // 128 + j
                                nt = min(128, nq - j * 128)
                                I("pe", "matmul", ["sq_t", "ones_bf"], [("ps", 7)], PS[7][0:nt, tt:tt + 1], sq_t[:, j * 128:j * 128 + nt],
                                  ones_bf[:, 0:1], start=True, stop=True)
                        pending.append(ssq_mm)

                pending = []
                for s in range(n + 2):
                    if pending and s % 2 == 0:
                        pending.pop(0)()
                    if s < n:
                        Zm(s)
                    if 0 <= s - 1 < n:
                        Am(s - 1)
                    if 0 <= s - 2 < n:
                        AVm(s - 2)
                    if bg is not None:
                        for _ in range(4):
                            next(bg, None)
                if bg is not None:
                    for _ in bg:
                        pass
                while pending:
                    pending.pop(0)()
                I("dve", "tensor_tensor", [("ps", 7), "ssq_sb"], ["ssq_sb"], out=ssq_sb[:, 0:16], in0=PS[7][:, 0:16], in1=ssq_sb[:, 0:16], op=ALU.add)
                I("dve", "tensor_tensor", [("ps", 7), "ssq_sb"], ["ssq_sb"], out=ssq_sb[0:64, 16:17], in0=PS[7][0:64, 16:17],
                  in1=ssq_sb[0:64, 16:17], op=ALU.add)

            nheads = 16 if not dbg else dbg.get("nheads", 16)

            def q_proj_gen(hd):
                qp = hd % 2
                slot = load_w(Q0 + hd * 128)
                for tbi, (t0, n) in enumerate(TBS):
                    bk = 4 + tbi % 2
                    for fc in range(32):
                        I("pe", "matmul", [("wt", slot, fc // 8)] + hkeys(t0, n), [("ps", bk)], PS[bk][:, 0:n],
                          wt[slot][:, fc, :], hT[:, fc, t0:t0 + n], start=(fc == 0), stop=(fc == 31))
                        yield
                    I("dve", "tensor_scalar", [("ps", bk)], [("qT", qp, tbi)], out=qTs[qp][:, t0:t0 + n], in0=PS[bk][:, 0:n],
                      scalar1=SCALE, scalar2=None, op0=ALU.mult)

            for _ in q_proj_gen(0):
                pass
            for hd in range(nheads):
                proj_tm(K0 + hd * 128, hd, k_out, True)
                proj_tm(V0 + hd * 128, hd, v_out, False)
                DMA("pool", ckb, ck[:, hd * 128:(hd + 1) * 128].rearrange("(j p) d -> p j d", p=128), [], [("racc", 0), ("racc", 1)])
                DMA("pool", vcb, cv[:, hd * 128:(hd + 1) * 128].rearrange("(j p) d -> p j d", p=128), [], ["vcb"])
                proj_fm(G0 + hd * 128, lambda bk, tbi, t0, n: I("act", "activation", [("ps", bk)], [("gs", tbi)], out=gs[:, t0:t0 + n],
                                                               in_=PS[bk][:, 0:n], func=AF.Silu))
                pv = psb(7).rearrange("p (j n) -> p j n", j=8)
                for j in range(8):
                    I("pe", "transpose", [("racc", 0), ("racc", 1), "ident_bf"], [("ps", 7)], out=pv[:, j, :], in_=ckb[:, j, :], identity=ident_bf)
                copy_on(evac_eng(), kTc.rearrange("p (j n) -> p j n", j=8), pv, [("ps", 7)], ["kTc"])
                attention(hd, qTs[hd % 2], hd % 2, q_proj_gen(hd + 1) if hd + 1 < nheads else None)
            if dbg:
                I("dve", "tensor_copy", ["ssq_sb"], ["dbgt"], out=og_t[:, 0:NT], in_=ssq_sb)
                DMA("sp", dbg_d[:, 0:NT], og_t[:, 0:NT], ["dbgt"], [])
            S.barrier()
            A.reset(mP)


        if "p2b" in phases:
            A.reset(mP)
            wt = [A.alloc([32, 128], BF16) for _ in range(2)]
            rA = A.alloc([2120], F32)
            wdt = rA[:, 0:512].bitcast(BF16).rearrange("p (c n) -> p c n", c=32)
            rB = A.alloc([2120], F32)
            xact = [A.alloc([T], BF16) for _ in range(2)]
            Bact = A.alloc([T], BF16)
            Cact = A.alloc([T], BF16)
            dt_tok = A.alloc([NT, 32], F32)
            a_b = A.alloc([32], F32)
            dsk_b = A.alloc([32], F32)
            hlast = A.alloc([32, 8], BF16)
            cst = [A.alloc([128], F32) for _ in range(1)]
            dAb4 = A.alloc([4, 128], F32)
            D4 = A.alloc([4, 128], F32)
            Wt4 = A.alloc([4, 128], BF16)
            xB_tok = A.alloc([3, 128], BF16)
            w4 = A.alloc([4], F32)
            xw = A.alloc([4, 64], BF16)
            dAx = A.alloc([2, 128], F32)
            t1 = A.alloc([128], F32)
            Hs = A.alloc([256], F32)
            Hbf = A.alloc([256], BF16)
            hst = [A.alloc([128], F32) for _ in range(1)] * 2
            zs_t = [A.alloc([512], BF16) for _ in range(1)]
            om_b = [A.alloc([512], BF16) for _ in range(1)]
            cntb = {"w": 0, "bank": 0, "z": 0, "c": 0, "h": 0}
            zpend = []

            def load_wb(col0):
                slot = cntb["w"] % 2
                cntb["w"] += 1
                for part in range(4):
                    src = w_in[part * 1024:(part + 1) * 1024, col0:col0 + 128].rearrange("(c p) n -> p c n", p=128)
                    DMA("pool", wt[slot][:, part * 8:(part + 1) * 8, :], src, [], [("wt", slot, part)])
                return slot

            def proj_fmb(col0, evac, extra=None):
                slot = load_wb(col0)
                for tbi, (t0, n) in enumerate(TBS):
                    bk = cntb["bank"] % 3
                    cntb["bank"] += 1
                    for fc in range(32):
                        I("pe", "matmul", [("wt", slot, fc // 8)] + hkeys(t0, n), [("ps", bk)], PS[bk][:, 0:n],
                          wt[slot][:, fc, :], hT[:, fc, t0:t0 + n], start=(fc == 0), stop=(fc == 31))
                    evac(bk, tbi, t0, n)
                if extra is not None:
                    extra(slot)

            DMA("pool", wdt, w_in[:, DT0:DT0 + 32].rearrange("(c p) n -> p c n", p=128), [], ["wdt", "rA"])
            DMA("sp", dsk_b, d_skip.partition_broadcast(128), [], ["dsk_b"])
            I("dve", "tensor_copy", ["dsk_b"], ["dskp"], out=dskp[0:64, :], in_=dsk_b[0:64, :].rearrange("p (j t) -> p j t", t=2)[:, :, 0])
            I("dve", "tensor_copy", ["dsk_b"], ["dskp"], out=dskp[64:128, :], in_=dsk_b[64:128, :].rearrange("p (j t) -> p j t", t=2)[:, :, 1])
            I("act", "activation", ["alog_b"], ["a_e"], out=a_b, in_=alog_b, func=AF.Exp)
            I("dve", "tensor_scalar", ["a_e"], ["a_b"], out=a_b, in0=a_b, scalar1=-1.0, scalar2=None, op0=ALU.mult)
            for tt in range(NT):
                nt = 128 if tt < 16 else 64
                bk, cc = (3, tt * 32) if tt < 16 else (4, 0)
                for fc in range(32):
                    I("pe", "matmul", ["wdt", "rA", ("hT", tt)], [("ps", bk)], PS[bk][0:nt, cc:cc + 32], hT[:, fc, tt * 128:tt * 128 + nt],
                      wdt[:, fc, :], start=(fc == 0), stop=(fc == 31))
            I("dve", "tensor_tensor", [("ps", 3), "dtb_b"], ["dt_tok"], out=dt_tok[:, 0:16, :],
              in0=PS[3][:].rearrange("p (t h) -> p t h", t=16),
              in1=dtb_b.rearrange("p (o h) -> p o h", o=1).broadcast_to([128, 16, 32]), op=ALU.add)
            I("dve", "tensor_tensor", [("ps", 4), "dtb_b"], ["dt_tok"], out=dt_tok[0:64, 16, :], in0=PS[4][0:64, 0:32], in1=dtb_b[0:64, :], op=ALU.add)
            I("act", "activation", ["dt_tok"], ["dt_e"], out=dt_tok[:, 0:16, :], in_=dt_tok[:, 0:16, :], func=AF.Exp)
            I("act", "activation", ["dt_tok", "dt_e"], ["dt_e"], out=dt_tok[0:64, 16, :], in_=dt_tok[0:64, 16, :], func=AF.Exp)
            I("act", "activation", ["dt_e"], ["dt_f"], out=dt_tok[:, 0:16, :], in_=dt_tok[:, 0:16, :], func=AF.Ln, bias=1.0)
            I("act", "activation", ["dt_e", "dt_f"], ["dt_f"], out=dt_tok[0:64, 16, :], in_=dt_tok[0:64, 16, :], func=AF.Ln, bias=1.0)
            I("dve", "tensor_copy", [("hT", 15)], ["hlast"], out=hlast[:, :, 0:3], in_=hT[:, :, TP - 3:TP])
            I("dve", "tensor_copy", [("hT", 16), "hlast"], ["hlast"], out=hlast[:, :, 3:6], in_=hT[:, :, T - 3:T])

            def proj_conv(col0, dst, dkey):
                ch = (col0 - XBC0) // 128
                raw, cvb = rA, rB

                def ev(bk, tbi, t0, n):
                    o0 = 3 + t0 if tbi < 4 else 2054
                    copy_on(evac_eng(), raw[:, o0:o0 + n], PS[bk][:, 0:n], [("ps", bk)], ["rA"])

                def extra(slot):
                    cs = 0
                    cntb["c"] += 1
                    for fc in range(32):
                        I("pe", "matmul", [("wt", slot, fc // 8), "hlast"], [("ps", 4)], PS[4][0:6, 128:256], hlast[:, fc, 0:6], wt[slot][:, fc, :],
                          start=(fc == 0), stop=(fc == 31))
                    copy_on("act", cst[cs][0:6, :], PS[4][0:6, 128:256], [("ps", 4)], [("cst", cs)])
                    DMA("sp", conv_out[0:6, ch * 128:(ch + 1) * 128], cst[cs][0:6, :], [("cst", cs)], [])

                proj_fmb(col0, ev, extra)
                I("dve", "memset", [], ["rA"], raw[:, 0:3], 0.0)
                I("dve", "tensor_copy", ["FV"], ["rA"], out=raw[:, 2051:2054], in_=FV[:, ch, 6:9])
                L = 2115
                I("dve", "tensor_scalar", ["rA", "FV"], ["rB"], out=cvb[:, 0:L], in0=raw[:, 0:L], scalar1=FV[:, ch, 0:1], scalar2=FV[:, ch, 4:5],
                  op0=ALU.mult, op1=ALU.add)
                for j in range(1, 4):
                    I("dve", "scalar_tensor_tensor", ["rA", "rB", "FV"], ["rB"], out=cvb[:, 0:L], in0=raw[:, j:j + L], scalar=FV[:, ch, j:j + 1],
                      in1=cvb[:, 0:L], op0=ALU.mult, op1=ALU.add)
                I("act", "activation", ["rB"], [dkey], out=dst[:, 0:TP], in_=cvb[:, 0:TP], func=AF.Silu)
                I("act", "activation", ["rB", dkey], [dkey], out=dst[:, TP:T], in_=cvb[:, 2051:2115], func=AF.Silu)

            w1f = wt[1].rearrange("p a b -> p (a b)")

            def carve(off, n, dtype):
                if dtype == F32:
                    return w1f[:, off // 2:off // 2 + 2 * n].bitcast(F32)
                return w1f[:, off // 2:off // 2 + n]
            TS2 = [dict(dAb4=dAb4, D4=D4, dAx=dAx, Wt4=Wt4, xB_tok=xB_tok, xw=xw, w4=w4),
                   dict(dAb4=carve(0, 512, F32).rearrange("p (a b) -> p a b", a=4), D4=carve(2048, 512, F32).rearrange("p (a b) -> p a b", a=4),
                        dAx=carve(4096, 256, F32).rearrange("p (a b) -> p a b", a=2), Wt4=carve(5120, 512, BF16).rearrange("p (a b) -> p a b", a=4),
                        xB_tok=carve(6144, 384, BF16).rearrange("p (a b) -> p a b", a=3), xw=carve(6912, 256, BF16).rearrange("p (a b) -> p a b", a=4),
                        w4=carve(7424, 4, F32))]
            dA_g = carve(7456, 68, F32).rearrange("p (c h) -> p c h", c=NT)
            negcum_g = carve(7744, 68, F32).rearrange("p (c h) -> p c h", c=NT)
            etot_g = A.alloc([NT, 4], F32)
            T1KEYS = [("dAb4", 1), ("dAx", 1), ("xB_tok", 1), ("w4", 1), ("xw", 1), "dA_g", "negcum_g"] + \
                     [("D4", 1, hq) for hq in range(4)] + [("Wt4", 1, hq) for hq in range(4)]
            W1KEYS = [("wt", 1, part) for part in range(4)]
            zflat = zs_t[0]
            oflat = om_b[0]
            ecxs = [[zflat[:, 0:256].bitcast(F32), zflat[:, 256:512].bitcast(F32)], [oflat[:, 0:256].bitcast(F32), oflat[:, 256:512].bitcast(F32)]]
            EKEYS = [("ecx", p_, r_) for p_ in range(2) for r_ in range(2)]
            ZKEYS = [("zs", 0), ("omb", 0)]

            def group_pre(g):
                hs = slice(4 * g, 4 * g + 4)
                I("dve", "tensor_tensor", ["dt_f", "a_b"], ["dA_g"], out=dA_g[:, 0:16, :], in0=dt_tok[:, 0:16, hs],
                  in1=a_b[:, hs].rearrange("p (o h) -> p o h", o=1).broadcast_to([128, 16, 4]), op=ALU.mult)
                I("dve", "tensor_tensor", ["dt_f", "a_b", "dA_g"], ["dA_g"], out=dA_g[0:64, 16, :], in0=dt_tok[0:64, 16, hs], in1=a_b[0:64, hs], op=ALU.mult)
                for c in range(NT):
                    nt = 128 if c < 16 else 64
                    I("pe", "matmul", ["dA_g", "tri_f"], [("ps", 3)], PS[3][0:nt, c * 4:c * 4 + 4], tri_f[0:nt, 0:nt], dA_g[0:nt, c, :], start=True, stop=True)
                    I("pe", "matmul", ["dA_g", "ones_f"], [("ps", 3)], PS[3][:, 128 + c * 4:128 + c * 4 + 4], ones_f[0:nt, :], dA_g[0:nt, c, :],
                      start=True, stop=True)
                I("dve", "tensor_scalar", [("ps", 3)], ["negcum_g"], out=negcum_g[:, 0:16, :], in0=PS[3][:, 0:64].rearrange("p (c h) -> p c h", c=16),
                  scalar1=-1.0, scalar2=None, op0=ALU.mult)
                I("dve", "tensor_scalar", [("ps", 3), "negcum_g"], ["negcum_g"], out=negcum_g[0:64, 16, :], in0=PS[3][0:64, 64:68], scalar1=-1.0, scalar2=None,
                  op0=ALU.mult)
                I("act", "activation", [("ps", 3)], ["etot_g"], out=etot_g, in_=PS[3][:, 128:196].rearrange("p (c h) -> p c h", c=NT), func=AF.Exp)

            def geo(c):
                nt = 128 if c < 16 else 64
                par = c % 2
                banks = (5, 4, 7, 3) if par == 0 else (6, 0, 1, 2)
                return nt, c * 128, par, TS2[par], banks

            def u1a(g, c, inter):
                nt, t0, par, tt_, _ = geo(c)
                dA4 = dA_g[0:nt, c, :]
                I("dve", "tensor_copy", ["dA_g"], [("dAb4", par)], out=tt_["dAb4"][0:nt, :, 0:nt],
                  in_=dA4.rearrange("p (h o) -> p h o", o=1).broadcast_to([nt, 4, nt]))
                if inter:
                    I("dve", "tensor_copy", ["dA_g"], [("dAx", par)], out=tt_["dAx"][0:nt, :, :].rearrange("p a (b q) -> p (a b) q", q=64),
                      in_=dA4.rearrange("p (h o) -> p h o", o=1).broadcast_to([nt, 4, 64]))

            def u1b(g, c, inter):
                nt, t0, par, tt_, (pb, b1, yb, tb) = geo(c)
                dAb4, D4, dAx, Wt4, xB_tok, xw, w4 = tt_["dAb4"], tt_["D4"], tt_["dAx"], tt_["Wt4"], tt_["xB_tok"], tt_["xw"], tt_["w4"]
                kdAb, kdAx, kxB, kw4, kxw = ("dAb4", par), ("dAx", par), ("xB_tok", par), ("w4", par), ("xw", par)
                hs = slice(4 * g, 4 * g + 4)
                I("pe", "matmul", ["Bact", "Cact"], [("ps", b1)], PS[b1][0:nt, 0:nt], Bact[:, t0:t0 + nt], Cact[:, t0:t0 + nt], start=True, stop=True)
                for hq in range(4):
                    I("pe", "matmul", [kdAb, "tri_f"], [("ps", pb)], PS[pb][0:nt, hq * 128:hq * 128 + nt], dAb4[0:nt, hq, 0:nt], tri_f[0:nt, 0:nt],
                      start=True, stop=False)
                    I("pe", "matmul", ["ident_bf", "negmask2"], [("ps", pb)], PS[pb][0:nt, hq * 128:hq * 128 + nt], ident_bf[0:nt, 0:nt],
                      negmask2[0:nt, 0:nt], start=False, stop=True)
                pv2 = psb(tb).rearrange("p (j n) -> p j n", j=8)
                for pr in range(2):
                    I("pe", "transpose", [("xact", pr), "ident_bf"], [("ps", tb)], out=pv2[0:nt, pr, :], in_=xact[pr][:, t0:t0 + nt], identity=ident_bf)
                I("pe", "transpose", ["Bact", "ident_bf"], [("ps", tb)], out=pv2[0:nt, 2, :], in_=Bact[:, t0:t0 + nt], identity=ident_bf)
                if inter:
                    for pr in range(2):
                        I("pe", "matmul", [kdAx, "tri_f"], [("ps", b1)], PS[b1][:, 128 + pr * 128:128 + pr * 128 + nt],
                          dAx[0:nt, pr, :], tri_f[0:nt, 0:nt], start=True, stop=True)
                for hq in range(4):
                    I("act", "activation", [("ps", pb), "negcum_g"], [("D4", par, hq)], out=D4[0:nt, hq, 0:nt], in_=PS[pb][0:nt, hq * 128:hq * 128 + nt],
                      func=AF.Exp, bias=negcum_g[0:nt, c, hq:hq + 1], scale=1.0)
                    I("dve", "scalar_tensor_tensor", [("D4", par, hq), "dt_f", ("ps", b1)], [("Wt4", par, hq)], out=Wt4[0:nt, hq, 0:nt],
                      in0=D4[0:nt, hq, 0:nt], scalar=dt_tok[0:nt, c, 4 * g + hq:4 * g + hq + 1], in1=PS[b1][0:nt, 0:nt], op0=ALU.mult, op1=ALU.mult)
                copy_on("act", xB_tok[0:nt, :, :], pv2[0:nt, 0:3, :], [("ps", tb)], [kxB])
                if inter:
                    for pr in range(2):
                        I("act", "activation", [("ps", b1)], [("ecx", par, pr)], out=ecxs[par][pr][:, 0:nt],
                          in_=PS[b1][:, 128 + pr * 128:128 + pr * 128 + nt], func=AF.Exp)
                D4k = [("D4", par, hq) for hq in range(4)]
                I("dve", "tensor_tensor", D4k + ["dt_f"], [kw4], out=w4[0:nt, :], in0=D4[0:nt, :, nt - 1], in1=dt_tok[0:nt, c, hs], op=ALU.mult)
                I("dve", "tensor_tensor", [kxB, kw4], [kxw], out=xw[0:nt, :, :],
                  in0=xB_tok[0:nt, 0:2, :].rearrange("p a (b q) -> p (a b) q", q=64),
                  in1=w4[0:nt, :].rearrange("p (h o) -> p h o", o=1).broadcast_to([nt, 4, 64]), op=ALU.mult)

            def u2(g, c, first, inter):
                nt, t0, par, tt_, (pb, b1, yb, tb) = geo(c)
                Wt4, xB_tok, xw = tt_["Wt4"], tt_["xB_tok"], tt_["xw"]
                kxB, kxw = ("xB_tok", par), ("xw", par)
                yall = [rA, rB]
                ykey = ["rA", "rB"]
                for pr in range(2):
                    for hh in range(2):
                        I("pe", "matmul", [kxB, ("Wt4", par, 2 * pr + hh)], [("ps", yb)], PS[yb][hh * 64:(hh + 1) * 64, pr * 128:pr * 128 + nt],
                          xB_tok[0:nt, pr, hh * 64:(hh + 1) * 64], Wt4[0:nt, 2 * pr + hh, 0:nt], start=True, stop=True)
                    if inter:
                        I("pe", "matmul", ["Hbf", "Cact"], [("ps", yb)], PS[yb][:, 256 + pr * 128:256 + pr * 128 + nt], Hbf[:, pr * 128:(pr + 1) * 128],
                          Cact[:, t0:t0 + nt], start=True, stop=True)
                I("pe", "matmul", [kxB, kxw], [("ps", tb)], PS[tb][:, 256:512], xB_tok[0:nt, 2, :], xw[0:nt, :, :].rearrange("p h q -> p (h q)"),
                  start=True, stop=True)
                if first:
                    I("dve", "tensor_copy", [("ps", tb)], ["Hs"], out=Hs, in_=PS[tb][:, 256:512])
                else:
                    I("dve", "tensor_tensor", ["Hs", "etot_g"], ["Hs"], out=Hs.rearrange("p (h q) -> p h q", q=64), in0=Hs.rearrange("p (h q) -> p h q", q=64),
                      in1=etot_g[:, c, :].rearrange("p (h o) -> p h o", o=1).broadcast_to([128, 4, 64]), op=ALU.mult)
                    I("dve", "tensor_tensor", ["Hs", ("ps", tb)], ["Hs"], out=Hs, in0=PS[tb][:, 256:512], in1=Hs, op=ALU.add)
                for pr in range(2):
                    if inter:
                        I("dve", "tensor_tensor", [("ps", yb), ("ecx", par, pr)], ["t1"], out=t1[:, 0:nt], in0=PS[yb][:, 256 + pr * 128:256 + pr * 128 + nt],
                          in1=ecxs[par][pr][:, 0:nt], op=ALU.mult)
                        I("dve", "tensor_tensor", [("ps", yb), "t1"], [ykey[pr]], out=yall[pr][:, t0:t0 + nt], in0=PS[yb][:, pr * 128:pr * 128 + nt],
                          in1=t1[:, 0:nt], op=ALU.add)
                    else:
                        I("dve", "tensor_copy", [("ps", yb)], [ykey[pr]], out=yall[pr][:, t0:t0 + nt], in_=PS[yb][:, pr * 128:pr * 128 + nt])

            def hbf_update():
                I("act", "activation", ["Hs"], ["Hbf"], out=Hbf, in_=Hs, func=AF.Copy)

            def state_out(g, row0):
                for pr in range(2):
                    hsx = cntb["h"] % 2
                    cntb["h"] += 1
                    I("pe", "transpose", ["Hs", "ident_f"], [("ps", 3)], out=PS[3][:, 0:128], in_=Hs[:, pr * 128:(pr + 1) * 128], identity=ident_f)
                    copy_on("act", hst[hsx], PS[3][:, 0:128], [("ps", 3)], ["hst"])
                    DMA("sp", ssd_out[row0 + g * 256 + pr * 128:row0 + g * 256 + (pr + 1) * 128, :], hst[hsx], ["hst"], [])

            def state_in(g):
                for pr in range(2):
                    hsx = cntb["h"] % 2
                    cntb["h"] += 1
                    DMA("sp", hst[hsx], sssd[g * 256 + pr * 128:g * 256 + (pr + 1) * 128, :], [], ["hst"])
                    I("pe", "transpose", ["hst", "ident_f"], [("ps", 3)], out=PS[3][:, 0:128], in_=hst[hsx], identity=ident_f)
                    I("dve", "tensor_copy", [("ps", 3)], ["Hs"], out=Hs[:, pr * 128:(pr + 1) * 128], in_=PS[3][:, 0:128])
                I("act", "activation", ["Hs"], ["Hbf"], out=Hbf, in_=Hs, func=AF.Copy)

            ngroups = 8 if not dbg else dbg.get("ngroups", 8)
            for g in range(ngroups):
                proj_conv(XBC0 + g * 256, xact[0], ("xact", 0))
                proj_conv(XBC0 + g * 256 + 128, xact[1], ("xact", 1))
                proj_conv(XBC0 + 2048 + g * 128, Bact, "Bact")
                proj_conv(XBC0 + 3072 + g * 128, Cact, "Cact")
                S.fence(W1KEYS, T1KEYS)
                S.fence(ZKEYS, EKEYS)
                group_pre(g)
                u1a(g, 0, False)
                for st_ in range(18):
                    if st_ + 1 < 17:
                        u1a(g, st_ + 1, True)
                    if st_ < 17:
                        u1b(g, st_, inter=(st_ > 0))
                    if st_ >= 1:
                        c = st_ - 1
                        if c == 16:
                            state_out(g, 0)
                            state_in(g)
                        u2(g, c, first=(c == 0), inter=(c > 0))
                        hbf_update()
                state_out(g, 2048)
                S.fence(T1KEYS, W1KEYS)
                S.fence(EKEYS, ZKEYS)
                yall = [rA, rB]
                ykey = ["rA", "rB"]
                for pr in range(2):
                    fcx = 16 + 2 * g + pr
                    I("dve", "scalar_tensor_tensor", [("xact", pr), "dskp", ykey[pr]], [ykey[pr]], out=yall[pr][:, 0:T], in0=xact[pr][:, 0:T],
                      scalar=dskp[:, 2 * g + pr:2 * g + pr + 1], in1=yall[pr][:, 0:T], op0=ALU.mult, op1=ALU.add)

                    def evz(bk, tbi, t0, n, pr=pr, fcx=fcx):
                        zi = 0
                        while zpend:
                            zpend.pop(0)()
                        cntb["z"] += 1
                        I("act", "activation", [("ps", bk)], [("zs", zi)], out=zs_t[zi][:, 0:n], in_=PS[bk][:, 0:n], func=AF.Silu)
                        I("dve", "tensor_tensor", [ykey[pr], ("zs", zi)], [ykey[pr]], out=yall[pr][:, t0:t0 + n], in0=yall[pr][:, t0:t0 + n],
                          in1=zs_t[zi][:, 0:n], op=ALU.mult)
                        I("dve", "tensor_scalar", [ykey[pr], "FV"], [("omb", zi)], out=om_b[zi][:, 0:n], in0=yall[pr][:, t0:t0 + n],
                          scalar1=FV[:, fcx, 5:6], scalar2=None, op0=ALU.mult)
                        DMA("sp", omix_d[fcx, :, t0:t0 + n], om_b[zi][:, 0:n], [("omb", zi)], [])
                        I("act", "activation", [ykey[pr]], [("zs", zi)], out=zs_t[zi][:, 0:n], in_=yall[pr][:, t0:t0 + n], func=AF.Square)
                        def ssq_mm(t0=t0, n=n, zi=zi):
                            for j in range((n + 127) // 128):
                                tt = t0 // 128 + j
                                nt = min(128, n - j * 128)
                                I("pe", "matmul", [("zs", zi), "ones_bf"], [("ps", 5)], PS[5][0:nt, tt:tt + 1], zs_t[zi][:, j * 128:j * 128 + nt],
                                  ones_bf[:, 0:1], start=True, stop=True)
                        zpend.append(ssq_mm)

                    proj_fmb(Z0 + g * 256 + pr * 128, evz)
                    while zpend:
                        zpend.pop(0)()
                    I("dve", "tensor_tensor", [("ps", 5), "ssq_ssd"], ["ssq_ssd"], out=ssq_ssd[:, 0:16], in0=PS[5][:, 0:16], in1=ssq_ssd[:, 0:16], op=ALU.add)
                    I("dve", "tensor_tensor", [("ps", 5), "ssq_ssd"], ["ssq_ssd"], out=ssq_ssd[0:64, 16:17], in0=PS[5][0:64, 16:17],
                      in1=ssq_ssd[0:64, 16:17], op=ALU.add)
            if dbg:
                I("dve", "tensor_copy", ["ssq_ssd"], ["dbgt"], out=t1[:, 0:NT], in_=ssq_ssd)
                DMA("sp", dbg_d[:, 32:32 + NT], t1[:, 0:NT], ["dbgt"], [])
            S.barrier()
            A.reset(mP)

        if "p3" in phases:
            A.reset(mC)
            if p3only:
                DMA("sp", ssq_sb, ssq_in[:, 0:NT], [], ["ssq_sb"])
                DMA("sp", ssq_ssd, ssq_in[:, NT:2 * NT], [], ["ssq_ssd"])
            GROUPS = [list(range(0, 6)), list(range(6, 12)), list(range(12, 17))]
            rs = A.alloc([4, NT], F32)
            for (src, key, a, b) in ((ssq_sb, "ssq_sb", 0, 1), (ssq_ssd, "ssq_ssd", 2, 3)):
                I("dve", "tensor_scalar", [key], [("rs", a)], out=rs[:, a, :], in0=src, scalar1=1.0 / 2048, scalar2=EPS,
                  op0=ALU.mult, op1=ALU.add)
                I("act", "activation", [("rs", a)], [("rsl", a)], out=rs[:, a, :], in_=rs[:, a, :], func=AF.Ln)
                I("act", "activation", [("rsl", a)], [("rs", b)], out=rs[:, b, :], in_=rs[:, a, :], func=AF.Exp, scale=-0.5)
            m3 = A.mark()

            def stage3(first):
                A.reset(m3)
                src_d = omix_d if first else x1T_d
                W = w_out if first else w_gate
                nrm = norm_post if first else ple_norm
                res_src = x if first else x1_d
                res_dst = x1_d if first else y
                oT = A.alloc([32, 768], BF16)
                msb = A.alloc([6, D], F32)
                NW3 = 4 if first else 3
                w3 = [A.alloc([4, 512], BF16) for _ in range(NW3)]
                nrmr = [A.alloc([512], F32) for _ in range(2)]
                xr = [A.alloc([1024], F32) for _ in range(2)]
                junk = A.alloc([512], F32) if first else None
                stt = A.alloc([6, 12], F32)
                rst = A.alloc([6], F32)
                if first:
                    rB = [A.alloc([768], F32) for _ in range(2)]
                    dg = A.alloc([128], F32)
                    x1b = [A.alloc([1024], BF16) for _ in range(4)]
                    x1Ts = [A.alloc([8, 128], BF16) for _ in range(2)]
                else:
                    pT = A.alloc([2, 768], BF16)
                    pst = A.alloc([256], F32)
                    pbf = A.alloc([256], BF16)
                    wp = [A.alloc([2, 512], BF16) for _ in range(2)]
                    sg = [A.alloc([512], F32) for _ in range(2)]
                    tmp = [A.alloc([512], F32) for _ in range(1)] * 2
                    e_sb = A.alloc([6, 512], F32)
                c3 = {"w": 0, "n": 0, "x": 0, "e": 0, "t": 0}

                def geom(tiles):
                    nts = [128 if t < 16 else 64 for t in tiles]
                    return tiles[0] * 128, nts, sum(nts)

                def load_group(tiles):
                    t0, nts, ng = geom(tiles)
                    S.tag = "load_group"
                    for c8 in range(4):
                        DMA("sp", oT[:, c8 * 8:(c8 + 1) * 8, 0:ng],
                            src_d[c8 * 8:(c8 + 1) * 8, :, t0:t0 + ng].rearrange("c p t -> p c t"), [], [("oT", c8)])
                    if first:
                        for which, col in ((0, 1), (1, 3)):
                            for j, tt in enumerate(tiles):
                                nt = nts[j]
                                bank = 6 + (j * 128) // 512
                                cc = (j * 128) % 512
                                I("dve", "tensor_scalar", ["ident_f", ("rs", col)], ["dg"], out=dg[0:nt, 0:nt], in0=ident_f[0:nt, 0:nt],
                                  scalar1=rs[0:nt, col, tt:tt + 1], scalar2=None, op0=ALU.mult)
                                I("pe", "matmul", ["dg", "ones_f"], [("ps", bank)], PS[bank][:, cc:cc + nt], ones_f[0:nt, :], dg[0:nt, 0:nt],
                                  start=True, stop=True)
                            n6 = min(ng, 512)
                            copy_on("act", rB[which][:, 0:n6], PS[6][:, 0:n6], [("ps", 6)], [("rB", which)])
                            if ng > 512:
                                copy_on("act", rB[which][:, 512:ng], PS[7][:, 0:ng - 512], [("ps", 7)], [("rB", which)])
                        for c8 in range(4):
                            which = 0 if c8 < 2 else 1
                            bc = rB[which][:, 0:ng].rearrange("p (o t) -> p o t", o=1).broadcast_to([128, 8, ng])
                            I("dve", "tensor_tensor", [("oT", c8), ("rB", which)], [("oT", c8)], out=oT[:, c8 * 8:(c8 + 1) * 8, 0:ng],
                              in0=oT[:, c8 * 8:(c8 + 1) * 8, 0:ng], in1=bc, op=ALU.mult)
                    else:
                        for j, tt in enumerate(tiles):
                            nt = nts[j]
                            r0 = tt * 128
                            DMA("sp", pst[0:nt, :], p_in[r0:r0 + nt, :], [], ["pst"])
                            I("dve", "tensor_copy", ["pst"], ["pbf"], out=pbf[0:nt, :], in_=pst[0:nt, :])
                            bk = 6 + j % 2
                            pvv = psb(bk).rearrange("p (j n) -> p j n", j=8)
                            for c in range(2):
                                I("pe", "transpose", ["pbf", "ident_bf"], [("ps", bk)], out=pvv[:, c, 0:nt],
                                  in_=pbf[0:nt, c * 128:(c + 1) * 128], identity=ident_bf[0:nt, 0:nt])
                            copy_on(evac_eng(), pT[:, :, j * 128:j * 128 + nt], pvv[:, 0:2, 0:nt], [("ps", bk)], [("pT", j)])

                def e_matmuls(tiles, ns):
                    t0, nts, ng = geom(tiles)
                    S.tag = "e_mm"
                    for j, tt in enumerate(tiles):
                        nt = nts[j]
                        eb = 6 + c3["e"] % 2
                        c3["e"] += 1
                        for c in range(2):
                            I("pe", "matmul", [("pT", j), ("wp", ns)], [("ps", eb)], PS[eb][0:nt, :], pT[:, c, j * 128:j * 128 + nt],
                              wp[ns][:, c, :], start=(c == 0), stop=(c == 1))
                        copy_on("act", e_sb[0:nt, j, :], PS[eb][0:nt, :], [("ps", eb)], [("e_sb", j)])

                def evac(tiles, cb, ns):
                    t0, nts, ng = geom(tiles)
                    if first:
                        for j, tt in enumerate(tiles):
                            copy_on("act" if j % 2 == 0 else "dve", msb[0:nts[j], j, cb * 512:(cb + 1) * 512], PS[j][0:nts[j], :],
                                    [("ps", j)], [("msb", j, cb)])
                    for j, tt in enumerate(tiles):
                        nt = nts[j]
                        cs = slice(cb * 512, (cb + 1) * 512)
                        if first:
                            I("dve", "scalar_tensor_tensor", [("msb", j, cb)], ["junk", ("stt", j, cb)], out=junk[0:nt, :], in0=msb[0:nt, j, cs],
                              scalar=1.0, in1=msb[0:nt, j, cs], op0=ALU.mult, op1=ALU.mult, accum_out=stt[0:nt, j, cb:cb + 1])
                            I("dve", "tensor_tensor", [("msb", j, cb), ("nrm", ns)], [("msb", j, cb)], out=msb[0:nt, j, cs],
                              in0=msb[0:nt, j, cs], in1=nrmr[ns][0:nt, :], op=ALU.mult)
                        else:
                            es = c3["t"] % 2
                            c3["t"] += 1
                            I("act", "activation", [("ps", j)], [("sg", es)], out=sg[es][0:nt, :], in_=PS[j][0:nt, :], func=AF.Sigmoid)
                            I("dve", "tensor_tensor", [("e_sb", j), ("sg", es)], ["tmp"], out=tmp[es][0:nt, :], in0=e_sb[0:nt, j, :],
                              in1=sg[es][0:nt, :], op=ALU.mult)
                            I("dve", "scalar_tensor_tensor", ["tmp"], [("sg", es), ("stt", j, cb)], out=sg[es][0:nt, :], in0=tmp[es][0:nt, :],
                              scalar=1.0, in1=tmp[es][0:nt, :], op0=ALU.mult, op1=ALU.mult, accum_out=stt[0:nt, j, cb:cb + 1])
                            I("dve", "tensor_tensor", ["tmp", ("nrm", ns)], [("msb", j, cb)], out=msb[0:nt, j, cs],
                              in0=tmp[es][0:nt, :], in1=nrmr[ns][0:nt, :], op=ALU.mult)

                def stats(tiles):
                    t0, nts, ng = geom(tiles)
                    for j, tt in enumerate(tiles):
                        nt = nts[j]
                        sk = [("stt", j, cb) for cb in range(8)]
                        I("dve", "tensor_reduce", sk, [("stt8", j)], out=stt[0:nt, j, 8:9], in_=stt[0:nt, j, 0:8], axis=mybir.AxisListType.X,
                          op=ALU.add)
                        I("dve", "tensor_scalar", [("stt8", j)], [("stt9", j)], out=stt[0:nt, j, 9:10], in0=stt[0:nt, j, 8:9],
                          scalar1=1.0 / D, scalar2=EPS, op0=ALU.mult, op1=ALU.add)
                        I("act", "activation", [("stt9", j)], [("stt10", j)], out=stt[0:nt, j, 10:11], in_=stt[0:nt, j, 9:10], func=AF.Ln)
                        I("act", "activation", [("stt10", j)], [("rst", j)], out=rst[0:nt, j:j + 1], in_=stt[0:nt, j, 10:11],
                          func=AF.Exp, scale=-0.5)

                def epiA(tiles, pc):
                    t0, nts, ng = geom(tiles)
                    n = len(tiles)
                    base = c3["x"]
                    c3["x"] += n
                    pcs = slice(pc * 1024, (pc + 1) * 1024)

                    def load(j):
                        xs = (base + j) % 2
                        r0 = tiles[j] * 128
                        DMA("sp", xr[xs][0:nts[j], :], res_src[r0:r0 + nts[j], pcs], [], [("xr", xs)])

                    load(0)
                    if n > 1:
                        load(1)
                    for j, tt in enumerate(tiles):
                        nt = nts[j]
                        r0 = tt * 128
                        xs = (base + j) % 2
                        mk = [("msb", j, 2 * pc), ("msb", j, 2 * pc + 1)]
                        I("dve", "scalar_tensor_tensor", mk + [("rst", j), ("xr", xs)], mk, out=msb[0:nt, j, pcs], in0=msb[0:nt, j, pcs],
                          scalar=rst[0:nt, j:j + 1], in1=xr[xs][0:nt, :], op0=ALU.mult, op1=ALU.add)
                        DMA("sp", res_dst[r0:r0 + nt, pcs], msb[0:nt, j, pcs], mk, [])
                        if j + 2 < n:
                            load(j + 2)

                def epiB(tiles, pc):
                    if not first:
                        return
                    S.tag = "epiB"
                    t0, nts, ng = geom(tiles)
                    for j, tt in enumerate(tiles):
                        nt = nts[j]
                        r0 = tt * 128
                        xs = c3["t"] % 4
                        c3["t"] += 1
                        pcs = slice(pc * 1024, (pc + 1) * 1024)
                        mk = [("msb", j, 2 * pc), ("msb", j, 2 * pc + 1)]
                        copy_on("act", x1b[xs][0:nt, :], msb[0:nt, j, pcs], mk, [("x1b", xs)])
                        bk = 6 + xs % 2
                        pvv = psb(bk).rearrange("p (j n) -> p j n", j=8)
                        for c in range(8):
                            I("pe", "transpose", [("x1b", xs), "ident_bf"], [("ps", bk)], out=pvv[:, c, 0:nt],
                              in_=x1b[xs][0:nt, c * 128:(c + 1) * 128], identity=ident_bf[0:nt, 0:nt])
                        copy_on("act", x1Ts[xs % 2][:, :, 0:nt], pvv[:, :, 0:nt], [("ps", bk)], [("x1Ts", xs % 2)])
                        DMA("sp", x1T_d[pc * 8:(pc + 1) * 8, :, r0:r0 + nt].rearrange("c p t -> p c t"), x1Ts[xs % 2][:, :, 0:nt],
                            [("x1Ts", xs % 2)], [])

                prev = None
                load_group(GROUPS[0])
                for gi, tiles in enumerate(GROUPS):
                    t0, nts, ng = geom(tiles)
                    for cb in range(8):
                        ns = c3["n"] % 2
                        c3["n"] += 1
                        DMA("sp", nrmr[ns], nrm[cb * 512:(cb + 1) * 512].partition_broadcast(128), [], [("nrm", ns)])
                        if not first:
                            DMA("pool", wp[ns], w_proj[:, cb * 512:(cb + 1) * 512].rearrange("(c p) n -> p c n", p=128), [], [("wp", ns)])
                        if prev is not None:
                            if cb % 2 == 0:
                                epiB(prev, cb // 2)
                            elif cb // 2 + 1 < 4:
                                epiA(prev, cb // 2 + 1)
                        for fq in range(8):
                            slot = c3["w"] % NW3
                            c3["w"] += 1
                            DMA("pool", w3[slot], W[fq * 512:(fq + 1) * 512, cb * 512:(cb + 1) * 512].rearrange("(c p) n -> p c n", p=128),
                                [], [("w3", slot)])
                            S.tag = "mm.fq%d.%s" % (fq, "a" if first else "b")
                            order = [(f4, j) for f4 in range(4) for j in range(len(tiles))]
                            if fq == 0:
                                order = [(f4, j) for j in range(len(tiles)) for f4 in range(4)]
                            for f4, j in order:
                                fc = fq * 4 + f4
                                nt = nts[j]
                                I("pe", "matmul", [("w3", slot), ("oT", fc // 8)], [("ps", j)], PS[j][0:nt, :],
                                  oT[:, fc, j * 128:j * 128 + nt], w3[slot][:, f4, :], start=(fc == 0), stop=(fc == 31))
                            if (not first) and fq == 5:
                                e_matmuls(tiles, ns)
                        evac(tiles, cb, ns)
                    if gi + 1 < len(GROUPS):
                        load_group(GROUPS[gi + 1])
                    stats(tiles)
                    epiA(tiles, 0)
                    prev = tiles
                for pc in range(4):
                    epiB(prev, pc)
                    if pc + 1 < 4:
                        epiA(prev, pc + 1)
                S.barrier()

            stage3(True)
            stage3(False)

        S.barrier()
        nsig = S.emit()
    nc._pe_tags = S.pe_tags
    return nc, nsig


_CACHE = {}


def _shard_inputs(inputs, c):
    f = np.float32
    g = lambda k: np.asarray(inputs[k])
    m = {
        "x": np.concatenate([g("x_prompt")[c], g("x_sample")[c]], axis=0).astype(f, copy=False),
        "p": np.concatenate([g("p_prompt")[0, c], g("p_sample")[0, c]], axis=0).astype(f, copy=False),
        "ck": np.ascontiguousarray(g("cache_k")[0, c].reshape(PAST, 2048)),
        "cv": np.ascontiguousarray(g("cache_v")[0, c].reshape(PAST, 2048)),
        "sconv": np.ascontiguousarray(g("state_conv")[0, c]),
        "sssd": np.ascontiguousarray(g("state_ssd")[0, c].reshape(2048, 128)),
        "w_in": np.ascontiguousarray(g("w_in")[0]),
        "w_out": np.ascontiguousarray(g("w_out")[0]),
        "w_gate": np.ascontiguousarray(g("w_ple_gate")[0]),
        "w_proj": np.ascontiguousarray(g("w_ple_proj")[0]),
    }
    for k in ("norm_pre", "norm_post", "ple_norm", "conv_b", "sb_norm", "ssd_norm", "dt_bias", "a_log", "d_skip"):
        m[k] = np.ascontiguousarray(g(k)[0].reshape(-1))
    m["conv_w"] = np.ascontiguousarray(g("conv_w")[0])
    return m


def kernel(**inputs):
    if "nc" not in _CACHE:
        _CACHE["nc"] = build_program()[0]
    nc = _CACHE["nc"]
    in_maps = [_shard_inputs(inputs, c) for c in range(N_CORES)]
    res = run_bass_kernel_spmd(nc, in_maps, core_ids=list(range(N_CORES)))
    R = res.results
    f = np.float32
    y = np.stack([r["y"] for r in R])
    ko = np.stack([r["k_out"] for r in R])
    vo = np.stack([r["v_out"] for r in R])
    co = np.stack([r["conv_out"] for r in R])
    so = np.stack([r["ssd_out"] for r in R])
    y_prompt = np.ascontiguousarray(y[:, :TP]).astype(f, copy=False)
    y_sample = np.ascontiguousarray(y[:, TP:]).astype(f, copy=False)
    k_prompt = np.ascontiguousarray(ko[:, :TP]).reshape(1, N_CORES, TP, 16, 128)
    v_prompt = np.ascontiguousarray(vo[:, :TP]).reshape(1, N_CORES, TP, 16, 128)
    k_sample = np.ascontiguousarray(ko[:, TP:]).reshape(1, N_CORES, TS, 16, 128)
    v_sample = np.ascontiguousarray(vo[:, TP:]).reshape(1, N_CORES, TS, 16, 128)
    conv_prompt = np.ascontiguousarray(co[:, 0:3]).reshape(1, N_CORES, 3, D)
    conv_sample = np.ascontiguousarray(co[:, 3:6]).reshape(1, N_CORES, 3, D)
    ssd_prompt = np.ascontiguousarray(so[:, 0:2048]).reshape(1, N_CORES, 32, 64, 128)
    ssd_sample = np.ascontiguousarray(so[:, 2048:4096]).reshape(1, N_CORES, 32, 64, 128)
    return (y_prompt, y_sample, k_prompt, v_prompt, conv_prompt, ssd_prompt, k_sample, v_sample, conv_sample, ssd_sample)
```

```python
import contextlib
import numpy as np
import concourse.bass as bass
import concourse.mybir as mybir
from concourse.bass_utils import run_bass_kernel_spmd

F32 = mybir.dt.float32
BF16 = mybir.dt.bfloat16
AF = mybir.ActivationFunctionType
ALU = mybir.AluOpType

N_CORES = 8
TP, TS = 2048, 64
T = TP + TS
NT = 17
D = 4096
NIN = 14368
PAST = 1024
EPS = 1e-6
Q0, K0, V0, G0, XBC0, Z0, DT0 = 0, 2048, 4096, 6144, 8192, 12288, 14336
TBS = [(0, 512), (512, 512), (1024, 512), (1536, 512), (2048, 64)]
NEG = -30000.0

ENGINES = ["pe", "act", "dve", "pool", "sp"]
SEM_CHUNK = 30000


class _Op:
    __slots__ = ("eng", "seq", "fn", "waits", "slot", "use", "sig")

    def __init__(self):
        self.sig = None


class Sched:
    def __init__(self, nc, slots_per_queue=8):
        self.nc = nc
        self.streams = {e: [] for e in ENGINES}
        self.nseq = {e: 0 for e in ENGINES}
        self.known = {e: {} for e in ENGINES}
        self.lastw = {}
        self.readers = {}
        self.clocks = {}
        self.waited = set()
        self.nslots = slots_per_queue
        self.slot_rr = {q: 0 for q in ("sp", "act", "pool")}
        self.slot_use = {}
        self.opmap = {}
        self.tag = ""
        self.pe_tags = []

    def _deps(self, eng, reads, writes):
        deps = set()
        for k in reads:
            ev = self.lastw.get(k)
            if ev is not None:
                deps.add(ev)
            if isinstance(k, tuple) and k[0] == "ps":
                for src, seq in self.readers.get(k, {}).items():
                    if src != eng:
                        deps.add((src, seq))
        for k in writes:
            ev = self.lastw.get(k)
            if ev is not None:
                deps.add(ev)
            for src, seq in self.readers.get(k, {}).items():
                deps.add((src, seq))
        if eng == "pe":
            deps = {d for d in deps if d[0] != "pe"}
        return deps

    def _wait_list(self, eng, deps):
        kn = self.known[eng]
        waits = []
        for ev in sorted(deps, key=lambda d: (str(d[0]), d[1])):
            src, seq = ev
            if kn.get(src, 0) >= seq:
                continue
            waits.append(ev)
            kn[src] = seq
            for s2, v2 in self.clocks.get(ev, {}).items():
                if kn.get(s2, 0) < v2:
                    kn[s2] = v2
            if not isinstance(src, tuple):
                self.waited.add(ev)
        return waits

    def _mark(self, ev, reads, writes):
        src, seq = ev
        for k in reads:
            self.readers.setdefault(k, {})[src] = seq
        for k in writes:
            self.lastw[k] = ev
            self.readers[k] = {}

    def op(self, eng, fn, reads=(), writes=()):
        deps = self._deps(eng, reads, writes)
        o = _Op()
        o.eng = eng
        o.waits = self._wait_list(eng, deps)
        self.nseq[eng] += 1
        o.seq = self.nseq[eng]
        o.fn = fn
        o.slot = None
        if eng == "pe":
            self.pe_tags.append(self.tag)
        ev = (eng, o.seq)
        self.clocks[ev] = dict(self.known[eng])
        self.streams[eng].append(o)
        self.opmap[ev] = o
        self._mark(ev, reads, writes)
        return ev

    def dma(self, q, fn, reads=(), writes=()):
        i = self.slot_rr[q]
        self.slot_rr[q] = (i + 1) % self.nslots
        slot = (q, i)
        use = self.slot_use.get(slot, 0)
        deps = self._deps(slot, reads, writes)
        if use > 0:
            deps.add((slot, use))
        o = _Op()
        o.eng = q
        o.waits = self._wait_list(q, deps)
        o.seq = None
        o.fn = fn
        o.slot = slot
        o.use = use + 1
        self.slot_use[slot] = use + 1
        ev = (slot, use + 1)
        self.clocks[ev] = dict(self.known[q])
        self.streams[q].append(o)
        self._mark(ev, reads, writes)
        return ev

    def fence(self, src_keys, dst_keys):
        evs = {}
        for k in src_keys:
            ev = self.lastw.get(k)
            if ev is not None:
                evs[ev[0]] = max(evs.get(ev[0], 0), ev[1])
            for s_, q_ in self.readers.get(k, {}).items():
                evs[s_] = max(evs.get(s_, 0), q_)
        for k in dst_keys:
            r = self.readers.setdefault(k, {})
            for s_, q_ in evs.items():
                r[s_] = max(r.get(s_, 0), q_)

    def barrier(self):
        deps = {(slot, use) for slot, use in self.slot_use.items()}
        for e in ENGINES:
            if self.nseq[e] > 0:
                deps.add((e, self.nseq[e]))
        for e in ENGINES:
            o = _Op()
            o.eng = e
            o.waits = self._wait_list(e, {d for d in deps if d[0] != e})
            o.seq = None
            o.fn = None
            o.slot = None
            self.streams[e].append(o)
        self.lastw = {}
        self.readers = {}

    def emit(self):
        nc = self.nc
        nsig = {}
        for e in ENGINES:
            c = 0
            for o in self.streams[e]:
                if o.seq is not None and (e, o.seq) in self.waited:
                    c += 1
                    o.sig = c
            nsig[e] = c
        with contextlib.ExitStack() as st:
            csem = {}
            for e in ENGINES:
                n = (nsig[e] + SEM_CHUNK - 1) // SEM_CHUNK
                csem[e] = [st.enter_context(nc.semaphore(f"c_{e}_{j}")) for j in range(n)]
            dsem = {}
            for slot in self.slot_use:
                dsem[slot] = st.enter_context(nc.semaphore(f"d_{slot[0]}_{slot[1]}"))

            def resolve(ev):
                src, seq = ev
                if isinstance(src, tuple):
                    return dsem[src], 16 * seq
                r = self.opmap[ev].sig
                return csem[src][(r - 1) // SEM_CHUNK], (r - 1) % SEM_CHUNK + 1

            def run(e, h):
                for o in self.streams[e]:
                    for ev in o.waits:
                        s, v = resolve(ev)
                        h.wait_ge(s, v)
                    if o.fn is None:
                        continue
                    ins = o.fn(h)
                    if o.slot is not None:
                        ins.then_inc(dsem[o.slot], 16)
                    elif o.sig is not None:
                        ins.then_inc(csem[e][(o.sig - 1) // SEM_CHUNK], 1)

            with nc.Block() as block:
                @block.tensor
                def _(h):
                    run("pe", h)

                @block.scalar
                def _(h):
                    run("act", h)

                @block.vector
                def _(h):
                    run("dve", h)

                @block.gpsimd
                def _(h):
                    run("pool", h)

                @block.sync
                def _(h):
                    run("sp", h)
        return nsig


class Arena:
    def __init__(self, tile_f32, nwords):
        self.t = tile_f32
        self.n = nwords * 4
        self.off = 0

    def mark(self):
        return self.off

    def reset(self, m):
        self.off = m

    def alloc(self, shape_free, dtype):
        esz = 4 if dtype == F32 else 2
        n = 1
        for s in shape_free:
            n *= s
        nbytes = (n * esz + 31) // 32 * 32
        assert self.off + nbytes <= self.n, f"arena overflow: {self.off}+{nbytes} > {self.n}"
        w0 = self.off // 4
        v = self.t[:, w0:w0 + nbytes // 4]
        self.off += nbytes
        if dtype != F32:
            v = v.bitcast(dtype)
        v = v[:, 0:n]
        if len(shape_free) == 2:
            v = v.rearrange("p (a b) -> p a b", a=shape_free[0])
        elif len(shape_free) == 3:
            v = v.rearrange("p (a b c) -> p a b c", a=shape_free[0], b=shape_free[1])
        return v


def build_program(phases=("p1", "p2a", "p2b", "p3"), dbg=False):
    nc = bass.Bass("TRN2", target_bir_lowering=False)

    def din(name, shape):
        return nc.dram_tensor(name, shape, F32, kind="ExternalInput").ap()

    def dout(name, shape):
        return nc.dram_tensor(name, shape, F32, kind="ExternalOutput").ap()

    x = din("x", [T, D])
    p_in = din("p", [T, 256])
    ck = din("ck", [PAST, 2048])
    cv = din("cv", [PAST, 2048])
    sconv = din("sconv", [3, D])
    sssd = din("sssd", [2048, 128])
    w_in = din("w_in", [D, NIN])
    w_out = din("w_out", [D, D])
    w_gate = din("w_gate", [D, D])
    w_proj = din("w_proj", [256, D])
    norm_pre = din("norm_pre", [D])
    norm_post = din("norm_post", [D])
    ple_norm = din("ple_norm", [D])
    conv_w = din("conv_w", [4, D])
    conv_b = din("conv_b", [D])
    sb_norm = din("sb_norm", [2048])
    ssd_norm = din("ssd_norm", [2048])
    dt_bias = din("dt_bias", [32])
    a_log = din("a_log", [32])
    d_skip = din("d_skip", [32])

    y = dout("y", [T, D])
    k_out = dout("k_out", [T, 2048])
    v_out = dout("v_out", [T, 2048])
    conv_out = dout("conv_out", [6, D])
    ssd_out = dout("ssd_out", [4096, 128])

    p3only = bool(dbg) and dbg.get("p3only", False)
    omix_d = nc.dram_tensor("omix_scr", [32, 128, T], BF16,
                            kind=("ExternalInput" if p3only else ("ExternalOutput" if dbg else "Internal"))).ap()
    ssq_in = din("ssq_in", [128, 2 * NT]) if p3only else None
    x1_d = nc.dram_tensor("x1_scr", [T, D], F32, kind="Internal").ap()
    x1T_d = nc.dram_tensor("x1T_scr", [32, 128, T], BF16, kind="Internal").ap()
    dbg_d = dout("dbg", [128, 4096]) if dbg else None

    S = Sched(nc)

    def row(ap):
        return ap.rearrange("(o n) -> o n", o=1)

    def I(eng, method, reads, writes, *args, **kw):
        return S.op(eng, lambda e: getattr(e, method)(*args, **kw), reads=reads, writes=writes)

    def DMA(q, out, in_, reads, writes):
        return S.dma(q, lambda e: e.dma_start(out=out, in_=in_), reads=reads, writes=writes)

    rr = {"ev": 0}

    def evac_eng():
        rr["ev"] += 1
        return "act" if rr["ev"] % 2 else "dve"

    def copy_on(eng, out, in_, reads, writes):
        if eng == "act":
            return I("act", "activation", reads, writes, out=out, in_=in_, func=AF.Copy)
        return I(eng, "tensor_copy", reads, writes, out=out, in_=in_)

    with contextlib.ExitStack() as st:
        NW = 52224
        arena_t = st.enter_context(nc.sbuf_tensor("arena", [128, NW], F32))
        A = Arena(arena_t, NW)
        PS = [st.enter_context(nc.psum_tensor(f"ps{i}", [128, 512], F32)) for i in range(8)]

        def psb(i):
            return PS[i][:].bitcast(BF16)

        ident_bf = A.alloc([128], BF16)
        ident_f = A.alloc([128], F32)
        negtri = A.alloc([128], BF16)
        negones = A.alloc([128], BF16)
        ones_bf = A.alloc([128], BF16)
        ones_f = A.alloc([128], F32)
        maskM = A.alloc([896], BF16)
        tri_f = A.alloc([128], F32)
        negmask2 = A.alloc([128], BF16)
        FV = A.alloc([32, 16], F32)
        ssq_sb = A.alloc([NT], F32)
        ssq_ssd = A.alloc([NT], F32)
        dtb_b = A.alloc([32], F32)
        alog_b = A.alloc([32], F32)
        dskp = A.alloc([16], F32)

        m0 = A.mark()
        zero_bf = A.alloc([896], BF16)
        I("pool", "memset", [], ["negones"], negones, -1.0)
        I("pool", "memset", [], ["ones_bf"], ones_bf, 1.0)
        I("pool", "memset", [], ["ones_f"], ones_f, 1.0)
        I("pool", "memset", [], ["zero_bf"], zero_bf, 0.0)
        I("pool", "memset", [], ["ssq_sb"], ssq_sb, 0.0)
        I("pool", "memset", [], ["ssq_ssd"], ssq_ssd, 0.0)
        I("pool", "affine_select", ["ones_bf"], ["ident_bf"], out=ident_bf, in_=ones_bf, pattern=[[-1, 128]],
          compare_op=ALU.is_equal, fill=0.0, base=0, channel_multiplier=1)
        I("pool", "affine_select", ["ones_f"], ["ident_f"], out=ident_f, in_=ones_f, pattern=[[-1, 128]],
          compare_op=ALU.is_equal, fill=0.0, base=0, channel_multiplier=1)
        I("pool", "affine_select", ["negones"], ["negtri"], out=negtri, in_=negones, pattern=[[-1, 128]],
          compare_op=ALU.is_ge, fill=0.0, base=0, channel_multiplier=1)
        I("pool", "affine_select", ["zero_bf"], ["maskM"], out=maskM, in_=zero_bf, pattern=[[1, 896]],
          compare_op=ALU.is_gt, fill=NEG, base=-384, channel_multiplier=-1)
        I("pool", "affine_select", ["ones_f"], ["tri_f"], out=tri_f, in_=ones_f, pattern=[[1, 128]],
          compare_op=ALU.is_ge, fill=0.0, base=0, channel_multiplier=-1)
        I("pool", "affine_select", ["zero_bf"], ["negmask2"], out=negmask2, in_=zero_bf[:, 0:128], pattern=[[1, 128]],
          compare_op=ALU.is_ge, fill=NEG, base=0, channel_multiplier=-1)
        CONSTS = ["ident_bf", "ident_f", "negtri", "negones", "ones_bf", "ones_f", "maskM", "tri_f", "negmask2"]

        fvs = A.alloc([D], F32)
        I("pool", "memset", [], ["fvs"], fvs[0:32, :], 0.0)
        DMA("sp", fvs[0:4, :], conv_w, [], ["fvs"])
        DMA("sp", fvs[4:5, :], row(conv_b), [], ["fvs"])
        DMA("sp", fvs[5:6, 0:2048], row(sb_norm), [], ["fvs"])
        DMA("sp", fvs[5:6, 2048:4096], row(ssd_norm), [], ["fvs"])
        DMA("sp", fvs[6:9, :], sconv, [], ["fvs"])
        DMA("sp", dtb_b, dt_bias.partition_broadcast(128), [], ["dtb_b"])
        DMA("sp", alog_b, a_log.partition_broadcast(128), [], ["alog_b"])
        for c in range(32):
            I("pe", "transpose", ["fvs", "ident_f"], [("ps", 7)], out=PS[7][:, c * 16:c * 16 + 16],
              in_=fvs[0:16, c * 128:(c + 1) * 128], identity=ident_f[0:16, 0:16])
        I("dve", "tensor_copy", [("ps", 7)], ["FV"], out=FV, in_=PS[7][:].rearrange("p (c r) -> p c r", c=32))
        S.barrier()
        A.reset(m0)

        mC = A.mark()
        hT = A.alloc([32, T], BF16)
        mP = A.mark()

        if "p1" in phases:
            npre_b = A.alloc([D], F32)
            xt = [A.alloc([D], F32) for _ in range(2)]
            hns = [A.alloc([D], BF16) for _ in range(2)]
            st1 = A.alloc([NT, 4], F32)
            DMA("sp", npre_b, norm_pre.partition_broadcast(128), [], ["npre_b"])
            def p1_load(tt):
                nt = 128 if tt < 16 else 64
                b = tt % 2
                DMA("sp", xt[b][0:nt, :], x[tt * 128:tt * 128 + nt, :], [], [("xt", b)])

            def p1_square(tt):
                nt = 128 if tt < 16 else 64
                b = tt % 2
                I("act", "activation", [("xt", b)], [("hn", b), ("st1", tt)], out=hns[b][0:nt, :], in_=xt[b][0:nt, :],
                  func=AF.Square, accum_out=st1[0:nt, tt, 0:1])

            def p1_stats(tt):
                nt = 128 if tt < 16 else 64
                I("dve", "tensor_scalar", [("st1", tt)], [("st1b", tt)], out=st1[0:nt, tt, 1:2], in0=st1[0:nt, tt, 0:1],
                  scalar1=1.0 / D, scalar2=EPS, op0=ALU.mult, op1=ALU.add)
                I("act", "activation", [("st1b", tt)], [("st1c", tt)], out=st1[0:nt, tt, 2:3], in_=st1[0:nt, tt, 1:2], func=AF.Ln)
                I("act", "activation", [("st1c", tt)], [("st1d", tt)], out=st1[0:nt, tt, 3:4], in_=st1[0:nt, tt, 2:3],
                  func=AF.Exp, scale=-0.5)

            def p1_main(tt):
                nt = 128 if tt < 16 else 64
                r0 = tt * 128
                b = tt % 2
                hn = hns[b]
                hk = ("hn", b)
                I("dve", "scalar_tensor_tensor", [("xt", b), ("st1d", tt), "npre_b"], [hk], out=hn[0:nt, :],
                  in0=xt[b][0:nt, :], scalar=st1[0:nt, tt, 3:4], in1=npre_b[0:nt, :], op0=ALU.mult, op1=ALU.mult)
                for c4 in range(4):
                    bk = (tt * 4 + c4) % 4
                    pv = psb(bk).rearrange("p (j n) -> p j n", j=8)
                    for j in range(8):
                        fc = c4 * 8 + j
                        I("pe", "transpose", [hk, "ident_bf"], [("ps", bk)], out=pv[:, j, 0:nt],
                          in_=hn[0:nt, fc * 128:(fc + 1) * 128], identity=ident_bf[0:nt, 0:nt])
                    copy_on(evac_eng(), hT[:, c4 * 8:(c4 + 1) * 8, r0:r0 + nt], pv[:, :, 0:nt], [("ps", bk)], [("hT", tt)])

            p1_load(0)
            p1_square(0)
            for tt in range(NT):
                if tt + 1 < NT:
                    p1_load(tt + 1)
                p1_stats(tt)
                if tt + 1 < NT:
                    p1_square(tt + 1)
                p1_main(tt)
            S.barrier()
            A.reset(mP)

        def hkeys(t0, n):
            return [("hT", tt) for tt in range(t0 // 128, (t0 + n + 127) // 128)]

        if "p2a" in phases:
            WSL = 2
            wt = [A.alloc([32, 128], BF16) for _ in range(WSL)]
            qTs = [A.alloc([T], BF16) for _ in range(2)]
            kT = A.alloc([T], BF16)
            gs = A.alloc([T], BF16)
            v_tok = A.alloc([NT, 128], BF16)
            kst = [A.alloc([4, 128], F32) for _ in range(2)]
            kb16 = [A.alloc([4, 128], BF16) for _ in range(2)]
            e_t = A.alloc([512], F32)
            og_t = e_t
            sp_t = [A.alloc([512], BF16) for _ in range(3)]
            racc2 = A.alloc([2, 512], BF16)
            racc = [racc2[:, 0, :], racc2[:, 1, :]]
            ckb = racc2.rearrange("p a b -> p (a b)").rearrange("p (j d) -> p j d", j=8)
            att_t = [A.alloc([512], BF16) for _ in range(2)]
            sq_t = A.alloc([512], BF16)
            om_t = [A.alloc([512], BF16) for _ in range(2)]
            vcb = A.alloc([8, 128], BF16)
            kTc = A.alloc([PAST], BF16)
            sp_s = A.alloc([64], BF16)
            I("pool", "memset", [], ["sp_s"], sp_s, 0.0)
            SCALE = 128.0 ** -0.5
            cnt = {"w": 0, "bank": 0, "st": 0}

            def load_w(col0):
                slot = cnt["w"] % WSL
                cnt["w"] += 1
                for part in range(4):
                    src = w_in[part * 1024:(part + 1) * 1024, col0:col0 + 128].rearrange("(c p) n -> p c n", p=128)
                    DMA("pool", wt[slot][:, part * 8:(part + 1) * 8, :], src, [], [("wt", slot, part)])
                return slot

            def proj_fm(col0, evac):
                slot = load_w(col0)
                for tbi, (t0, n) in enumerate(TBS):
                    bk = cnt["bank"] % 4
                    cnt["bank"] += 1
                    for fc in range(32):
                        I("pe", "matmul", [("wt", slot, fc // 8)] + hkeys(t0, n), [("ps", bk)], PS[bk][:, 0:n],
                          wt[slot][:, fc, :], hT[:, fc, t0:t0 + n], start=(fc == 0), stop=(fc == 31))
                    evac(bk, tbi, t0, n)

            def proj_tm(col0, hd, dst_out, is_k):
                slot = load_w(col0)
                kpend = []
                for j4 in range(5):
                    tiles = list(range(j4 * 4, min(j4 * 4 + 4, NT)))
                    bk = cnt["bank"] % 4
                    cnt["bank"] += 1
                    for j, tt in enumerate(tiles):
                        nt = 128 if tt < 16 else 64
                        for fc in range(32):
                            I("pe", "matmul", [("wt", slot, fc // 8), ("hT", tt)], [("ps", bk)],
                              PS[bk][0:nt, j * 128:(j + 1) * 128], hT[:, fc, tt * 128:tt * 128 + nt], wt[slot][:, fc, :],
                              start=(fc == 0), stop=(fc == 31))
                    while kpend:
                        kpend.pop(0)()
                    s = cnt["st"] % 2
                    cnt["st"] += 1
                    nj = len(tiles)
                    npart = 128 if tiles[0] < 16 else 64
                    copy_on(evac_eng(), kst[s][0:npart, 0:nj, :], PS[bk][0:npart, 0:nj * 128].rearrange("p (j d) -> p j d", j=nj),
                            [("ps", bk)], [("kst", s)])
                    r0 = tiles[0] * 128
                    if npart == 128:
                        dst = dst_out[r0:r0 + nj * 128, hd * 128:(hd + 1) * 128].rearrange("(j p) d -> p j d", p=128)
                        DMA("sp", dst, kst[s][:, 0:nj, :], [("kst", s)], [])
                    else:
                        DMA("sp", dst_out[r0:r0 + 64, hd * 128:(hd + 1) * 128], kst[s][0:64, 0, :], [("kst", s)], [])
                    if is_k:
                        I("pool", "tensor_copy", [("kst", s)], [("kb16", s)], out=kb16[s][0:npart, 0:nj, :], in_=kst[s][0:npart, 0:nj, :])

                        def tr(s=s, tiles=tiles, npart=npart, nj=nj, r0=r0, j4=j4):
                            pv = psb(4 + s).rearrange("p (j n) -> p j n", j=8)
                            for j, tt in enumerate(tiles):
                                I("pe", "transpose", [("kb16", s), "ident_bf"], [("ps", 4 + s)], out=pv[:, j, 0:npart],
                                  in_=kb16[s][0:npart, j, :], identity=ident_bf[0:npart, 0:npart])
                            if npart == 128:
                                copy_on(evac_eng(), kT[:, r0:r0 + nj * 128].rearrange("p (j n) -> p j n", j=nj), pv[:, 0:nj, :],
                                        [("ps", 4 + s)], [("kT", j4)])
                            else:
                                copy_on(evac_eng(), kT[:, r0:r0 + 64], pv[:, 0, 0:64], [("ps", 4 + s)], [("kT", j4)])
                        kpend.append(tr)
                    else:
                        I("pool", "tensor_copy", [("kst", s)], [("v_tok", j4)], out=v_tok[0:npart, tiles[0]:tiles[0] + nj, :],
                          in_=kst[s][0:npart, 0:nj, :])
                while kpend:
                    kpend.pop(0)()

            def attention(hd, qT, qp, bg=None):
                blocks = []
                for qb in range(4):
                    kbs = list(range(4 * qb + 3, -1, -1))
                    for i, kb in enumerate(kbs):
                        m = kb - 4 * qb
                        blocks.append(dict(q0=qb * 512, nq=512, k_ap=kT[:, kb * 128:(kb + 1) * 128], nk=128,
                                           v_ap=v_tok[:, kb, :], mask=(maskM[:, 384 - 128 * m:384 - 128 * m + 512] if m >= 0 else None),
                                           first=(i == 0), last=(i == len(kbs) - 1), qkey=("qT", qp, qb), kkey=("kT", kb // 4),
                                           vkey=("v_tok", kb // 4), ckeys=[], sps=None, obank=6, qbi=qb))
                blocks.append(dict(q0=TP, nq=64, k_ap=kT[:, TP:T], nk=64, v_ap=v_tok[0:64, 16, :], mask=maskM[0:64, 384:448],
                                   first=True, last=False, qkey=("qT", qp, 4), kkey=("kT", 4), vkey=("v_tok", 4), ckeys=[],
                                   sps=sp_s, obank=6, qbi=4))
                for i, kb in enumerate(range(7, -1, -1)):
                    blocks.append(dict(q0=TP, nq=64, k_ap=kTc[:, kb * 128:(kb + 1) * 128], nk=128, v_ap=vcb[:, kb, :], mask=None,
                                       first=False, last=(i == 7), qkey=("qT", qp, 4), kkey="kTc", vkey="vcb", ckeys=[], sps=None,
                                       obank=6, qbi=4))
                n = len(blocks)
                state = {"R": None, "Rkey": None, "ra": 0}

                def Zm(i):
                    b = blocks[i]
                    zb = i % 3
                    nk, nq = b["nk"], b["nq"]
                    I("pe", "matmul", [b["kkey"], b["qkey"]], [("ps", zb)], PS[zb][0:nk, 0:nq], b["k_ap"], qT[:, b["q0"]:b["q0"] + nq],
                      start=True, stop=False)
                    if b["mask"] is not None:
                        I("pe", "matmul", ["ident_bf", "maskM"], [("ps", zb)], PS[zb][0:nk, 0:nq], ident_bf[0:nk, 0:nk], b["mask"],
                          start=False, stop=False)
                    I("act", "activation", [("ps", zb)], ["e_t"], out=e_t[0:nk, 0:nq], in_=PS[zb][0:nk, 0:nq], func=AF.Exp)
                    if b["sps"] is not None:
                        sp_ap, spk = b["sps"], "sp_s"
                        b["sp_full"] = sp_ap[:, 0:nq]
                    else:
                        si = i % 3
                        sp_ap, spk = sp_t[si], ("sp", si)
                        b["sp_full"] = sp_ap[:, 0:nq]
                    b["sp"], b["spk"] = sp_ap, spk
                    I("act", "activation", ["e_t"], [spk], out=sp_ap[0:nk, 0:nq], in_=e_t[0:nk, 0:nq], func=AF.Ln, bias=1.0)
                    if b["first"]:
                        b["R"], b["Rk"] = None, None
                    else:
                        pb = blocks[i - 1]
                        if pb["first"]:
                            b["R"], b["Rk"] = pb["sp_full"], pb["spk"]
                        else:
                            ra = state["ra"] % 2
                            state["ra"] += 1
                            I("dve", "tensor_tensor", [pb["Rk"], pb["spk"]], [("racc", ra)], out=racc[ra][:, 0:nq],
                              in0=pb["R"], in1=pb["sp_full"], op=ALU.add)
                            b["R"], b["Rk"] = racc[ra][:, 0:nq], ("racc", ra)

                def Am(i):
                    b = blocks[i]
                    ab = i % 3
                    nk, nq = b["nk"], b["nq"]
                    I("pe", "matmul", ["negtri", b["spk"]], [("ps", ab)], PS[ab][0:nk, 0:nq], negtri[0:nk, 0:nk], b["sp"][0:nk, 0:nq],
                      start=False, stop=(b["R"] is None))
                    if b["R"] is not None:
                        I("pe", "matmul", ["negones", b["Rk"]], [("ps", ab)], PS[ab][0:nk, 0:nq], negones[:, 0:nk], b["R"],
                          start=False, stop=True)
                    ai = i % 2
                    I("act", "activation", [("ps", ab)], [("att", ai)], out=att_t[ai][0:nk, 0:nq], in_=PS[ab][0:nk, 0:nq], func=AF.Exp)

                def AVm(i):
                    b = blocks[i]
                    nk, nq = b["nk"], b["nq"]
                    ob = b["obank"]
                    ai = i % 2
                    I("pe", "matmul", [b["vkey"], ("att", ai)], [("ps", ob)], PS[ob][:, 0:nq], b["v_ap"][0:nk, :], att_t[ai][0:nk, 0:nq],
                      start=b["first"], stop=b["last"])
                    if b["last"]:
                        q0, qbi = b["q0"], b["qbi"]
                        I("dve", "tensor_tensor", [("ps", ob), ("gs", qbi)], ["e_t"], out=og_t[:, 0:nq], in0=PS[ob][:, 0:nq],
                          in1=gs[:, q0:q0 + nq], op=ALU.mult)
                        oi = qbi % 2
                        I("dve", "tensor_scalar", ["e_t", "FV"], [("om", oi)], out=om_t[oi][:, 0:nq], in0=og_t[:, 0:nq],
                          scalar1=FV[:, hd, 5:6], scalar2=None, op0=ALU.mult)
                        DMA("sp", omix_d[hd, :, q0:q0 + nq], om_t[oi][:, 0:nq], [("om", oi)], [("omix", hd, qbi)])
                        I("act", "activation", ["e_t"], ["sq_t"], out=sq_t[:, 0:nq], in_=og_t[:, 0:nq], func=AF.Square)
                        def ssq_mm(q0=q0, nq=nq):
                            for j in range((nq + 127) // 128):
                                tt = q0 // 128 + j
                                nt = min(128, nq - j * 128)
                                I("pe", "matmul", ["sq_t", "ones_bf"], [("ps", 7)], PS[7][0:nt, tt:tt + 1], sq_t[:, j * 128:j * 128 + nt],
                                  ones_bf[:, 0:1], start=True, stop=True)
                        pending.append(ssq_mm)

                pending = []
                for s in range(n + 2):
                    if pending and s % 2 == 0:
                        pending.pop(0)()
                    if s < n:
                        Zm(s)
                    if 0 <= s - 1 < n:
                        Am(s - 1)
                    if 0 <= s - 2 < n:
                        AVm(s - 2)
                    if bg is not None:
                        for _ in range(4):
                            next(bg, None)
                if bg is not None:
                    for _ in bg:
                        pass
                while pending:
                    pending.pop(0)()
                I("dve", "tensor_tensor", [("ps", 7), "ssq_sb"], ["ssq_sb"], out=ssq_sb[:, 0:16], in0=PS[7][:, 0:16], in1=ssq_sb[:, 0:16], op=ALU.add)
                I("dve", "tensor_tensor", [("ps", 7), "ssq_sb"], ["ssq_sb"], out=ssq_sb[0:64, 16:17], in0=PS[7][0:64, 16:17],
                  in1=ssq_sb[0:64, 16:17], op=ALU.add)

            nheads = 16 if not dbg else dbg.get("nheads", 16)

            def q_proj_gen(hd):
                qp = hd % 2
                slot = load_w(Q0 + hd * 128)
                for tbi, (t0, n) in enumerate(TBS):
                    bk = 4 + tbi % 2
                    for fc in range(32):
                        I("pe", "matmul", [("wt", slot, fc // 8)] + hkeys(t0, n), [("ps", bk)], PS[bk][:, 0:n],
                          wt[slot][:, fc, :], hT[:, fc, t0:t0 + n], start=(fc == 0), stop=(fc == 31))
                        yield
                    I("dve", "tensor_scalar", [("ps", bk)], [("qT", qp, tbi)], out=qTs[qp][:, t0:t0 + n], in0=PS[bk][:, 0:n],
                      scalar1=SCALE, scalar2=None, op0=ALU.mult)

            for _ in q_proj_gen(0):
                pass
            for hd in range(nheads):
                proj_tm(K0 + hd * 128, hd, k_out, True)
                proj_tm(V0 + hd * 128, hd, v_out, False)
                DMA("pool", ckb, ck[:, hd * 128:(hd + 1) * 128].rearrange("(j p) d -> p j d", p=128), [], [("racc", 0), ("racc", 1)])
                DMA("pool", vcb, cv[:, hd * 128:(hd + 1) * 128].rearrange("(j p) d -> p j d", p=128), [], ["vcb"])
                proj_fm(G0 + hd * 128, lambda bk, tbi, t0, n: I("act", "activation", [("ps", bk)], [("gs", tbi)], out=gs[:, t0:t0 + n],
                                                               in_=PS[bk][:, 0:n], func=AF.Silu))
                pv = psb(7).rearrange("p (j n) -> p j n", j=8)
                for j in range(8):
                    I("pe", "transpose", [("racc", 0), ("racc", 1), "ident_bf"], [("ps", 7)], out=pv[:, j, :], in_=ckb[:, j, :], identity=ident_bf)
                copy_on(evac_eng(), kTc.rearrange("p (j n) -> p j n", j=8), pv, [("ps", 7)], ["kTc"])
                attention(hd, qTs[hd % 2], hd % 2, q_proj_gen(hd + 1) if hd + 1 < nheads else None)
            if dbg:
                I("dve", "tensor_copy", ["ssq_sb"], ["dbgt"], out=og_t[:, 0:NT], in_=ssq_sb)
                DMA("sp", dbg_d[:, 0:NT], og_t[:, 0:NT], ["dbgt"], [])
            S.barrier()
            A.reset(mP)


        if "p2b" in phases:
            A.reset(mP)
            wt = [A.alloc([32, 128], BF16) for _ in range(2)]
            rA = A.alloc([2120], F32)
            wdt = rA[:, 0:512].bitcast(BF16).rearrange("p (c n) -> p c n", c=32)
            rB = A.alloc([2120], F32)
            xact = [A.alloc([T], BF16) for _ in range(2)]
            Bact = A.alloc([T], BF16)
            Cact = A.alloc([T], BF16)
            dt_tok = A.alloc([NT, 32], F32)
            a_b = A.alloc([32], F32)
            dsk_b = A.alloc([32], F32)
            hlast = A.alloc([32, 8], BF16)
            cst = [A.alloc([128], F32) for _ in range(1)]
            dAb4 = A.alloc([4, 128], F32)
            D4 = A.alloc([4, 128], F32)
            Wt4 = A.alloc([4, 128], BF16)
            xB_tok = A.alloc([3, 128], BF16)
            w4 = A.alloc([4], F32)
            xw = A.alloc([4, 64], BF16)
            dAx = A.alloc([2, 128], F32)
            t1 = A.alloc([128], F32)
            Hs = A.alloc([256], F32)
            Hbf = A.alloc([256], BF16)
            hst = [A.alloc([128], F32) for _ in range(1)] * 2
            zs_t = [A.alloc([512], BF16) for _ in range(1)]
            om_b = [A.alloc([512], BF16) for _ in range(1)]
            cntb = {"w": 0, "bank": 0, "z": 0, "c": 0, "h": 0}
            zpend = []

            def load_wb(col0):
                slot = cntb["w"] % 2
                cntb["w"] += 1
                for part in range(4):
                    src = w_in[part * 1024:(part + 1) * 1024, col0:col0 + 128].rearrange("(c p) n -> p c n", p=128)
                    DMA("pool", wt[slot][:, part * 8:(part + 1) * 8, :], src, [], [("wt", slot, part)])
                return slot

            def proj_fmb(col0, evac, extra=None):
                slot = load_wb(col0)
                for tbi, (t0, n) in enumerate(TBS):
                    bk = cntb["bank"] % 3
                    cntb["bank"] += 1
                    for fc in range(32):
                        I("pe", "matmul", [("wt", slot, fc // 8)] + hkeys(t0, n), [("ps", bk)], PS[bk][:, 0:n],
                          wt[slot][:, fc, :], hT[:, fc, t0:t0 + n], start=(fc == 0), stop=(fc == 31))
                    evac(bk, tbi, t0, n)
                if extra is not None:
                    extra(slot)

            DMA("pool", wdt, w_in[:, DT0:DT0 + 32].rearrange("(c p) n -> p c n", p=128), [], ["wdt", "rA"])
            DMA("sp", dsk_b, d_skip.partition_broadcast(128), [], ["dsk_b"])
            I("dve", "tensor_copy", ["dsk_b"], ["dskp"], out=dskp[0:64, :], in_=dsk_b[0:64, :].rearrange("p (j t) -> p j t", t=2)[:, :, 0])
            I("dve", "tensor_copy", ["dsk_b"], ["dskp"], out=dskp[64:128, :], in_=dsk_b[64:128, :].rearrange("p (j t) -> p j t", t=2)[:, :, 1])
            I("act", "activation", ["alog_b"], ["a_e"], out=a_b, in_=alog_b, func=AF.Exp)
            I("dve", "tensor_scalar", ["a_e"], ["a_b"], out=a_b, in0=a_b, scalar1=-1.0, scalar2=None, op0=ALU.mult)
            for tt in range(NT):
                nt = 128 if tt < 16 else 64
                bk, cc = (3, tt * 32) if tt < 16 else (4, 0)
                for fc in range(32):
                    I("pe", "matmul", ["wdt", "rA", ("hT", tt)], [("ps", bk)], PS[bk][0:nt, cc:cc + 32], hT[:, fc, tt * 128:tt * 128 + nt],
                      wdt[:, fc, :], start=(fc == 0), stop=(fc == 31))
            I("dve", "tensor_tensor", [("ps", 3), "dtb_b"], ["dt_tok"], out=dt_tok[:, 0:16, :],
              in0=PS[3][:].rearrange("p (t h) -> p t h", t=16),
              in1=dtb_b.rearrange("p (o h) -> p o h", o=1).broadcast_to([128, 16, 32]), op=ALU.add)
            I("dve", "tensor_tensor", [("ps", 4), "dtb_b"], ["dt_tok"], out=dt_tok[0:64, 16, :], in0=PS[4][0:64, 0:32], in1=dtb_b[0:64, :], op=ALU.add)
            I("act", "activation", ["dt_tok"], ["dt_e"], out=dt_tok[:, 0:16, :], in_=dt_tok[:, 0:16, :], func=AF.Exp)
            I("act", "activation", ["dt_tok", "dt_e"], ["dt_e"], out=dt_tok[0:64, 16, :], in_=dt_tok[0:64, 16, :], func=AF.Exp)
            I("act", "activation", ["dt_e"], ["dt_f"], out=dt_tok[:, 0:16, :], in_=dt_tok[:, 0:16, :], func=AF.Ln, bias=1.0)
            I("act", "activation", ["dt_e", "dt_f"], ["dt_f"], out=dt_tok[0:64, 16, :], in_=dt_tok[0:64, 16, :], func=AF.Ln, bias=1.0)
            I("dve", "tensor_copy", [("hT", 15)], ["hlast"], out=hlast[:, :, 0:3], in_=hT[:, :, TP - 3:TP])
            I("dve", "tensor_copy", [("hT", 16), "hlast"], ["hlast"], out=hlast[:, :, 3:6], in_=hT[:, :, T - 3:T])

            def proj_conv(col0, dst, dkey):
                ch = (col0 - XBC0) // 128
                raw, cvb = rA, rB

                def ev(bk, tbi, t0, n):
                    o0 = 3 + t0 if tbi < 4 else 2054
                    copy_on(evac_eng(), raw[:, o0:o0 + n], PS[bk][:, 0:n], [("ps", bk)], ["rA"])

                def extra(slot):
                    cs = 0
                    cntb["c"] += 1
                    for fc in range(32):
                        I("pe", "matmul", [("wt", slot, fc // 8), "hlast"], [("ps", 4)], PS[4][0:6, 128:256], hlast[:, fc, 0:6], wt[slot][:, fc, :],
                          start=(fc == 0), stop=(fc == 31))
                    copy_on("act", cst[cs][0:6, :], PS[4][0:6, 128:256], [("ps", 4)], [("cst", cs)])
                    DMA("sp", conv_out[0:6, ch * 128:(ch + 1) * 128], cst[cs][0:6, :], [("cst", cs)], [])

                proj_fmb(col0, ev, extra)
                I("dve", "memset", [], ["rA"], raw[:, 0:3], 0.0)
                I("dve", "tensor_copy", ["FV"], ["rA"], out=raw[:, 2051:2054], in_=FV[:, ch, 6:9])
                L = 2115
                I("dve", "tensor_scalar", ["rA", "FV"], ["rB"], out=cvb[:, 0:L], in0=raw[:, 0:L], scalar1=FV[:, ch, 0:1], scalar2=FV[:, ch, 4:5],
                  op0=ALU.mult, op1=ALU.add)
                for j in range(1, 4):
                    I("dve", "scalar_tensor_tensor", ["rA", "rB", "FV"], ["rB"], out=cvb[:, 0:L], in0=raw[:, j:j + L], scalar=FV[:, ch, j:j + 1],
                      in1=cvb[:, 0:L], op0=ALU.mult, op1=ALU.add)
                I("act", "activation", ["rB"], [dkey], out=dst[:, 0:TP], in_=cvb[:, 0:TP], func=AF.Silu)
                I("act", "activation", ["rB", dkey], [dkey], out=dst[:, TP:T], in_=cvb[:, 2051:2115], func=AF.Silu)

            w1f = wt[1].rearrange("p a b -> p (a b)")

            def carve(off, n, dtype):
                if dtype == F32:
                    return w1f[:, off // 2:off // 2 + 2 * n].bitcast(F32)
                return w1f[:, off // 2:off // 2 + n]
            TS2 = [dict(dAb4=dAb4, D4=D4, dAx=dAx, Wt4=Wt4, xB_tok=xB_tok, xw=xw, w4=w4),
                   dict(dAb4=carve(0, 512, F32).rearrange("p (a b) -> p a b", a=4), D4=carve(2048, 512, F32).rearrange("p (a b) -> p a b", a=4),
                        dAx=carve(4096, 256, F32).rearrange("p (a b) -> p a b", a=2), Wt4=carve(5120, 512, BF16).rearrange("p (a b) -> p a b", a=4),
                        xB_tok=carve(6144, 384, BF16).rearrange("p (a b) -> p a b", a=3), xw=carve(6912, 256, BF16).rearrange("p (a b) -> p a b", a=4),
                        w4=carve(7424, 4, F32))]
            dA_g = carve(7456, 68, F32).rearrange("p (c h) -> p c h", c=NT)
            negcum_g = carve(7744, 68, F32).rearrange("p (c h) -> p c h", c=NT)
            etot_g = A.alloc([NT, 4], F32)
            T1KEYS = [("dAb4", 1), ("dAx", 1), ("xB_tok", 1), ("w4", 1), ("xw", 1), "dA_g", "negcum_g"] + \
                     [("D4", 1, hq) for hq in range(4)] + [("Wt4", 1, hq) for hq in range(4)]
            W1KEYS = [("wt", 1, part) for part in range(4)]
            zflat = zs_t[0]
            oflat = om_b[0]
            ecxs = [[zflat[:, 0:256].bitcast(F32), zflat[:, 256:512].bitcast(F32)], [oflat[:, 0:256].bitcast(F32), oflat[:, 256:512].bitcast(F32)]]
            EKEYS = [("ecx", p_, r_) for p_ in range(2) for r_ in range(2)]
            ZKEYS = [("zs", 0), ("omb", 0)]

            def group_pre(g):
                hs = slice(4 * g, 4 * g + 4)
                I("dve", "tensor_tensor", ["dt_f", "a_b"], ["dA_g"], out=dA_g[:, 0:16, :], in0=dt_tok[:, 0:16, hs],
                  in1=a_b[:, hs].rearrange("p (o h) -> p o h", o=1).broadcast_to([128, 16, 4]), op=ALU.mult)
                I("dve", "tensor_tensor", ["dt_f", "a_b", "dA_g"], ["dA_g"], out=dA_g[0:64, 16, :], in0=dt_tok[0:64, 16, hs], in1=a_b[0:64, hs], op=ALU.mult)
                for c in range(NT):
                    nt = 128 if c < 16 else 64
                    I("pe", "matmul", ["dA_g", "tri_f"], [("ps", 3)], PS[3][0:nt, c * 4:c * 4 + 4], tri_f[0:nt, 0:nt], dA_g[0:nt, c, :], start=True, stop=True)
                    I("pe", "matmul", ["dA_g", "ones_f"], [("ps", 3)], PS[3][:, 128 + c * 4:128 + c * 4 + 4], ones_f[0:nt, :], dA_g[0:nt, c, :],
                      start=True, stop=True)
                I("dve", "tensor_scalar", [("ps", 3)], ["negcum_g"], out=negcum_g[:, 0:16, :], in0=PS[3][:, 0:64].rearrange("p (c h) -> p c h", c=16),
                  scalar1=-1.0, scalar2=None, op0=ALU.mult)
                I("dve", "tensor_scalar", [("ps", 3), "negcum_g"], ["negcum_g"], out=negcum_g[0:64, 16, :], in0=PS[3][0:64, 64:68], scalar1=-1.0, scalar2=None,
                  op0=ALU.mult)
                I("act", "activation", [("ps", 3)], ["etot_g"], out=etot_g, in_=PS[3][:, 128:196].rearrange("p (c h) -> p c h", c=NT), func=AF.Exp)

            def geo(c):
                nt = 128 if c < 16 else 64
                par = c % 2
                banks = (5, 4, 7, 3) if par == 0 else (6, 0, 1, 2)
                return nt, c * 128, par, TS2[par], banks

            def u1a(g, c, inter):
                nt, t0, par, tt_, _ = geo(c)
                dA4 = dA_g[0:nt, c, :]
                I("dve", "tensor_copy", ["dA_g"], [("dAb4", par)], out=tt_["dAb4"][0:nt, :, 0:nt],
                  in_=dA4.rearrange("p (h o) -> p h o", o=1).broadcast_to([nt, 4, nt]))
                if inter:
                    I("dve", "tensor_copy", ["dA_g"], [("dAx", par)], out=tt_["dAx"][0:nt, :, :].rearrange("p a (b q) -> p (a b) q", q=64),
                      in_=dA4.rearrange("p (h o) -> p h o", o=1).broadcast_to([nt, 4, 64]))

            def u1b(g, c, inter):
                nt, t0, par, tt_, (pb, b1, yb, tb) = geo(c)
                dAb4, D4, dAx, Wt4, xB_tok, xw, w4 = tt_["dAb4"], tt_["D4"], tt_["dAx"], tt_["Wt4"], tt_["xB_tok"], tt_["xw"], tt_["w4"]
                kdAb, kdAx, kxB, kw4, kxw = ("dAb4", par), ("dAx", par), ("xB_tok", par), ("w4", par), ("xw", par)
                hs = slice(4 * g, 4 * g + 4)
                I("pe", "matmul", ["Bact", "Cact"], [("ps", b1)], PS[b1][0:nt, 0:nt], Bact[:, t0:t0 + nt], Cact[:, t0:t0 + nt], start=True, stop=True)
                for hq in range(4):
                    I("pe", "matmul", [kdAb, "tri_f"], [("ps", pb)], PS[pb][0:nt, hq * 128:hq * 128 + nt], dAb4[0:nt, hq, 0:nt], tri_f[0:nt, 0:nt],
                      start=True, stop=False)
                    I("pe", "matmul", ["ident_bf", "negmask2"], [("ps", pb)], PS[pb][0:nt, hq * 128:hq * 128 + nt], ident_bf[0:nt, 0:nt],
                      negmask2[0:nt, 0:nt], start=False, stop=True)
                pv2 = psb(tb).rearrange("p (j n) -> p j n", j=8)
                for pr in range(2):
                    I("pe", "transpose", [("xact", pr), "ident_bf"], [("ps", tb)], out=pv2[0:nt, pr, :], in_=xact[pr][:, t0:t0 + nt], identity=ident_bf)
                I("pe", "transpose", ["Bact", "ident_bf"], [("ps", tb)], out=pv2[0:nt, 2, :], in_=Bact[:, t0:t0 + nt], identity=ident_bf)
                if inter:
                    for pr in range(2):
                        I("pe", "matmul", [kdAx, "tri_f"], [("ps", b1)], PS[b1][:, 128 + pr * 128:128 + pr * 128 + nt],
                          dAx[0:nt, pr, :], tri_f[0:nt, 0:nt], start=True, stop=True)
                for hq in range(4):
                    I("act", "activation", [("ps", pb), "negcum_g"], [("D4", par, hq)], out=D4[0:nt, hq, 0:nt], in_=PS[pb][0:nt, hq * 128:hq * 128 + nt],
                      func=AF.Exp, bias=negcum_g[0:nt, c, hq:hq + 1], scale=1.0)
                    I("dve", "scalar_tensor_tensor", [("D4", par, hq), "dt_f", ("ps", b1)], [("Wt4", par, hq)], out=Wt4[0:nt, hq, 0:nt],
                      in0=D4[0:nt, hq, 0:nt], scalar=dt_tok[0:nt, c, 4 * g + hq:4 * g + hq + 1], in1=PS[b1][0:nt, 0:nt], op0=ALU.mult, op1=ALU.mult)
                copy_on("act", xB_tok[0:nt, :, :], pv2[0:nt, 0:3, :], [("ps", tb)], [kxB])
                if inter:
                    for pr in range(2):
                        I("act", "activation", [("ps", b1)], [("ecx", par, pr)], out=ecxs[par][pr][:, 0:nt],
                          in_=PS[b1][:, 128 + pr * 128:128 + pr * 128 + nt], func=AF.Exp)
                D4k = [("D4", par, hq) for hq in range(4)]
                I("dve", "tensor_tensor", D4k + ["dt_f"], [kw4], out=w4[0:nt, :], in0=D4[0:nt, :, nt - 1], in1=dt_tok[0:nt, c, hs], op=ALU.mult)
                I("dve", "tensor_tensor", [kxB, kw4], [kxw], out=xw[0:nt, :, :],
                  in0=xB_tok[0:nt, 0:2, :].rearrange("p a (b q) -> p (a b) q", q=64),
                  in1=w4[0:nt, :].rearrange("p (h o) -> p h o", o=1).broadcast_to([nt, 4, 64]), op=ALU.mult)

            def u2(g, c, first, inter):
                nt, t0, par, tt_, (pb, b1, yb, tb) = geo(c)
                Wt4, xB_tok, xw = tt_["Wt4"], tt_["xB_tok"], tt_["xw"]
                kxB, kxw = ("xB_tok", par), ("xw", par)
                yall = [rA, rB]
                ykey = ["rA", "rB"]
                for pr in range(2):
                    for hh in range(2):
                        I("pe", "matmul", [kxB, ("Wt4", par, 2 * pr + hh)], [("ps", yb)], PS[yb][hh * 64:(hh + 1) * 64, pr * 128:pr * 128 + nt],
                          xB_tok[0:nt, pr, hh * 64:(hh + 1) * 64], Wt4[0:nt, 2 * pr + hh, 0:nt], start=True, stop=True)
                    if inter:
                        I("pe", "matmul", ["Hbf", "Cact"], [("ps", yb)], PS[yb][:, 256 + pr * 128:256 + pr * 128 + nt], Hbf[:, pr * 128:(pr + 1) * 128],
                          Cact[:, t0:t0 + nt], start=True, stop=True)
                I("pe", "matmul", [kxB, kxw], [("ps", tb)], PS[tb][:, 256:512], xB_tok[0:nt, 2, :], xw[0:nt, :, :].rearrange("p h q -> p (h q)"),
                  start=True, stop=True)
                if first:
                    I("dve", "tensor_copy", [("ps", tb)], ["Hs"], out=Hs, in_=PS[tb][:, 256:512])
                else:
                    I("dve", "tensor_tensor", ["Hs", "etot_g"], ["Hs"], out=Hs.rearrange("p (h q) -> p h q", q=64), in0=Hs.rearrange("p (h q) -> p h q", q=64),
                      in1=etot_g[:, c, :].rearrange("p (h o) -> p h o", o=1).broadcast_to([128, 4, 64]), op=ALU.mult)
                    I("dve", "tensor_tensor", ["Hs", ("ps", tb)], ["Hs"], out=Hs, in0=PS[tb][:, 256:512], in1=Hs, op=ALU.add)
                for pr in range(2):
                    if inter:
                        I("dve", "tensor_tensor", [("ps", yb), ("ecx", par, pr)], ["t1"], out=t1[:, 0:nt], in0=PS[yb][:, 256 + pr * 128:256 + pr * 128 + nt],
                          in1=ecxs[par][pr][:, 0:nt], op=ALU.mult)
                        I("dve", "tensor_tensor", [("ps", yb), "t1"], [ykey[pr]], out=yall[pr][:, t0:t0 + nt], in0=PS[yb][:, pr * 128:pr * 128 + nt],
                          in1=t1[:, 0:nt], op=ALU.add)
                    else:
                        I("dve", "tensor_copy", [("ps", yb)], [ykey[pr]], out=yall[pr][:, t0:t0 + nt], in_=PS[yb][:, pr * 128:pr * 128 + nt])

            def hbf_update():
                I("act", "activation", ["Hs"], ["Hbf"], out=Hbf, in_=Hs, func=AF.Copy)

            def state_out(g, row0):
                for pr in range(2):
                    hsx = cntb["h"] % 2
                    cntb["h"] += 1
                    I("pe", "transpose", ["Hs", "ident_f"], [("ps", 3)], out=PS[3][:, 0:128], in_=Hs[:, pr * 128:(pr + 1) * 128], identity=ident_f)
                    copy_on("act", hst[hsx], PS[3][:, 0:128], [("ps", 3)], ["hst"])
                    DMA("sp", ssd_out[row0 + g * 256 + pr * 128:row0 + g * 256 + (pr + 1) * 128, :], hst[hsx], ["hst"], [])

            def state_in(g):
                for pr in range(2):
                    hsx = cntb["h"] % 2
                    cntb["h"] += 1
                    DMA("sp", hst[hsx], sssd[g * 256 + pr * 128:g * 256 + (pr + 1) * 128, :], [], ["hst"])
                    I("pe", "transpose", ["hst", "ident_f"], [("ps", 3)], out=PS[3][:, 0:128], in_=hst[hsx], identity=ident_f)
                    I("dve", "tensor_copy", [("ps", 3)], ["Hs"], out=Hs[:, pr * 128:(pr + 1) * 128], in_=PS[3][:, 0:128])
                I("act", "activation", ["Hs"], ["Hbf"], out=Hbf, in_=Hs, func=AF.Copy)

            ngroups = 8 if not dbg else dbg.get("ngroups", 8)
            for g in range(ngroups):
                proj_conv(XBC0 + g * 256, xact[0], ("xact", 0))
                proj_conv(XBC0 + g * 256 + 128, xact[1], ("xact", 1))
                proj_conv(XBC0 + 2048 + g * 128, Bact, "Bact")
                proj_conv(XBC0 + 3072 + g * 128, Cact, "Cact")
                S.fence(W1KEYS, T1KEYS)
                S.fence(ZKEYS, EKEYS)
                group_pre(g)
                u1a(g, 0, False)
                for st_ in range(18):
                    if st_ + 1 < 17:
                        u1a(g, st_ + 1, True)
                    if st_ < 17:
                        u1b(g, st_, inter=(st_ > 0))
                    if st_ >= 1:
                        c = st_ - 1
                        if c == 16:
                            state_out(g, 0)
                            state_in(g)
                        u2(g, c, first=(c == 0), inter=(c > 0))
                        hbf_update()
                state_out(g, 2048)
                S.fence(T1KEYS, W1KEYS)
                S.fence(EKEYS, ZKEYS)
                yall = [rA, rB]
                ykey = ["rA", "rB"]
                for pr in range(2):
                    fcx = 16 + 2 * g + pr
                    I("dve", "scalar_tensor_tensor", [("xact", pr), "dskp", ykey[pr]], [ykey[pr]], out=yall[pr][:, 0:T], in0=xact[pr][:, 0:T],
                      scalar=dskp[:, 2 * g + pr:2 * g + pr + 1], in1=yall[pr][:, 0:T], op0=ALU.mult, op1=ALU.add)

                    def evz(bk, tbi, t0, n, pr=pr, fcx=fcx):
                        zi = 0
                        while zpend:
                            zpend.pop(0)()
                        cntb["z"] += 1
                        I("act", "activation", [("ps", bk)], [("zs", zi)], out=zs_t[zi][:, 0:n], in_=PS[bk][:, 0:n], func=AF.Silu)
                        I("dve", "tensor_tensor", [ykey[pr], ("zs", zi)], [ykey[pr]], out=yall[pr][:, t0:t0 + n], in0=yall[pr][:, t0:t0 + n],
                          in1=zs_t[zi][:, 0:n], op=ALU.mult)
                        I("dve", "tensor_scalar", [ykey[pr], "FV"], [("omb", zi)], out=om_b[zi][:, 0:n], in0=yall[pr][:, t0:t0 + n],
                          scalar1=FV[:, fcx, 5:6], scalar2=None, op0=ALU.mult)
                        DMA("sp", omix_d[fcx, :, t0:t0 + n], om_b[zi][:, 0:n], [("omb", zi)], [])
                        I("act", "activation", [ykey[pr]], [("zs", zi)], out=zs_t[zi][:, 0:n], in_=yall[pr][:, t0:t0 + n], func=AF.Square)
                        def ssq_mm(t0=t0, n=n, zi=zi):
                            for j in range((n + 127) // 128):
                                tt = t0 // 128 + j
                                nt = min(128, n - j * 128)
                                I("pe", "matmul", [("zs", zi), "ones_bf"], [("ps", 5)], PS[5][0:nt, tt:tt + 1], zs_t[zi][:, j * 128:j * 128 + nt],
                                  ones_bf[:, 0:1], start=True, stop=True)
                        zpend.append(ssq_mm)

                    proj_fmb(Z0 + g * 256 + pr * 128, evz)
                    while zpend:
                        zpend.pop(0)()
                    I("dve", "tensor_tensor", [("ps", 5), "ssq_ssd"], ["ssq_ssd"], out=ssq_ssd[:, 0:16], in0=PS[5][:, 0:16], in1=ssq_ssd[:, 0:16], op=ALU.add)
                    I("dve", "tensor_tensor", [("ps", 5), "ssq_ssd"], ["ssq_ssd"], out=ssq_ssd[0:64, 16:17], in0=PS[5][0:64, 16:17],
                      in1=ssq_ssd[0:64, 16:17], op=ALU.add)
            if dbg:
                I("dve", "tensor_copy", ["ssq_ssd"], ["dbgt"], out=t1[:, 0:NT], in_=ssq_ssd)
                DMA("sp", dbg_d[:, 32:32 + NT], t1[:, 0:NT], ["dbgt"], [])
            S.barrier()
            A.reset(mP)

        if "p3" in phases:
            A.reset(mC)
            if p3only:
                DMA("sp", ssq_sb, ssq_in[:, 0:NT], [], ["ssq_sb"])
                DMA("sp", ssq_ssd, ssq_in[:, NT:2 * NT], [], ["ssq_ssd"])
            GROUPS = [list(range(0, 6)), list(range(6, 12)), list(range(12, 17))]
            rs = A.alloc([4, NT], F32)
            for (src, key, a, b) in ((ssq_sb, "ssq_sb", 0, 1), (ssq_ssd, "ssq_ssd", 2, 3)):
                I("dve", "tensor_scalar", [key], [("rs", a)], out=rs[:, a, :], in0=src, scalar1=1.0 / 2048, scalar2=EPS,
                  op0=ALU.mult, op1=ALU.add)
                I("act", "activation", [("rs", a)], [("rsl", a)], out=rs[:, a, :], in_=rs[:, a, :], func=AF.Ln)
                I("act", "activation", [("rsl", a)], [("rs", b)], out=rs[:, b, :], in_=rs[:, a, :], func=AF.Exp, scale=-0.5)
            m3 = A.mark()

            def stage3(first):
                A.reset(m3)
                src_d = omix_d if first else x1T_d
                W = w_out if first else w_gate
                nrm = norm_post if first else ple_norm
                res_src = x if first else x1_d
                res_dst = x1_d if first else y
                oT = A.alloc([32, 768], BF16)
                msb = A.alloc([6, D], F32)
                NW3 = 4 if first else 3
                w3 = [A.alloc([4, 512], BF16) for _ in range(NW3)]
                nrmr = [A.alloc([512], F32) for _ in range(2)]
                xr = [A.alloc([1024], F32) for _ in range(2)]
                junk = A.alloc([512], F32) if first else None
                stt = A.alloc([6, 12], F32)
                rst = A.alloc([6], F32)
                if first:
                    rB = [A.alloc([768], F32) for _ in range(2)]
                    dg = A.alloc([128], F32)
                    x1b = [A.alloc([1024], BF16) for _ in range(4)]
                    x1Ts = [A.alloc([8, 128], BF16) for _ in range(2)]
                else:
                    pT = A.alloc([2, 768], BF16)
                    pst = A.alloc([256], F32)
                    pbf = A.alloc([256], BF16)
                    wp = [A.alloc([2, 512], BF16) for _ in range(2)]
                    sg = [A.alloc([512], F32) for _ in range(2)]
                    tmp = [A.alloc([512], F32) for _ in range(1)] * 2
                    e_sb = A.alloc([6, 512], F32)
                c3 = {"w": 0, "n": 0, "x": 0, "e": 0, "t": 0}

                def geom(tiles):
                    nts = [128 if t < 16 else 64 for t in tiles]
                    return tiles[0] * 128, nts, sum(nts)

                def load_group(tiles):
                    t0, nts, ng = geom(tiles)
                    S.tag = "load_group"
                    for c8 in range(4):
                        DMA("sp", oT[:, c8 * 8:(c8 + 1) * 8, 0:ng],
                            src_d[c8 * 8:(c8 + 1) * 8, :, t0:t0 + ng].rearrange("c p t -> p c t"), [], [("oT", c8)])
                    if first:
                        for which, col in ((0, 1), (1, 3)):
                            for j, tt in enumerate(tiles):
                                nt = nts[j]
                                bank = 6 + (j * 128) // 512
                                cc = (j * 128) % 512
                                I("dve", "tensor_scalar", ["ident_f", ("rs", col)], ["dg"], out=dg[0:nt, 0:nt], in0=ident_f[0:nt, 0:nt],
                                  scalar1=rs[0:nt, col, tt:tt + 1], scalar2=None, op0=ALU.mult)
                                I("pe", "matmul", ["dg", "ones_f"], [("ps", bank)], PS[bank][:, cc:cc + nt], ones_f[0:nt, :], dg[0:nt, 0:nt],
                                  start=True, stop=True)
                            n6 = min(ng, 512)
                            copy_on("act", rB[which][:, 0:n6], PS[6][:, 0:n6], [("ps", 6)], [("rB", which)])
                            if ng > 512:
                                copy_on("act", rB[which][:, 512:ng], PS[7][:, 0:ng - 512], [("ps", 7)], [("rB", which)])
                        for c8 in range(4):
                            which = 0 if c8 < 2 else 1
                            bc = rB[which][:, 0:ng].rearrange("p (o t) -> p o t", o=1).broadcast_to([128, 8, ng])
                            I("dve", "tensor_tensor", [("oT", c8), ("rB", which)], [("oT", c8)], out=oT[:, c8 * 8:(c8 + 1) * 8, 0:ng],
                              in0=oT[:, c8 * 8:(c8 + 1) * 8, 0:ng], in1=bc, op=ALU.mult)
                    else:
                        for j, tt in enumerate(tiles):
                            nt = nts[j]
                            r0 = tt * 128
                            DMA("sp", pst[0:nt, :], p_in[r0:r0 + nt, :], [], ["pst"])
                            I("dve", "tensor_copy", ["pst"], ["pbf"], out=pbf[0:nt, :], in_=pst[0:nt, :])
                            bk = 6 + j % 2
                            pvv = psb(bk).rearrange("p (j n) -> p j n", j=8)
                            for c in range(2):
                                I("pe", "transpose", ["pbf", "ident_bf"], [("ps", bk)], out=pvv[:, c, 0:nt],
                                  in_=pbf[0:nt, c * 128:(c + 1) * 128], identity=ident_bf[0:nt, 0:nt])
                            copy_on(evac_eng(), pT[:, :, j * 128:j * 128 + nt], pvv[:, 0:2, 0:nt], [("ps", bk)], [("pT", j)])

                def e_matmuls(tiles, ns):
                    t0, nts, ng = geom(tiles)
                    S.tag = "e_mm"
                    for j, tt in enumerate(tiles):
                        nt = nts[j]
                        eb = 6 + c3["e"] % 2
                        c3["e"] += 1
                        for c in range(2):
                            I("pe", "matmul", [("pT", j), ("wp", ns)], [("ps", eb)], PS[eb][0:nt, :], pT[:, c, j * 128:j * 128 + nt],
                              wp[ns][:, c, :], start=(c == 0), stop=(c == 1))
                        copy_on("act", e_sb[0:nt, j, :], PS[eb][0:nt, :], [("ps", eb)], [("e_sb", j)])

                def evac(tiles, cb, ns):
                    t0, nts, ng = geom(tiles)
                    if first:
                        for j, tt in enumerate(tiles):
                            copy_on("act" if j % 2 == 0 else "dve", msb[0:nts[j], j, cb * 512:(cb + 1) * 512], PS[j][0:nts[j], :],
                                    [("ps", j)], [("msb", j, cb)])
                    for j, tt in enumerate(tiles):
                        nt = nts[j]
                        cs = slice(cb * 512, (cb + 1) * 512)
                        if first:
                            I("dve", "scalar_tensor_tensor", [("msb", j, cb)], ["junk", ("stt", j, cb)], out=junk[0:nt, :], in0=msb[0:nt, j, cs],
                              scalar=1.0, in1=msb[0:nt, j, cs], op0=ALU.mult, op1=ALU.mult, accum_out=stt[0:nt, j, cb:cb + 1])
                            I("dve", "tensor_tensor", [("msb", j, cb), ("nrm", ns)], [("msb", j, cb)], out=msb[0:nt, j, cs],
                              in0=msb[0:nt, j, cs], in1=nrmr[ns][0:nt, :], op=ALU.mult)
                        else:
                            es = c3["t"] % 2
                            c3["t"] += 1
                            I("act", "activation", [("ps", j)], [("sg", es)], out=sg[es][0:nt, :], in_=PS[j][0:nt, :], func=AF.Sigmoid)
                            I("dve", "tensor_tensor", [("e_sb", j), ("sg", es)], ["tmp"], out=tmp[es][0:nt, :], in0=e_sb[0:nt, j, :],
                              in1=sg[es][0:nt, :], op=ALU.mult)
                            I("dve", "scalar_tensor_tensor", ["tmp"], [("sg", es), ("stt", j, cb)], out=sg[es][0:nt, :], in0=tmp[es][0:nt, :],
                              scalar=1.0, in1=tmp[es][0:nt, :], op0=ALU.mult, op1=ALU.mult, accum_out=stt[0:nt, j, cb:cb + 1])
                            I("dve", "tensor_tensor", ["tmp", ("nrm", ns)], [("msb", j, cb)], out=msb[0:nt, j, cs],
                              in0=tmp[es][0:nt, :], in1=nrmr[ns][0:nt, :], op=ALU.mult)

                def stats(tiles):
                    t0, nts, ng = geom(tiles)
                    for j, tt in enumerate(tiles):
                        nt = nts[j]
                        sk = [("stt", j, cb) for cb in range(8)]
                        I("dve", "tensor_reduce", sk, [("stt8", j)], out=stt[0:nt, j, 8:9], in_=stt[0:nt, j, 0:8], axis=mybir.AxisListType.X,
                          op=ALU.add)
                        I("dve", "tensor_scalar", [("stt8", j)], [("stt9", j)], out=stt[0:nt, j, 9:10], in0=stt[0:nt, j, 8:9],
                          scalar1=1.0 / D, scalar2=EPS, op0=ALU.mult, op1=ALU.add)
                        I("act", "activation", [("stt9", j)], [("stt10", j)], out=stt[0:nt, j, 10:11], in_=stt[0:nt, j, 9:10], func=AF.Ln)
                        I("act", "activation", [("stt10", j)], [("rst", j)], out=rst[0:nt, j:j + 1], in_=stt[0:nt, j, 10:11],
                          func=AF.Exp, scale=-0.5)

                def epiA(tiles, pc):
                    t0, nts, ng = geom(tiles)
                    n = len(tiles)
                    base = c3["x"]
                    c3["x"] += n
                    pcs = slice(pc * 1024, (pc + 1) * 1024)

                    def load(j):
                        xs = (base + j) % 2
                        r0 = tiles[j] * 128
                        DMA("sp", xr[xs][0:nts[j], :], res_src[r0:r0 + nts[j], pcs], [], [("xr", xs)])

                    load(0)
                    if n > 1:
                        load(1)
                    for j, tt in enumerate(tiles):
                        nt = nts[j]
                        r0 = tt * 128
                        xs = (base + j) % 2
                        mk = [("msb", j, 2 * pc), ("msb", j, 2 * pc + 1)]
                        I("dve", "scalar_tensor_tensor", mk + [("rst", j), ("xr", xs)], mk, out=msb[0:nt, j, pcs], in0=msb[0:nt, j, pcs],
                          scalar=rst[0:nt, j:j + 1], in1=xr[xs][0:nt, :], op0=ALU.mult, op1=ALU.add)
                        DMA("sp", res_dst[r0:r0 + nt, pcs], msb[0:nt, j, pcs], mk, [])
                        if j + 2 < n:
                            load(j + 2)

                def epiB(tiles, pc):
                    if not first:
                        return
                    S.tag = "epiB"
                    t0, nts, ng = geom(tiles)
                    for j, tt in enumerate(tiles):
                        nt = nts[j]
                        r0 = tt * 128
                        xs = c3["t"] % 4
                        c3["t"] += 1
                        pcs = slice(pc * 1024, (pc + 1) * 1024)
                        mk = [("msb", j, 2 * pc), ("msb", j, 2 * pc + 1)]
                        copy_on("act", x1b[xs][0:nt, :], msb[0:nt, j, pcs], mk, [("x1b", xs)])
                        bk = 6 + xs % 2
                        pvv = psb(bk).rearrange("p (j n) -> p j n", j=8)
                        for c in range(8):
                            I("pe", "transpose", [("x1b", xs), "ident_bf"], [("ps", bk)], out=pvv[:, c, 0:nt],
                              in_=x1b[xs][0:nt, c * 128:(c + 1) * 128], identity=ident_bf[0:nt, 0:nt])
                        copy_on("act", x1Ts[xs % 2][:, :, 0:nt], pvv[:, :, 0:nt], [("ps", bk)], [("x1Ts", xs % 2)])
                        DMA("sp", x1T_d[pc * 8:(pc + 1) * 8, :, r0:r0 + nt].rearrange("c p t -> p c t"), x1Ts[xs % 2][:, :, 0:nt],
                            [("x1Ts", xs % 2)], [])

                prev = None
                load_group(GROUPS[0])
                for gi, tiles in enumerate(GROUPS):
                    t0, nts, ng = geom(tiles)
                    for cb in range(8):
                        ns = c3["n"] % 2
                        c3["n"] += 1
                        DMA("sp", nrmr[ns], nrm[cb * 512:(cb + 1) * 512].partition_broadcast(128), [], [("nrm", ns)])
                        if not first:
                            DMA("pool", wp[ns], w_proj[:, cb * 512:(cb + 1) * 512].rearrange("(c p) n -> p c n", p=128), [], [("wp", ns)])
                        if prev is not None:
                            if cb % 2 == 0:
                                epiB(prev, cb // 2)
                            elif cb // 2 + 1 < 4:
                                epiA(prev, cb // 2 + 1)
                        for fq in range(8):
                            slot = c3["w"] % NW3
                            c3["w"] += 1
                            DMA("pool", w3[slot], W[fq * 512:(fq + 1) * 512, cb * 512:(cb + 1) * 512].rearrange("(c p) n -> p c n", p=128),
                                [], [("w3", slot)])
                            S.tag = "mm.fq%d.%s" % (fq, "a" if first else "b")
                            order = [(f4, j) for f4 in range(4) for j in range(len(tiles))]
                            if fq == 0:
                                order = [(f4, j) for j in range(len(tiles)) for f4 in range(4)]
                            for f4, j in order:
                                fc = fq * 4 + f4
                                nt = nts[j]
                                I("pe", "matmul", [("w3", slot), ("oT", fc // 8)], [("ps", j)], PS[j][0:nt, :],
                                  oT[:, fc, j * 128:j * 128 + nt], w3[slot][:, f4, :], start=(fc == 0), stop=(fc == 31))
                            if (not first) and fq == 5:
                                e_matmuls(tiles, ns)
                        evac(tiles, cb, ns)
                    if gi + 1 < len(GROUPS):
                        load_group(GROUPS[gi + 1])
                    stats(tiles)
                    epiA(tiles, 0)
                    prev = tiles
                for pc in range(4):
                    epiB(prev, pc)
                    if pc + 1 < 4:
                        epiA(prev, pc + 1)
                S.barrier()

            stage3(True)
            stage3(False)

        S.barrier()
        nsig = S.emit()
    nc._pe_tags = S.pe_tags
    return nc, nsig


_CACHE = {}


def _shard_inputs(inputs, c):
    f = np.float32
    g = lambda k: np.asarray(inputs[k])
    m = {
        "x": np.concatenate([g("x_prompt")[c], g("x_sample")[c]], axis=0).astype(f, copy=False),
        "p": np.concatenate([g("p_prompt")[0, c], g("p_sample")[0, c]], axis=0).astype(f, copy=False),
        "ck": np.ascontiguousarray(g("cache_k")[0, c].reshape(PAST, 2048)),
        "cv": np.ascontiguousarray(g("cache_v")[0, c].reshape(PAST, 2048)),
        "sconv": np.ascontiguousarray(g("state_conv")[0, c]),
        "sssd": np.ascontiguousarray(g("state_ssd")[0, c].reshape(2048, 128)),
        "w_in": np.ascontiguousarray(g("w_in")[0]),
        "w_out": np.ascontiguousarray(g("w_out")[0]),
        "w_gate": np.ascontiguousarray(g("w_ple_gate")[0]),
        "w_proj": np.ascontiguousarray(g("w_ple_proj")[0]),
    }
    for k in ("norm_pre", "norm_post", "ple_norm", "conv_b", "sb_norm", "ssd_norm", "dt_bias", "a_log", "d_skip"):
        m[k] = np.ascontiguousarray(g(k)[0].reshape(-1))
    m["conv_w"] = np.ascontiguousarray(g("conv_w")[0])
    return m


def kernel(**inputs):
    if "nc" not in _CACHE:
        _CACHE["nc"] = build_program()[0]
    nc = _CACHE["nc"]
    in_maps = [_shard_inputs(inputs, c) for c in range(N_CORES)]
    res = run_bass_kernel_spmd(nc, in_maps, core_ids=list(range(N_CORES)))
    R = res.results
    f = np.float32
    y = np.stack([r["y"] for r in R])
    ko = np.stack([r["k_out"] for r in R])
    vo = np.stack([r["v_out"] for r in R])
    co = np.stack([r["conv_out"] for r in R])
    so = np.stack([r["ssd_out"] for r in R])
    y_prompt = np.ascontiguousarray(y[:, :TP]).astype(f, copy=False)
    y_sample = np.ascontiguousarray(y[:, TP:]).astype(f, copy=False)
    k_prompt = np.ascontiguousarray(ko[:, :TP]).reshape(1, N_CORES, TP, 16, 128)
    v_prompt = np.ascontiguousarray(vo[:, :TP]).reshape(1, N_CORES, TP, 16, 128)
    k_sample = np.ascontiguousarray(ko[:, TP:]).reshape(1, N_CORES, TS, 16, 128)
    v_sample = np.ascontiguousarray(vo[:, TP:]).reshape(1, N_CORES, TS, 16, 128)
    conv_prompt = np.ascontiguousarray(co[:, 0:3]).reshape(1, N_CORES, 3, D)
    conv_sample = np.ascontiguousarray(co[:, 3:6]).reshape(1, N_CORES, 3, D)
    ssd_prompt = np.ascontiguousarray(so[:, 0:2048]).reshape(1, N_CORES, 32, 64, 128)
    ssd_sample = np.ascontiguousarray(so[:, 2048:4096]).reshape(1, N_CORES, 32, 64, 128)
    return (y_prompt, y_sample, k_prompt, v_prompt, conv_prompt, ssd_prompt, k_sample, v_sample, conv_sample, ssd_sample)
```

```python
import contextlib
import numpy as np
import concourse.bass as bass
import concourse.mybir as mybir
from concourse.bass_utils import run_bass_kernel_spmd

F32 = mybir.dt.float32
BF16 = mybir.dt.bfloat16
AF = mybir.ActivationFunctionType
ALU = mybir.AluOpType

N_CORES = 8
TP, TS = 2048, 64
T = TP + TS
NT = 17
D = 4096
NIN = 14368
PAST = 1024
EPS = 1e-6
Q0, K0, V0, G0, XBC0, Z0, DT0 = 0, 2048, 4096, 6144, 8192, 12288, 14336
TBS = [(0, 512), (512, 512), (1024, 512), (1536, 512), (2048, 64)]
NEG = -30000.0

ENGINES = ["pe", "act", "dve", "pool", "sp"]
SEM_CHUNK = 30000


class _Op:
    __slots__ = ("eng", "seq", "fn", "waits", "slot", "use", "sig")

    def __init__(self):
        self.sig = None


class Sched:
    def __init__(self, nc, slots_per_queue=8):
        self.nc = nc
        self.streams = {e: [] for e in ENGINES}
        self.nseq = {e: 0 for e in ENGINES}
        self.known = {e: {} for e in ENGINES}
        self.lastw = {}
        self.readers = {}
        self.clocks = {}
        self.waited = set()
        self.nslots = slots_per_queue
        self.slot_rr = {q: 0 for q in ("sp", "act", "pool")}
        self.slot_use = {}
        self.opmap = {}

    def _deps(self, eng, reads, writes):
        deps = set()
        for k in reads:
            ev = self.lastw.get(k)
            if ev is not None:
                deps.add(ev)
            if isinstance(k, tuple) and k[0] == "ps":
                for src, seq in self.readers.get(k, {}).items():
                    if src != eng:
                        deps.add((src, seq))
        for k in writes:
            ev = self.lastw.get(k)
            if ev is not None:
                deps.add(ev)
            for src, seq in self.readers.get(k, {}).items():
                deps.add((src, seq))
        if eng == "pe":
            deps = {d for d in deps if d[0] != "pe"}
        return deps

    def _wait_list(self, eng, deps):
        kn = self.known[eng]
        waits = []
        for ev in sorted(deps, key=lambda d: (str(d[0]), d[1])):
            src, seq = ev
            if kn.get(src, 0) >= seq:
                continue
            waits.append(ev)
            kn[src] = seq
            for s2, v2 in self.clocks.get(ev, {}).items():
                if kn.get(s2, 0) < v2:
                    kn[s2] = v2
            if not isinstance(src, tuple):
                self.waited.add(ev)
        return waits

    def _mark(self, ev, reads, writes):
        src, seq = ev
        for k in reads:
            self.readers.setdefault(k, {})[src] = seq
        for k in writes:
            self.lastw[k] = ev
            self.readers[k] = {}

    def op(self, eng, fn, reads=(), writes=()):
        deps = self._deps(eng, reads, writes)
        o = _Op()
        o.eng = eng
        o.waits = self._wait_list(eng, deps)
        self.nseq[eng] += 1
        o.seq = self.nseq[eng]
        o.fn = fn
        o.slot = None
        ev = (eng, o.seq)
        self.clocks[ev] = dict(self.known[eng])
        self.streams[eng].append(o)
        self.opmap[ev] = o
        self._mark(ev, reads, writes)
        return ev

    def dma(self, q, fn, reads=(), writes=()):
        i = self.slot_rr[q]
        self.slot_rr[q] = (i + 1) % self.nslots
        slot = (q, i)
        use = self.slot_use.get(slot, 0)
        deps = self._deps(slot, reads, writes)
        if use > 0:
            deps.add((slot, use))
        o = _Op()
        o.eng = q
        o.waits = self._wait_list(q, deps)
        o.seq = None
        o.fn = fn
        o.slot = slot
        o.use = use + 1
        self.slot_use[slot] = use + 1
        ev = (slot, use + 1)
        self.clocks[ev] = dict(self.known[q])
        self.streams[q].append(o)
        self._mark(ev, reads, writes)
        return ev

    def fence(self, src_keys, dst_keys):
        evs = {}
        for k in src_keys:
            ev = self.lastw.get(k)
            if ev is not None:
                evs[ev[0]] = max(evs.get(ev[0], 0), ev[1])
            for s_, q_ in self.readers.get(k, {}).items():
                evs[s_] = max(evs.get(s_, 0), q_)
        for k in dst_keys:
            r = self.readers.setdefault(k, {})
            for s_, q_ in evs.items():
                r[s_] = max(r.get(s_, 0), q_)

    def barrier(self):
        deps = {(slot, use) for slot, use in self.slot_use.items()}
        for e in ENGINES:
            if self.nseq[e] > 0:
                deps.add((e, self.nseq[e]))
        for e in ENGINES:
            o = _Op()
            o.eng = e
            o.waits = self._wait_list(e, {d for d in deps if d[0] != e})
            o.seq = None
            o.fn = None
            o.slot = None
            self.streams[e].append(o)
        self.lastw = {}
        self.readers = {}

    def emit(self):
        nc = self.nc
        nsig = {}
        for e in ENGINES:
            c = 0
            for o in self.streams[e]:
                if o.seq is not None and (e, o.seq) in self.waited:
                    c += 1
                    o.sig = c
            nsig[e] = c
        with contextlib.ExitStack() as st:
            csem = {}
            for e in ENGINES:
                n = (nsig[e] + SEM_CHUNK - 1) // SEM_CHUNK
                csem[e] = [st.enter_context(nc.semaphore(f"c_{e}_{j}")) for j in range(n)]
            dsem = {}
            for slot in self.slot_use:
                dsem[slot] = st.enter_context(nc.semaphore(f"d_{slot[0]}_{slot[1]}"))

            def resolve(ev):
                src, seq = ev
                if isinstance(src, tuple):
                    return dsem[src], 16 * seq
                r = self.opmap[ev].sig
                return csem[src][(r - 1) // SEM_CHUNK], (r - 1) % SEM_CHUNK + 1

            def run(e, h):
                for o in self.streams[e]:
                    for ev in o.waits:
                        s, v = resolve(ev)
                        h.wait_ge(s, v)
                    if o.fn is None:
                        continue
                    ins = o.fn(h)
                    if o.slot is not None:
                        ins.then_inc(dsem[o.slot], 16)
                    elif o.sig is not None:
                        ins.then_inc(csem[e][(o.sig - 1) // SEM_CHUNK], 1)

            with nc.Block() as block:
                @block.tensor
                def _(h):
                    run("pe", h)

                @block.scalar
                def _(h):
                    run("act", h)

                @block.vector
                def _(h):
                    run("dve", h)

                @block.gpsimd
                def _(h):
                    run("pool", h)

                @block.sync
                def _(h):
                    run("sp", h)
        return nsig


class Arena:
    def __init__(self, tile_f32, nwords):
        self.t = tile_f32
        self.n = nwords * 4
        self.off = 0

    def mark(self):
        return self.off

    def reset(self, m):
        self.off = m

    def alloc(self, shape_free, dtype):
        esz = 4 if dtype == F32 else 2
        n = 1
        for s in shape_free:
            n *= s
        nbytes = (n * esz + 31) // 32 * 32
        assert self.off + nbytes <= self.n, f"arena overflow: {self.off}+{nbytes} > {self.n}"
        w0 = self.off // 4
        v = self.t[:, w0:w0 + nbytes // 4]
        self.off += nbytes
        if dtype != F32:
            v = v.bitcast(dtype)
        v = v[:, 0:n]
        if len(shape_free) == 2:
            v = v.rearrange("p (a b) -> p a b", a=shape_free[0])
        elif len(shape_free) == 3:
            v = v.rearrange("p (a b c) -> p a b c", a=shape_free[0], b=shape_free[1])
        return v


def build_program(phases=("p1", "p2a", "p2b", "p3"), dbg=False):
    nc = bass.Bass("TRN2", target_bir_lowering=False)

    def din(name, shape):
        return nc.dram_tensor(name, shape, F32, kind="ExternalInput").ap()

    def dout(name, shape):
        return nc.dram_tensor(name, shape, F32, kind="ExternalOutput").ap()

    x = din("x", [T, D])
    p_in = din("p", [T, 256])
    ck = din("ck", [PAST, 2048])
    cv = din("cv", [PAST, 2048])
    sconv = din("sconv", [3, D])
    sssd = din("sssd", [2048, 128])
    w_in = din("w_in", [D, NIN])
    w_out = din("w_out", [D, D])
    w_gate = din("w_gate", [D, D])
    w_proj = din("w_proj", [256, D])
    norm_pre = din("norm_pre", [D])
    norm_post = din("norm_post", [D])
    ple_norm = din("ple_norm", [D])
    conv_w = din("conv_w", [4, D])
    conv_b = din("conv_b", [D])
    sb_norm = din("sb_norm", [2048])
    ssd_norm = din("ssd_norm", [2048])
    dt_bias = din("dt_bias", [32])
    a_log = din("a_log", [32])
    d_skip = din("d_skip", [32])

    y = dout("y", [T, D])
    k_out = dout("k_out", [T, 2048])
    v_out = dout("v_out", [T, 2048])
    conv_out = dout("conv_out", [6, D])
    ssd_out = dout("ssd_out", [4096, 128])

    p3only = bool(dbg) and dbg.get("p3only", False)
    omix_d = nc.dram_tensor("omix_scr", [32, 128, T], BF16,
                            kind=("ExternalInput" if p3only else ("ExternalOutput" if dbg else "Internal"))).ap()
    ssq_in = din("ssq_in", [128, 2 * NT]) if p3only else None
    x1_d = nc.dram_tensor("x1_scr", [T, D], F32, kind="Internal").ap()
    x1T_d = nc.dram_tensor("x1T_scr", [32, 128, T], BF16, kind="Internal").ap()
    dbg_d = dout("dbg", [128, 4096]) if dbg else None

    S = Sched(nc)

    def row(ap):
        return ap.rearrange("(o n) -> o n", o=1)

    def I(eng, method, reads, writes, *args, **kw):
        return S.op(eng, lambda e: getattr(e, method)(*args, **kw), reads=reads, writes=writes)

    def DMA(q, out, in_, reads, writes):
        return S.dma(q, lambda e: e.dma_start(out=out, in_=in_), reads=reads, writes=writes)

    rr = {"ev": 0}

    def evac_eng():
        rr["ev"] += 1
        return "act" if rr["ev"] % 2 else "dve"

    def copy_on(eng, out, in_, reads, writes):
        if eng == "act":
            return I("act", "activation", reads, writes, out=out, in_=in_, func=AF.Copy)
        return I(eng, "tensor_copy", reads, writes, out=out, in_=in_)

    with contextlib.ExitStack() as st:
        NW = 52224
        arena_t = st.enter_context(nc.sbuf_tensor("arena", [128, NW], F32))
        A = Arena(arena_t, NW)
        PS = [st.enter_context(nc.psum_tensor(f"ps{i}", [128, 512], F32)) for i in range(8)]

        def psb(i):
            return PS[i][:].bitcast(BF16)

        ident_bf = A.alloc([128], BF16)
        ident_f = A.alloc([128], F32)
        negtri = A.alloc([128], BF16)
        negones = A.alloc([128], BF16)
        ones_bf = A.alloc([128], BF16)
        ones_f = A.alloc([128], F32)
        maskM = A.alloc([896], BF16)
        tri_f = A.alloc([128], F32)
        negmask2 = A.alloc([128], BF16)
        FV = A.alloc([32, 16], F32)
        ssq_sb = A.alloc([NT], F32)
        ssq_ssd = A.alloc([NT], F32)
        dtb_b = A.alloc([32], F32)
        alog_b = A.alloc([32], F32)
        dskp = A.alloc([16], F32)

        m0 = A.mark()
        zero_bf = A.alloc([896], BF16)
        I("pool", "memset", [], ["negones"], negones, -1.0)
        I("pool", "memset", [], ["ones_bf"], ones_bf, 1.0)
        I("pool", "memset", [], ["ones_f"], ones_f, 1.0)
        I("pool", "memset", [], ["zero_bf"], zero_bf, 0.0)
        I("pool", "memset", [], ["ssq_sb"], ssq_sb, 0.0)
        I("pool", "memset", [], ["ssq_ssd"], ssq_ssd, 0.0)
        I("pool", "affine_select", ["ones_bf"], ["ident_bf"], out=ident_bf, in_=ones_bf, pattern=[[-1, 128]],
          compare_op=ALU.is_equal, fill=0.0, base=0, channel_multiplier=1)
        I("pool", "affine_select", ["ones_f"], ["ident_f"], out=ident_f, in_=ones_f, pattern=[[-1, 128]],
          compare_op=ALU.is_equal, fill=0.0, base=0, channel_multiplier=1)
        I("pool", "affine_select", ["negones"], ["negtri"], out=negtri, in_=negones, pattern=[[-1, 128]],
          compare_op=ALU.is_ge, fill=0.0, base=0, channel_multiplier=1)
        I("pool", "affine_select", ["zero_bf"], ["maskM"], out=maskM, in_=zero_bf, pattern=[[1, 896]],
          compare_op=ALU.is_gt, fill=NEG, base=-384, channel_multiplier=-1)
        I("pool", "affine_select", ["ones_f"], ["tri_f"], out=tri_f, in_=ones_f, pattern=[[1, 128]],
          compare_op=ALU.is_ge, fill=0.0, base=0, channel_multiplier=-1)
        I("pool", "affine_select", ["zero_bf"], ["negmask2"], out=negmask2, in_=zero_bf[:, 0:128], pattern=[[1, 128]],
          compare_op=ALU.is_ge, fill=NEG, base=0, channel_multiplier=-1)
        CONSTS = ["ident_bf", "ident_f", "negtri", "negones", "ones_bf", "ones_f", "maskM", "tri_f", "negmask2"]

        fvs = A.alloc([D], F32)
        I("pool", "memset", [], ["fvs"], fvs[0:32, :], 0.0)
        DMA("sp", fvs[0:4, :], conv_w, [], ["fvs"])
        DMA("sp", fvs[4:5, :], row(conv_b), [], ["fvs"])
        DMA("sp", fvs[5:6, 0:2048], row(sb_norm), [], ["fvs"])
        DMA("sp", fvs[5:6, 2048:4096], row(ssd_norm), [], ["fvs"])
        DMA("sp", fvs[6:9, :], sconv, [], ["fvs"])
        DMA("sp", dtb_b, dt_bias.partition_broadcast(128), [], ["dtb_b"])
        DMA("sp", alog_b, a_log.partition_broadcast(128), [], ["alog_b"])
        for c in range(32):
            I("pe", "transpose", ["fvs", "ident_f"], [("ps", 7)], out=PS[7][:, c * 16:c * 16 + 16],
              in_=fvs[0:16, c * 128:(c + 1) * 128], identity=ident_f[0:16, 0:16])
        I("dve", "tensor_copy", [("ps", 7)], ["FV"], out=FV, in_=PS[7][:].rearrange("p (c r) -> p c r", c=32))
        S.barrier()
        A.reset(m0)

        mC = A.mark()
        hT = A.alloc([32, T], BF16)
        mP = A.mark()

        if "p1" in phases:
            npre_b = A.alloc([D], F32)
            xt = [A.alloc([D], F32) for _ in range(2)]
            hns = [A.alloc([D], BF16) for _ in range(2)]
            st1 = A.alloc([NT, 4], F32)
            DMA("sp", npre_b, norm_pre.partition_broadcast(128), [], ["npre_b"])
            def p1_load(tt):
                nt = 128 if tt < 16 else 64
                b = tt % 2
                DMA("sp", xt[b][0:nt, :], x[tt * 128:tt * 128 + nt, :], [], [("xt", b)])

            def p1_square(tt):
                nt = 128 if tt < 16 else 64
                b = tt % 2
                I("act", "activation", [("xt", b)], [("hn", b), ("st1", tt)], out=hns[b][0:nt, :], in_=xt[b][0:nt, :],
                  func=AF.Square, accum_out=st1[0:nt, tt, 0:1])

            def p1_stats(tt):
                nt = 128 if tt < 16 else 64
                I("dve", "tensor_scalar", [("st1", tt)], [("st1b", tt)], out=st1[0:nt, tt, 1:2], in0=st1[0:nt, tt, 0:1],
                  scalar1=1.0 / D, scalar2=EPS, op0=ALU.mult, op1=ALU.add)
                I("act", "activation", [("st1b", tt)], [("st1c", tt)], out=st1[0:nt, tt, 2:3], in_=st1[0:nt, tt, 1:2], func=AF.Ln)
                I("act", "activation", [("st1c", tt)], [("st1d", tt)], out=st1[0:nt, tt, 3:4], in_=st1[0:nt, tt, 2:3],
                  func=AF.Exp, scale=-0.5)

            def p1_main(tt):
                nt = 128 if tt < 16 else 64
                r0 = tt * 128
                b = tt % 2
                hn = hns[b]
                hk = ("hn", b)
                I("dve", "scalar_tensor_tensor", [("xt", b), ("st1d", tt), "npre_b"], [hk], out=hn[0:nt, :],
                  in0=xt[b][0:nt, :], scalar=st1[0:nt, tt, 3:4], in1=npre_b[0:nt, :], op0=ALU.mult, op1=ALU.mult)
                for c4 in range(4):
                    bk = (tt * 4 + c4) % 4
                    pv = psb(bk).rearrange("p (j n) -> p j n", j=8)
                    for j in range(8):
                        fc = c4 * 8 + j
                        I("pe", "transpose", [hk, "ident_bf"], [("ps", bk)], out=pv[:, j, 0:nt],
                          in_=hn[0:nt, fc * 128:(fc + 1) * 128], identity=ident_bf[0:nt, 0:nt])
                    copy_on(evac_eng(), hT[:, c4 * 8:(c4 + 1) * 8, r0:r0 + nt], pv[:, :, 0:nt], [("ps", bk)], [("hT", tt)])

            p1_load(0)
            p1_square(0)
            for tt in range(NT):
                if tt + 1 < NT:
                    p1_load(tt + 1)
                p1_stats(tt)
                if tt + 1 < NT:
                    p1_square(tt + 1)
                p1_main(tt)
            S.barrier()
            A.reset(mP)

        def hkeys(t0, n):
            return [("hT", tt) for tt in range(t0 // 128, (t0 + n + 127) // 128)]

        if "p2a" in phases:
            WSL = 2
            wt = [A.alloc([32, 128], BF16) for _ in range(WSL)]
            qTs = [A.alloc([T], BF16) for _ in range(2)]
            kT = A.alloc([T], BF16)
            gs = A.alloc([T], BF16)
            v_tok = A.alloc([NT, 128], BF16)
            kst = [A.alloc([4, 128], F32) for _ in range(2)]
            kb16 = [A.alloc([4, 128], BF16) for _ in range(2)]
            e_t = A.alloc([512], F32)
            og_t = e_t
            sp_t = [A.alloc([512], BF16) for _ in range(3)]
            racc2 = A.alloc([2, 512], BF16)
            racc = [racc2[:, 0, :], racc2[:, 1, :]]
            ckb = racc2.rearrange("p a b -> p (a b)").rearrange("p (j d) -> p j d", j=8)
            att_t = [A.alloc([512], BF16) for _ in range(2)]
            sq_t = A.alloc([512], BF16)
            om_t = [A.alloc([512], BF16) for _ in range(2)]
            vcb = A.alloc([8, 128], BF16)
            kTc = A.alloc([PAST], BF16)
            sp_s = A.alloc([64], BF16)
            I("pool", "memset", [], ["sp_s"], sp_s, 0.0)
            SCALE = 128.0 ** -0.5
            cnt = {"w": 0, "bank": 0, "st": 0}

            def load_w(col0):
                slot = cnt["w"] % WSL
                cnt["w"] += 1
                for part in range(4):
                    src = w_in[part * 1024:(part + 1) * 1024, col0:col0 + 128].rearrange("(c p) n -> p c n", p=128)
                    DMA("pool", wt[slot][:, part * 8:(part + 1) * 8, :], src, [], [("wt", slot, part)])
                return slot

            def proj_fm(col0, evac):
                slot = load_w(col0)
                for tbi, (t0, n) in enumerate(TBS):
                    bk = cnt["bank"] % 4
                    cnt["bank"] += 1
                    for fc in range(32):
                        I("pe", "matmul", [("wt", slot, fc // 8)] + hkeys(t0, n), [("ps", bk)], PS[bk][:, 0:n],
                          wt[slot][:, fc, :], hT[:, fc, t0:t0 + n], start=(fc == 0), stop=(fc == 31))
                    evac(bk, tbi, t0, n)

            def proj_tm(col0, hd, dst_out, is_k):
                slot = load_w(col0)
                kpend = []
                for j4 in range(5):
                    tiles = list(range(j4 * 4, min(j4 * 4 + 4, NT)))
                    bk = cnt["bank"] % 4
                    cnt["bank"] += 1
                    for j, tt in enumerate(tiles):
                        nt = 128 if tt < 16 else 64
                        for fc in range(32):
                            I("pe", "matmul", [("wt", slot, fc // 8), ("hT", tt)], [("ps", bk)],
                              PS[bk][0:nt, j * 128:(j + 1) * 128], hT[:, fc, tt * 128:tt * 128 + nt], wt[slot][:, fc, :],
                              start=(fc == 0), stop=(fc == 31))
                    while kpend:
                        kpend.pop(0)()
                    s = cnt["st"] % 2
                    cnt["st"] += 1
                    nj = len(tiles)
                    npart = 128 if tiles[0] < 16 else 64
                    copy_on(evac_eng(), kst[s][0:npart, 0:nj, :], PS[bk][0:npart, 0:nj * 128].rearrange("p (j d) -> p j d", j=nj),
                            [("ps", bk)], [("kst", s)])
                    r0 = tiles[0] * 128
                    if npart == 128:
                        dst = dst_out[r0:r0 + nj * 128, hd * 128:(hd + 1) * 128].rearrange("(j p) d -> p j d", p=128)
                        DMA("sp", dst, kst[s][:, 0:nj, :], [("kst", s)], [])
                    else:
                        DMA("sp", dst_out[r0:r0 + 64, hd * 128:(hd + 1) * 128], kst[s][0:64, 0, :], [("kst", s)], [])
                    if is_k:
                        I("pool", "tensor_copy", [("kst", s)], [("kb16", s)], out=kb16[s][0:npart, 0:nj, :], in_=kst[s][0:npart, 0:nj, :])

                        def tr(s=s, tiles=tiles, npart=npart, nj=nj, r0=r0, j4=j4):
                            pv = psb(4 + s).rearrange("p (j n) -> p j n", j=8)
                            for j, tt in enumerate(tiles):
                                I("pe", "transpose", [("kb16", s), "ident_bf"], [("ps", 4 + s)], out=pv[:, j, 0:npart],
                                  in_=kb16[s][0:npart, j, :], identity=ident_bf[0:npart, 0:npart])
                            if npart == 128:
                                copy_on(evac_eng(), kT[:, r0:r0 + nj * 128].rearrange("p (j n) -> p j n", j=nj), pv[:, 0:nj, :],
                                        [("ps", 4 + s)], [("kT", j4)])
                            else:
                                copy_on(evac_eng(), kT[:, r0:r0 + 64], pv[:, 0, 0:64], [("ps", 4 + s)], [("kT", j4)])
                        kpend.append(tr)
                    else:
                        I("pool", "tensor_copy", [("kst", s)], [("v_tok", j4)], out=v_tok[0:npart, tiles[0]:tiles[0] + nj, :],
                          in_=kst[s][0:npart, 0:nj, :])
                while kpend:
                    kpend.pop(0)()

            def attention(hd, qT, qp, bg=None):
                blocks = []
                for qb in range(4):
                    kbs = list(range(4 * qb + 3, -1, -1))
                    for i, kb in enumerate(kbs):
                        m = kb - 4 * qb
                        blocks.append(dict(q0=qb * 512, nq=512, k_ap=kT[:, kb * 128:(kb + 1) * 128], nk=128,
                                           v_ap=v_tok[:, kb, :], mask=(maskM[:, 384 - 128 * m:384 - 128 * m + 512] if m >= 0 else None),
                                           first=(i == 0), last=(i == len(kbs) - 1), qkey=("qT", qp, qb), kkey=("kT", kb // 4),
                                           vkey=("v_tok", kb // 4), ckeys=[], sps=None, obank=6, qbi=qb))
                blocks.append(dict(q0=TP, nq=64, k_ap=kT[:, TP:T], nk=64, v_ap=v_tok[0:64, 16, :], mask=maskM[0:64, 384:448],
                                   first=True, last=False, qkey=("qT", qp, 4), kkey=("kT", 4), vkey=("v_tok", 4), ckeys=[],
                                   sps=sp_s, obank=6, qbi=4))
                for i, kb in enumerate(range(7, -1, -1)):
                    blocks.append(dict(q0=TP, nq=64, k_ap=kTc[:, kb * 128:(kb + 1) * 128], nk=128, v_ap=vcb[:, kb, :], mask=None,
                                       first=False, last=(i == 7), qkey=("qT", qp, 4), kkey="kTc", vkey="vcb", ckeys=[], sps=None,
                                       obank=6, qbi=4))
                n = len(blocks)
                state = {"R": None, "Rkey": None, "ra": 0}

                def Zm(i):
                    b = blocks[i]
                    zb = i % 3
                    nk, nq = b["nk"], b["nq"]
                    I("pe", "matmul", [b["kkey"], b["qkey"]], [("ps", zb)], PS[zb][0:nk, 0:nq], b["k_ap"], qT[:, b["q0"]:b["q0"] + nq],
                      start=True, stop=False)
                    if b["mask"] is not None:
                        I("pe", "matmul", ["ident_bf", "maskM"], [("ps", zb)], PS[zb][0:nk, 0:nq], ident_bf[0:nk, 0:nk], b["mask"],
                          start=False, stop=False)
                    I("act", "activation", [("ps", zb)], ["e_t"], out=e_t[0:nk, 0:nq], in_=PS[zb][0:nk, 0:nq], func=AF.Exp)
                    if b["sps"] is not None:
                        sp_ap, spk = b["sps"], "sp_s"
                        b["sp_full"] = sp_ap[:, 0:nq]
                    else:
                        si = i % 3
                        sp_ap, spk = sp_t[si], ("sp", si)
                        b["sp_full"] = sp_ap[:, 0:nq]
                    b["sp"], b["spk"] = sp_ap, spk
                    I("act", "activation", ["e_t"], [spk], out=sp_ap[0:nk, 0:nq], in_=e_t[0:nk, 0:nq], func=AF.Ln, bias=1.0)
                    if b["first"]:
                        b["R"], b["Rk"] = None, None
                    else:
                        pb = blocks[i - 1]
                        if pb["first"]:
                            b["R"], b["Rk"] = pb["sp_full"], pb["spk"]
                        else:
                            ra = state["ra"] % 2
                            state["ra"] += 1
                            I("dve", "tensor_tensor", [pb["Rk"], pb["spk"]], [("racc", ra)], out=racc[ra][:, 0:nq],
                              in0=pb["R"], in1=pb["sp_full"], op=ALU.add)
                            b["R"], b["Rk"] = racc[ra][:, 0:nq], ("racc", ra)

                def Am(i):
                    b = blocks[i]
                    ab = i % 3
                    nk, nq = b["nk"], b["nq"]
                    I("pe", "matmul", ["negtri", b["spk"]], [("ps", ab)], PS[ab][0:nk, 0:nq], negtri[0:nk, 0:nk], b["sp"][0:nk, 0:nq],
                      start=False, stop=(b["R"] is None))
                    if b["R"] is not None:
                        I("pe", "matmul", ["negones", b["Rk"]], [("ps", ab)], PS[ab][0:nk, 0:nq], negones[:, 0:nk], b["R"],
                          start=False, stop=True)
                    ai = i % 2
                    I("act", "activation", [("ps", ab)], [("att", ai)], out=att_t[ai][0:nk, 0:nq], in_=PS[ab][0:nk, 0:nq], func=AF.Exp)

                def AVm(i):
                    b = blocks[i]
                    nk, nq = b["nk"], b["nq"]
                    ob = b["obank"]
                    ai = i % 2
                    I("pe", "matmul", [b["vkey"], ("att", ai)], [("ps", ob)], PS[ob][:, 0:nq], b["v_ap"][0:nk, :], att_t[ai][0:nk, 0:nq],
                      start=b["first"], stop=b["last"])
                    if b["last"]:
                        q0, qbi = b["q0"], b["qbi"]
                        I("dve", "tensor_tensor", [("ps", ob), ("gs", qbi)], ["e_t"], out=og_t[:, 0:nq], in0=PS[ob][:, 0:nq],
                          in1=gs[:, q0:q0 + nq], op=ALU.mult)
                        oi = qbi % 2
                        I("dve", "tensor_scalar", ["e_t", "FV"], [("om", oi)], out=om_t[oi][:, 0:nq], in0=og_t[:, 0:nq],
                          scalar1=FV[:, hd, 5:6], scalar2=None, op0=ALU.mult)
                        DMA("sp", omix_d[hd, :, q0:q0 + nq], om_t[oi][:, 0:nq], [("om", oi)], [("omix", hd, qbi)])
                        I("act", "activation", ["e_t"], ["sq_t"], out=sq_t[:, 0:nq], in_=og_t[:, 0:nq], func=AF.Square)
                        def ssq_mm(q0=q0, nq=nq):
                            for j in range((nq + 127) // 128):
                                tt = q0 // 128 + j
                                nt = min(128, nq - j * 128)
                                I("pe", "matmul", ["sq_t", "ones_bf"], [("ps", 7)], PS[7][0:nt, tt:tt + 1], sq_t[:, j * 128:j * 128 + nt],
                                  ones_bf[:, 0:1], start=True, stop=True)
                        pending.append(ssq_mm)

                pending = []
                for s in range(n + 2):
                    if pending and s % 2 == 0:
                        pending.pop(0)()
                    if s < n:
                        Zm(s)
                    if 0 <= s - 1 < n:
                        Am(s - 1)
                    if 0 <= s - 2 < n:
                        AVm(s - 2)
                    if bg is not None:
                        for _ in range(4):
                            next(bg, None)
                if bg is not None:
                    for _ in bg:
                        pass
                while pending:
                    pending.pop(0)()
                I("dve", "tensor_tensor", [("ps", 7), "ssq_sb"], ["ssq_sb"], out=ssq_sb[:, 0:16], in0=PS[7][:, 0:16], in1=ssq_sb[:, 0:16], op=ALU.add)
                I("dve", "tensor_tensor", [("ps", 7), "ssq_sb"], ["ssq_sb"], out=ssq_sb[0:64, 16:17], in0=PS[7][0:64, 16:17],
                  in1=ssq_sb[0:64, 16:17], op=ALU.add)

            nheads = 16 if not dbg else dbg.get("nheads", 16)

            def q_proj_gen(hd):
                qp = hd % 2
                slot = load_w(Q0 + hd * 128)
                for tbi, (t0, n) in enumerate(TBS):
                    bk = 4 + tbi % 2
                    for fc in range(32):
                        I("pe", "matmul", [("wt", slot, fc // 8)] + hkeys(t0, n), [("ps", bk)], PS[bk][:, 0:n],
                          wt[slot][:, fc, :], hT[:, fc, t0:t0 + n], start=(fc == 0), stop=(fc == 31))
                        yield
                    I("dve", "tensor_scalar", [("ps", bk)], [("qT", qp, tbi)], out=qTs[qp][:, t0:t0 + n], in0=PS[bk][:, 0:n],
                      scalar1=SCALE, scalar2=None, op0=ALU.mult)

            for _ in q_proj_gen(0):
                pass
            for hd in range(nheads):
                proj_tm(K0 + hd * 128, hd, k_out, True)
                proj_tm(V0 + hd * 128, hd, v_out, False)
                DMA("pool", ckb, ck[:, hd * 128:(hd + 1) * 128].rearrange("(j p) d -> p j d", p=128), [], [("racc", 0), ("racc", 1)])
                DMA("pool", vcb, cv[:, hd * 128:(hd + 1) * 128].rearrange("(j p) d -> p j d", p=128), [], ["vcb"])
                proj_fm(G0 + hd * 128, lambda bk, tbi, t0, n: I("act", "activation", [("ps", bk)], [("gs", tbi)], out=gs[:, t0:t0 + n],
                                                               in_=PS[bk][:, 0:n], func=AF.Silu))
                pv = psb(7).rearrange("p (j n) -> p j n", j=8)
                for j in range(8):
                    I("pe", "transpose", [("racc", 0), ("racc", 1), "ident_bf"], [("ps", 7)], out=pv[:, j, :], in_=ckb[:, j, :], identity=ident_bf)
                copy_on(evac_eng(), kTc.rearrange("p (j n) -> p j n", j=8), pv, [("ps", 7)], ["kTc"])
                attention(hd, qTs[hd % 2], hd % 2, q_proj_gen(hd + 1) if hd + 1 < nheads else None)
            if dbg:
                I("dve", "tensor_copy", ["ssq_sb"], ["dbgt"], out=og_t[:, 0:NT], in_=ssq_sb)
                DMA("sp", dbg_d[:, 0:NT], og_t[:, 0:NT], ["dbgt"], [])
            S.barrier()
            A.reset(mP)


        if "p2b" in phases:
            A.reset(mP)
            wt = [A.alloc([32, 128], BF16) for _ in range(2)]
            rA = A.alloc([2120], F32)
            wdt = rA[:, 0:512].bitcast(BF16).rearrange("p (c n) -> p c n", c=32)
            rB = A.alloc([2120], F32)
            xact = [A.alloc([T], BF16) for _ in range(2)]
            Bact = A.alloc([T], BF16)
            Cact = A.alloc([T], BF16)
            dt_tok = A.alloc([NT, 32], F32)
            a_b = A.alloc([32], F32)
            dsk_b = A.alloc([32], F32)
            hlast = A.alloc([32, 8], BF16)
            cst = [A.alloc([128], F32) for _ in range(1)]
            dAb4 = A.alloc([4, 128], F32)
            D4 = A.alloc([4, 128], F32)
            Wt4 = A.alloc([4, 128], BF16)
            xB_tok = A.alloc([3, 128], BF16)
            w4 = A.alloc([4], F32)
            xw = A.alloc([4, 64], BF16)
            dAx = A.alloc([2, 128], F32)
            t1 = A.alloc([128], F32)
            Hs = A.alloc([256], F32)
            Hbf = A.alloc([256], BF16)
            hst = [A.alloc([128], F32) for _ in range(1)] * 2
            zs_t = [A.alloc([512], BF16) for _ in range(1)]
            om_b = [A.alloc([512], BF16) for _ in range(1)]
            cntb = {"w": 0, "bank": 0, "z": 0, "c": 0, "h": 0}
            zpend = []

            def load_wb(col0):
                slot = cntb["w"] % 2
                cntb["w"] += 1
                for part in range(4):
                    src = w_in[part * 1024:(part + 1) * 1024, col0:col0 + 128].rearrange("(c p) n -> p c n", p=128)
                    DMA("pool", wt[slot][:, part * 8:(part + 1) * 8, :], src, [], [("wt", slot, part)])
                return slot

            def proj_fmb(col0, evac, extra=None):
                slot = load_wb(col0)
                for tbi, (t0, n) in enumerate(TBS):
                    bk = cntb["bank"] % 3
                    cntb["bank"] += 1
                    for fc in range(32):
                        I("pe", "matmul", [("wt", slot, fc // 8)] + hkeys(t0, n), [("ps", bk)], PS[bk][:, 0:n],
                          wt[slot][:, fc, :], hT[:, fc, t0:t0 + n], start=(fc == 0), stop=(fc == 31))
                    evac(bk, tbi, t0, n)
                if extra is not None:
                    extra(slot)

            DMA("pool", wdt, w_in[:, DT0:DT0 + 32].rearrange("(c p) n -> p c n", p=128), [], ["wdt", "rA"])
            DMA("sp", dsk_b, d_skip.partition_broadcast(128), [], ["dsk_b"])
            I("dve", "tensor_copy", ["dsk_b"], ["dskp"], out=dskp[0:64, :], in_=dsk_b[0:64, :].rearrange("p (j t) -> p j t", t=2)[:, :, 0])
            I("dve", "tensor_copy", ["dsk_b"], ["dskp"], out=dskp[64:128, :], in_=dsk_b[64:128, :].rearrange("p (j t) -> p j t", t=2)[:, :, 1])
            I("act", "activation", ["alog_b"], ["a_e"], out=a_b, in_=alog_b, func=AF.Exp)
            I("dve", "tensor_scalar", ["a_e"], ["a_b"], out=a_b, in0=a_b, scalar1=-1.0, scalar2=None, op0=ALU.mult)
            for tt in range(NT):
                nt = 128 if tt < 16 else 64
                bk, cc = (3, tt * 32) if tt < 16 else (4, 0)
                for fc in range(32):
                    I("pe", "matmul", ["wdt", "rA", ("hT", tt)], [("ps", bk)], PS[bk][0:nt, cc:cc + 32], hT[:, fc, tt * 128:tt * 128 + nt],
                      wdt[:, fc, :], start=(fc == 0), stop=(fc == 31))
            I("dve", "tensor_tensor", [("ps", 3), "dtb_b"], ["dt_tok"], out=dt_tok[:, 0:16, :],
              in0=PS[3][:].rearrange("p (t h) -> p t h", t=16),
              in1=dtb_b.rearrange("p (o h) -> p o h", o=1).broadcast_to([128, 16, 32]), op=ALU.add)
            I("dve", "tensor_tensor", [("ps", 4), "dtb_b"], ["dt_tok"], out=dt_tok[0:64, 16, :], in0=PS[4][0:64, 0:32], in1=dtb_b[0:64, :], op=ALU.add)
            I("act", "activation", ["dt_tok"], ["dt_e"], out=dt_tok[:, 0:16, :], in_=dt_tok[:, 0:16, :], func=AF.Exp)
            I("act", "activation", ["dt_tok", "dt_e"], ["dt_e"], out=dt_tok[0:64, 16, :], in_=dt_tok[0:64, 16, :], func=AF.Exp)
            I("act", "activation", ["dt_e"], ["dt_f"], out=dt_tok[:, 0:16, :], in_=dt_tok[:, 0:16, :], func=AF.Ln, bias=1.0)
            I("act", "activation", ["dt_e", "dt_f"], ["dt_f"], out=dt_tok[0:64, 16, :], in_=dt_tok[0:64, 16, :], func=AF.Ln, bias=1.0)
            I("dve", "tensor_copy", [("hT", 15)], ["hlast"], out=hlast[:, :, 0:3], in_=hT[:, :, TP - 3:TP])
            I("dve", "tensor_copy", [("hT", 16), "hlast"], ["hlast"], out=hlast[:, :, 3:6], in_=hT[:, :, T - 3:T])

            def proj_conv(col0, dst, dkey):
                ch = (col0 - XBC0) // 128
                raw, cvb = rA, rB

                def ev(bk, tbi, t0, n):
                    o0 = 3 + t0 if tbi < 4 else 2054
                    copy_on(evac_eng(), raw[:, o0:o0 + n], PS[bk][:, 0:n], [("ps", bk)], ["rA"])

                def extra(slot):
                    cs = 0
                    cntb["c"] += 1
                    for fc in range(32):
                        I("pe", "matmul", [("wt", slot, fc // 8), "hlast"], [("ps", 4)], PS[4][0:6, 128:256], hlast[:, fc, 0:6], wt[slot][:, fc, :],
                          start=(fc == 0), stop=(fc == 31))
                    copy_on("act", cst[cs][0:6, :], PS[4][0:6, 128:256], [("ps", 4)], [("cst", cs)])
                    DMA("sp", conv_out[0:6, ch * 128:(ch + 1) * 128], cst[cs][0:6, :], [("cst", cs)], [])

                proj_fmb(col0, ev, extra)
                I("dve", "memset", [], ["rA"], raw[:, 0:3], 0.0)
                I("dve", "tensor_copy", ["FV"], ["rA"], out=raw[:, 2051:2054], in_=FV[:, ch, 6:9])
                L = 2115
                I("dve", "tensor_scalar", ["rA", "FV"], ["rB"], out=cvb[:, 0:L], in0=raw[:, 0:L], scalar1=FV[:, ch, 0:1], scalar2=FV[:, ch, 4:5],
                  op0=ALU.mult, op1=ALU.add)
                for j in range(1, 4):
                    I("dve", "scalar_tensor_tensor", ["rA", "rB", "FV"], ["rB"], out=cvb[:, 0:L], in0=raw[:, j:j + L], scalar=FV[:, ch, j:j + 1],
                      in1=cvb[:, 0:L], op0=ALU.mult, op1=ALU.add)
                I("act", "activation", ["rB"], [dkey], out=dst[:, 0:TP], in_=cvb[:, 0:TP], func=AF.Silu)
                I("act", "activation", ["rB", dkey], [dkey], out=dst[:, TP:T], in_=cvb[:, 2051:2115], func=AF.Silu)

            w1f = wt[1].rearrange("p a b -> p (a b)")

            def carve(off, n, dtype):
                if dtype == F32:
                    return w1f[:, off // 2:off // 2 + 2 * n].bitcast(F32)
                return w1f[:, off // 2:off // 2 + n]
            TS2 = [dict(dAb4=dAb4, D4=D4, dAx=dAx, Wt4=Wt4, xB_tok=xB_tok, xw=xw, w4=w4),
                   dict(dAb4=carve(0, 512, F32).rearrange("p (a b) -> p a b", a=4), D4=carve(2048, 512, F32).rearrange("p (a b) -> p a b", a=4),
                        dAx=carve(4096, 256, F32).rearrange("p (a b) -> p a b", a=2), Wt4=carve(5120, 512, BF16).rearrange("p (a b) -> p a b", a=4),
                        xB_tok=carve(6144, 384, BF16).rearrange("p (a b) -> p a b", a=3), xw=carve(6912, 256, BF16).rearrange("p (a b) -> p a b", a=4),
                        w4=carve(7424, 4, F32))]
            dA_g = carve(7456, 68, F32).rearrange("p (c h) -> p c h", c=NT)
            negcum_g = carve(7744, 68, F32).rearrange("p (c h) -> p c h", c=NT)
            etot_g = A.alloc([NT, 4], F32)
            T1KEYS = [("dAb4", 1), ("dAx", 1), ("xB_tok", 1), ("w4", 1), ("xw", 1), "dA_g", "negcum_g"] + \
                     [("D4", 1, hq) for hq in range(4)] + [("Wt4", 1, hq) for hq in range(4)]
            W1KEYS = [("wt", 1, part) for part in range(4)]
            zflat = zs_t[0]
            oflat = om_b[0]
            ecxs = [[zflat[:, 0:256].bitcast(F32), zflat[:, 256:512].bitcast(F32)], [oflat[:, 0:256].bitcast(F32), oflat[:, 256:512].bitcast(F32)]]
            EKEYS = [("ecx", p_, r_) for p_ in range(2) for r_ in range(2)]
            ZKEYS = [("zs", 0), ("omb", 0)]

            def group_pre(g):
                hs = slice(4 * g, 4 * g + 4)
                I("dve", "tensor_tensor", ["dt_f", "a_b"], ["dA_g"], out=dA_g[:, 0:16, :], in0=dt_tok[:, 0:16, hs],
                  in1=a_b[:, hs].rearrange("p (o h) -> p o h", o=1).broadcast_to([128, 16, 4]), op=ALU.mult)
                I("dve", "tensor_tensor", ["dt_f", "a_b", "dA_g"], ["dA_g"], out=dA_g[0:64, 16, :], in0=dt_tok[0:64, 16, hs], in1=a_b[0:64, hs], op=ALU.mult)
                for c in range(NT):
                    nt = 128 if c < 16 else 64
                    I("pe", "matmul", ["dA_g", "tri_f"], [("ps", 3)], PS[3][0:nt, c * 4:c * 4 + 4], tri_f[0:nt, 0:nt], dA_g[0:nt, c, :], start=True, stop=True)
                    I("pe", "matmul", ["dA_g", "ones_f"], [("ps", 3)], PS[3][:, 128 + c * 4:128 + c * 4 + 4], ones_f[0:nt, :], dA_g[0:nt, c, :],
                      start=True, stop=True)
                I("dve", "tensor_scalar", [("ps", 3)], ["negcum_g"], out=negcum_g[:, 0:16, :], in0=PS[3][:, 0:64].rearrange("p (c h) -> p c h", c=16),
                  scalar1=-1.0, scalar2=None, op0=ALU.mult)
                I("dve", "tensor_scalar", [("ps", 3), "negcum_g"], ["negcum_g"], out=negcum_g[0:64, 16, :], in0=PS[3][0:64, 64:68], scalar1=-1.0, scalar2=None,
                  op0=ALU.mult)
                I("act", "activation", [("ps", 3)], ["etot_g"], out=etot_g, in_=PS[3][:, 128:196].rearrange("p (c h) -> p c h", c=NT), func=AF.Exp)

            def geo(c):
                nt = 128 if c < 16 else 64
                par = c % 2
                banks = (5, 4, 7, 3) if par == 0 else (6, 0, 1, 2)
                return nt, c * 128, par, TS2[par], banks

            def u1a(g, c, inter):
                nt, t0, par, tt_, _ = geo(c)
                dA4 = dA_g[0:nt, c, :]
                I("dve", "tensor_copy", ["dA_g"], [("dAb4", par)], out=tt_["dAb4"][0:nt, :, 0:nt],
                  in_=dA4.rearrange("p (h o) -> p h o", o=1).broadcast_to([nt, 4, nt]))
                if inter:
                    I("dve", "tensor_copy", ["dA_g"], [("dAx", par)], out=tt_["dAx"][0:nt, :, :].rearrange("p a (b q) -> p (a b) q", q=64),
                      in_=dA4.rearrange("p (h o) -> p h o", o=1).broadcast_to([nt, 4, 64]))

            def u1b(g, c, inter):
                nt, t0, par, tt_, (pb, b1, yb, tb) = geo(c)
                dAb4, D4, dAx, Wt4, xB_tok, xw, w4 = tt_["dAb4"], tt_["D4"], tt_["dAx"], tt_["Wt4"], tt_["xB_tok"], tt_["xw"], tt_["w4"]
                kdAb, kdAx, kxB, kw4, kxw = ("dAb4", par), ("dAx", par), ("xB_tok", par), ("w4", par), ("xw", par)
                hs = slice(4 * g, 4 * g + 4)
                I("pe", "matmul", ["Bact", "Cact"], [("ps", b1)], PS[b1][0:nt, 0:nt], Bact[:, t0:t0 + nt], Cact[:, t0:t0 + nt], start=True, stop=True)
                for hq in range(4):
                    I("pe", "matmul", [kdAb, "tri_f"], [("ps", pb)], PS[pb][0:nt, hq * 128:hq * 128 + nt], dAb4[0:nt, hq, 0:nt], tri_f[0:nt, 0:nt],
                      start=True, stop=False)
                    I("pe", "matmul", ["ident_bf", "negmask2"], [("ps", pb)], PS[pb][0:nt, hq * 128:hq * 128 + nt], ident_bf[0:nt, 0:nt],
                      negmask2[0:nt, 0:nt], start=False, stop=True)
                pv2 = psb(tb).rearrange("p (j n) -> p j n", j=8)
                for pr in range(2):
                    I("pe", "transpose", [("xact", pr), "ident_bf"], [("ps", tb)], out=pv2[0:nt, pr, :], in_=xact[pr][:, t0:t0 + nt], identity=ident_bf)
                I("pe", "transpose", ["Bact", "ident_bf"], [("ps", tb)], out=pv2[0:nt, 2, :], in_=Bact[:, t0:t0 + nt], identity=ident_bf)
                if inter:
                    for pr in range(2):
                        I("pe", "matmul", [kdAx, "tri_f"], [("ps", b1)], PS[b1][:, 128 + pr * 128:128 + pr * 128 + nt],
                          dAx[0:nt, pr, :], tri_f[0:nt, 0:nt], start=True, stop=True)
                for hq in range(4):
                    I("act", "activation", [("ps", pb), "negcum_g"], [("D4", par, hq)], out=D4[0:nt, hq, 0:nt], in_=PS[pb][0:nt, hq * 128:hq * 128 + nt],
                      func=AF.Exp, bias=negcum_g[0:nt, c, hq:hq + 1], scale=1.0)
                    I("dve", "scalar_tensor_tensor", [("D4", par, hq), "dt_f", ("ps", b1)], [("Wt4", par, hq)], out=Wt4[0:nt, hq, 0:nt],
                      in0=D4[0:nt, hq, 0:nt], scalar=dt_tok[0:nt, c, 4 * g + hq:4 * g + hq + 1], in1=PS[b1][0:nt, 0:nt], op0=ALU.mult, op1=ALU.mult)
                copy_on("act", xB_tok[0:nt, :, :], pv2[0:nt, 0:3, :], [("ps", tb)], [kxB])
                if inter:
                    for pr in range(2):
                        I("act", "activation", [("ps", b1)], [("ecx", par, pr)], out=ecxs[par][pr][:, 0:nt],
                          in_=PS[b1][:, 128 + pr * 128:128 + pr * 128 + nt], func=AF.Exp)
                D4k = [("D4", par, hq) for hq in range(4)]
                I("dve", "tensor_tensor", D4k + ["dt_f"], [kw4], out=w4[0:nt, :], in0=D4[0:nt, :, nt - 1], in1=dt_tok[0:nt, c, hs], op=ALU.mult)
                I("dve", "tensor_tensor", [kxB, kw4], [kxw], out=xw[0:nt, :, :],
                  in0=xB_tok[0:nt, 0:2, :].rearrange("p a (b q) -> p (a b) q", q=64),
                  in1=w4[0:nt, :].rearrange("p (h o) -> p h o", o=1).broadcast_to([nt, 4, 64]), op=ALU.mult)

            def u2(g, c, first, inter):
                nt, t0, par, tt_, (pb, b1, yb, tb) = geo(c)
                Wt4, xB_tok, xw = tt_["Wt4"], tt_["xB_tok"], tt_["xw"]
                kxB, kxw = ("xB_tok", par), ("xw", par)
                yall = [rA, rB]
                ykey = ["rA", "rB"]
                for pr in range(2):
                    for hh in range(2):
                        I("pe", "matmul", [kxB, ("Wt4", par, 2 * pr + hh)], [("ps", yb)], PS[yb][hh * 64:(hh + 1) * 64, pr * 128:pr * 128 + nt],
                          xB_tok[0:nt, pr, hh * 64:(hh + 1) * 64], Wt4[0:nt, 2 * pr + hh, 0:nt], start=True, stop=True)
                    if inter:
                        I("pe", "matmul", ["Hbf", "Cact"], [("ps", yb)], PS[yb][:, 256 + pr * 128:256 + pr * 128 + nt], Hbf[:, pr * 128:(pr + 1) * 128],
                          Cact[:, t0:t0 + nt], start=True, stop=True)
                I("pe", "matmul", [kxB, kxw], [("ps", tb)], PS[tb][:, 256:512], xB_tok[0:nt, 2, :], xw[0:nt, :, :].rearrange("p h q -> p (h q)"),
                  start=True, stop=True)
                if first:
                    I("dve", "tensor_copy", [("ps", tb)], ["Hs"], out=Hs, in_=PS[tb][:, 256:512])
                else:
                    I("dve", "tensor_tensor", ["Hs", "etot_g"], ["Hs"], out=Hs.rearrange("p (h q) -> p h q", q=64), in0=Hs.rearrange("p (h q) -> p h q", q=64),
                      in1=etot_g[:, c, :].rearrange("p (h o) -> p h o", o=1).broadcast_to([128, 4, 64]), op=ALU.mult)
                    I("dve", "tensor_tensor", ["Hs", ("ps", tb)], ["Hs"], out=Hs, in0=PS[tb][:, 256:512], in1=Hs, op=ALU.add)
                for pr in range(2):
                    if inter:
                        I("dve", "tensor_tensor", [("ps", yb), ("ecx", par, pr)], ["t1"], out=t1[:, 0:nt], in0=PS[yb][:, 256 + pr * 128:256 + pr * 128 + nt],
                          in1=ecxs[par][pr][:, 0:nt], op=ALU.mult)
                        I("dve", "tensor_tensor", [("ps", yb), "t1"], [ykey[pr]], out=yall[pr][:, t0:t0 + nt], in0=PS[yb][:, pr * 128:pr * 128 + nt],
                          in1=t1[:, 0:nt], op=ALU.add)
                    else:
                        I("dve", "tensor_copy", [("ps", yb)], [ykey[pr]], out=yall[pr][:, t0:t0 + nt], in_=PS[yb][:, pr * 128:pr * 128 + nt])

            def hbf_update():
                I("act", "activation", ["Hs"], ["Hbf"], out=Hbf, in_=Hs, func=AF.Copy)

            def state_out(g, row0):
                for pr in range(2):
                    hsx = cntb["h"] % 2
                    cntb["h"] += 1
                    I("pe", "transpose", ["Hs", "ident_f"], [("ps", 3)], out=PS[3][:, 0:128], in_=Hs[:, pr * 128:(pr + 1) * 128], identity=ident_f)
                    copy_on("act", hst[hsx], PS[3][:, 0:128], [("ps", 3)], ["hst"])
                    DMA("sp", ssd_out[row0 + g * 256 + pr * 128:row0 + g * 256 + (pr + 1) * 128, :], hst[hsx], ["hst"], [])

            def state_in(g):
                for pr in range(2):
                    hsx = cntb["h"] % 2
                    cntb["h"] += 1
                    DMA("sp", hst[hsx], sssd[g * 256 + pr * 128:g * 256 + (pr + 1) * 128, :], [], ["hst"])
                    I("pe", "transpose", ["hst", "ident_f"], [("ps", 3)], out=PS[3][:, 0:128], in_=hst[hsx], identity=ident_f)
                    I("dve", "tensor_copy", [("ps", 3)], ["Hs"], out=Hs[:, pr * 128:(pr + 1) * 128], in_=PS[3][:, 0:128])
                I("act", "activation", ["Hs"], ["Hbf"], out=Hbf, in_=Hs, func=AF.Copy)

            ngroups = 8 if not dbg else dbg.get("ngroups", 8)
            for g in range(ngroups):
                proj_conv(XBC0 + g * 256, xact[0], ("xact", 0))
                proj_conv(XBC0 + g * 256 + 128, xact[1], ("xact", 1))
                proj_conv(XBC0 + 2048 + g * 128, Bact, "Bact")
                proj_conv(XBC0 + 3072 + g * 128, Cact, "Cact")
                S.fence(W1KEYS, T1KEYS)
                S.fence(ZKEYS, EKEYS)
                group_pre(g)
                u1a(g, 0, False)
                for st_ in range(18):
                    if st_ + 1 < 17:
                        u1a(g, st_ + 1, True)
                    if st_ < 17:
                        u1b(g, st_, inter=(st_ > 0))
                    if st_ >= 1:
                        c = st_ - 1
                        if c == 16:
                            state_out(g, 0)
                            state_in(g)
                        u2(g, c, first=(c == 0), inter=(c > 0))
                        hbf_update()
                state_out(g, 2048)
                S.fence(T1KEYS, W1KEYS)
                S.fence(EKEYS, ZKEYS)
                yall = [rA, rB]
                ykey = ["rA", "rB"]
                for pr in range(2):
                    fcx = 16 + 2 * g + pr
                    I("dve", "scalar_tensor_tensor", [("xact", pr), "dskp", ykey[pr]], [ykey[pr]], out=yall[pr][:, 0:T], in0=xact[pr][:, 0:T],
                      scalar=dskp[:, 2 * g + pr:2 * g + pr + 1], in1=yall[pr][:, 0:T], op0=ALU.mult, op1=ALU.add)

                    def evz(bk, tbi, t0, n, pr=pr, fcx=fcx):
                        zi = 0
                        while zpend:
                            zpend.pop(0)()
                        cntb["z"] += 1
                        I("act", "activation", [("ps", bk)], [("zs", zi)], out=zs_t[zi][:, 0:n], in_=PS[bk][:, 0:n], func=AF.Silu)
                        I("dve", "tensor_tensor", [ykey[pr], ("zs", zi)], [ykey[pr]], out=yall[pr][:, t0:t0 + n], in0=yall[pr][:, t0:t0 + n],
                          in1=zs_t[zi][:, 0:n], op=ALU.mult)
                        I("dve", "tensor_scalar", [ykey[pr], "FV"], [("omb", zi)], out=om_b[zi][:, 0:n], in0=yall[pr][:, t0:t0 + n],
                          scalar1=FV[:, fcx, 5:6], scalar2=None, op0=ALU.mult)
                        DMA("sp", omix_d[fcx, :, t0:t0 + n], om_b[zi][:, 0:n], [("omb", zi)], [])
                        I("act", "activation", [ykey[pr]], [("zs", zi)], out=zs_t[zi][:, 0:n], in_=yall[pr][:, t0:t0 + n], func=AF.Square)
                        def ssq_mm(t0=t0, n=n, zi=zi):
                            for j in range((n + 127) // 128):
                                tt = t0 // 128 + j
                                nt = min(128, n - j * 128)
                                I("pe", "matmul", [("zs", zi), "ones_bf"], [("ps", 5)], PS[5][0:nt, tt:tt + 1], zs_t[zi][:, j * 128:j * 128 + nt],
                                  ones_bf[:, 0:1], start=True, stop=True)
                        zpend.append(ssq_mm)

                    proj_fmb(Z0 + g * 256 + pr * 128, evz)
                    while zpend:
                        zpend.pop(0)()
                    I("dve", "tensor_tensor", [("ps", 5), "ssq_ssd"], ["ssq_ssd"], out=ssq_ssd[:, 0:16], in0=PS[5][:, 0:16], in1=ssq_ssd[:, 0:16], op=ALU.add)
                    I("dve", "tensor_tensor", [("ps", 5), "ssq_ssd"], ["ssq_ssd"], out=ssq_ssd[0:64, 16:17], in0=PS[5][0:64, 16:17],
                      in1=ssq_ssd[0:64, 16:17], op=ALU.add)
            if dbg:
                I("dve", "tensor_copy", ["ssq_ssd"], ["dbgt"], out=t1[:, 0:NT], in_=ssq_ssd)
                DMA("sp", dbg_d[:, 32:32 + NT], t1[:, 0:NT], ["dbgt"], [])
            S.barrier()
            A.reset(mP)

        if "p3" in phases:
            A.reset(mC)
            if p3only:
                DMA("sp", ssq_sb, ssq_in[:, 0:NT], [], ["ssq_sb"])
                DMA("sp", ssq_ssd, ssq_in[:, NT:2 * NT], [], ["ssq_ssd"])
            GROUPS = [list(range(0, 6)), list(range(6, 12)), list(range(12, 17))]
            rs = A.alloc([4, NT], F32)
            for (src, key, a, b) in ((ssq_sb, "ssq_sb", 0, 1), (ssq_ssd, "ssq_ssd", 2, 3)):
                I("dve", "tensor_scalar", [key], [("rs", a)], out=rs[:, a, :], in0=src, scalar1=1.0 / 2048, scalar2=EPS,
                  op0=ALU.mult, op1=ALU.add)
                I("act", "activation", [("rs", a)], [("rsl", a)], out=rs[:, a, :], in_=rs[:, a, :], func=AF.Ln)
                I("act", "activation", [("rsl", a)], [("rs", b)], out=rs[:, b, :], in_=rs[:, a, :], func=AF.Exp, scale=-0.5)
            m3 = A.mark()

            def stage3(first):
                A.reset(m3)
                src_d = omix_d if first else x1T_d
                W = w_out if first else w_gate
                nrm = norm_post if first else ple_norm
                res_src = x if first else x1_d
                res_dst = x1_d if first else y
                oT = A.alloc([32, 768], BF16)
                msb = A.alloc([6, D], F32)
                NW3 = 4 if first else 3
                w3 = [A.alloc([4, 512], BF16) for _ in range(NW3)]
                nrmr = [A.alloc([512], F32) for _ in range(2)]
                xr = [A.alloc([1024], F32) for _ in range(2)]
                junk = A.alloc([512], F32) if first else None
                stt = A.alloc([6, 12], F32)
                rst = A.alloc([6], F32)
                if first:
                    rB = [A.alloc([768], F32) for _ in range(2)]
                    dg = A.alloc([2, 6, 128], F32)
                    x1b = [A.alloc([1024], BF16) for _ in range(3)]
                    x1Ts = [A.alloc([8, 128], BF16) for _ in range(2)]
                else:
                    pT = A.alloc([2, 768], BF16)
                    pst = A.alloc([256], F32)
                    pbf = A.alloc([256], BF16)
                    wp = [A.alloc([2, 512], BF16) for _ in range(2)]
                    sg = [A.alloc([512], F32) for _ in range(2)]
                    tmp = [A.alloc([512], F32) for _ in range(1)] * 2
                    e_sb = A.alloc([6, 512], F32)
                c3 = {"w": 0, "n": 0, "x": 0, "e": 0, "t": 0}

                def geom(tiles):
                    nts = [128 if t < 16 else 64 for t in tiles]
                    return tiles[0] * 128, nts, sum(nts)

                def rstd_bcast(tiles):
                    t0, nts, ng = geom(tiles)
                    if first:
                        nj = len(tiles)
                        for which, col in ((0, 1), (1, 3)):
                            I("dve", "tensor_tensor", ["ident_f", ("rs", col)], [("dg", which)], out=dg[:, which, 0:nj, :],
                              in0=ident_f.rearrange("p (o l) -> p o l", o=1).broadcast_to([128, nj, 128]),
                              in1=rs[:, col, tiles[0]:tiles[0] + nj].rearrange("p (j o) -> p j o", o=1).broadcast_to([128, nj, 128]), op=ALU.mult)
                        for which, col in ((0, 1), (1, 3)):
                            for j, tt in enumerate(tiles):
                                nt = nts[j]
                                bank = 6 + (j * 128) // 512
                                cc = (j * 128) % 512
                                I("pe", "matmul", [("dg", which), "ones_f"], [("ps", bank)], PS[bank][:, cc:cc + nt], ones_f[0:nt, :],
                                  dg[0:nt, which, j, 0:nt], start=True, stop=True)
                            n6 = min(ng, 512)
                            copy_on("act", rB[which][:, 0:n6], PS[6][:, 0:n6], [("ps", 6)], [("rB", which)])
                            if ng > 512:
                                copy_on("act", rB[which][:, 512:ng], PS[7][:, 0:ng - 512], [("ps", 7)], [("rB", which)])

                def load_group(tiles):
                    t0, nts, ng = geom(tiles)
                    for c8 in range(4):
                        DMA("sp", oT[:, c8 * 8:(c8 + 1) * 8, 0:ng],
                            src_d[c8 * 8:(c8 + 1) * 8, :, t0:t0 + ng].rearrange("c p t -> p c t"), [], [("oT", c8)])
                    if first:
                        for c8 in range(4):
                            which = 0 if c8 < 2 else 1
                            bc = rB[which][:, 0:ng].rearrange("p (o t) -> p o t", o=1).broadcast_to([128, 8, ng])
                            I("dve", "tensor_tensor", [("oT", c8), ("rB", which)], [("oT", c8)], out=oT[:, c8 * 8:(c8 + 1) * 8, 0:ng],
                              in0=oT[:, c8 * 8:(c8 + 1) * 8, 0:ng], in1=bc, op=ALU.mult)
                    else:
                        for j, tt in enumerate(tiles):
                            nt = nts[j]
                            r0 = tt * 128
                            DMA("sp", pst[0:nt, :], p_in[r0:r0 + nt, :], [], ["pst"])
                            I("dve", "tensor_copy", ["pst"], ["pbf"], out=pbf[0:nt, :], in_=pst[0:nt, :])
                            bk = 6 + j % 2
                            pvv = psb(bk).rearrange("p (j n) -> p j n", j=8)
                            for c in range(2):
                                I("pe", "transpose", ["pbf", "ident_bf"], [("ps", bk)], out=pvv[:, c, 0:nt],
                                  in_=pbf[0:nt, c * 128:(c + 1) * 128], identity=ident_bf[0:nt, 0:nt])
                            copy_on(evac_eng(), pT[:, :, j * 128:j * 128 + nt], pvv[:, 0:2, 0:nt], [("ps", bk)], [("pT", j)])

                def e_matmuls(tiles, ns):
                    t0, nts, ng = geom(tiles)
                    for j, tt in enumerate(tiles):
                        nt = nts[j]
                        eb = 6 + c3["e"] % 2
                        c3["e"] += 1
                        for c in range(2):
                            I("pe", "matmul", [("pT", j), ("wp", ns)], [("ps", eb)], PS[eb][0:nt, :], pT[:, c, j * 128:j * 128 + nt],
                              wp[ns][:, c, :], start=(c == 0), stop=(c == 1))
                        copy_on("act", e_sb[0:nt, j, :], PS[eb][0:nt, :], [("ps", eb)], [("e_sb", j)])

                def evac(tiles, cb, ns):
                    t0, nts, ng = geom(tiles)
                    if first:
                        for j, tt in enumerate(tiles):
                            copy_on("act" if j % 2 == 0 else "dve", msb[0:nts[j], j, cb * 512:(cb + 1) * 512], PS[j][0:nts[j], :],
                                    [("ps", j)], [("msb", j, cb)])
                    for j, tt in enumerate(tiles):
                        nt = nts[j]
                        cs = slice(cb * 512, (cb + 1) * 512)
                        if first:
                            I("dve", "scalar_tensor_tensor", [("msb", j, cb)], ["junk", ("stt", j, cb)], out=junk[0:nt, :], in0=msb[0:nt, j, cs],
                              scalar=1.0, in1=msb[0:nt, j, cs], op0=ALU.mult, op1=ALU.mult, accum_out=stt[0:nt, j, cb:cb + 1])
                            I("dve", "tensor_tensor", [("msb", j, cb), ("nrm", ns)], [("msb", j, cb)], out=msb[0:nt, j, cs],
                              in0=msb[0:nt, j, cs], in1=nrmr[ns][0:nt, :], op=ALU.mult)
                        else:
                            es = c3["t"] % 2
                            c3["t"] += 1
                            I("act", "activation", [("ps", j)], [("sg", es)], out=sg[es][0:nt, :], in_=PS[j][0:nt, :], func=AF.Sigmoid)
                            I("dve", "tensor_tensor", [("e_sb", j), ("sg", es)], ["tmp"], out=tmp[es][0:nt, :], in0=e_sb[0:nt, j, :],
                              in1=sg[es][0:nt, :], op=ALU.mult)
                            I("dve", "scalar_tensor_tensor", ["tmp"], [("sg", es), ("stt", j, cb)], out=sg[es][0:nt, :], in0=tmp[es][0:nt, :],
                              scalar=1.0, in1=tmp[es][0:nt, :], op0=ALU.mult, op1=ALU.mult, accum_out=stt[0:nt, j, cb:cb + 1])
                            I("dve", "tensor_tensor", ["tmp", ("nrm", ns)], [("msb", j, cb)], out=msb[0:nt, j, cs],
                              in0=tmp[es][0:nt, :], in1=nrmr[ns][0:nt, :], op=ALU.mult)

                def stats(tiles):
                    t0, nts, ng = geom(tiles)
                    for j, tt in enumerate(tiles):
                        nt = nts[j]
                        sk = [("stt", j, cb) for cb in range(8)]
                        I("dve", "tensor_reduce", sk, [("stt8", j)], out=stt[0:nt, j, 8:9], in_=stt[0:nt, j, 0:8], axis=mybir.AxisListType.X,
                          op=ALU.add)
                        I("dve", "tensor_scalar", [("stt8", j)], [("stt9", j)], out=stt[0:nt, j, 9:10], in0=stt[0:nt, j, 8:9],
                          scalar1=1.0 / D, scalar2=EPS, op0=ALU.mult, op1=ALU.add)
                        I("act", "activation", [("stt9", j)], [("stt10", j)], out=stt[0:nt, j, 10:11], in_=stt[0:nt, j, 9:10], func=AF.Ln)
                        I("act", "activation", [("stt10", j)], [("rst", j)], out=rst[0:nt, j:j + 1], in_=stt[0:nt, j, 10:11],
                          func=AF.Exp, scale=-0.5)

                def epiA(tiles, pc):
                    t0, nts, ng = geom(tiles)
                    n = len(tiles)
                    base = c3["x"]
                    c3["x"] += n
                    pcs = slice(pc * 1024, (pc + 1) * 1024)

                    def load(j):
                        xs = (base + j) % 2
                        r0 = tiles[j] * 128
                        DMA("sp", xr[xs][0:nts[j], :], res_src[r0:r0 + nts[j], pcs], [], [("xr", xs)])

                    load(0)
                    if n > 1:
                        load(1)
                    for j, tt in enumerate(tiles):
                        nt = nts[j]
                        r0 = tt * 128
                        xs = (base + j) % 2
                        mk = [("msb", j, 2 * pc), ("msb", j, 2 * pc + 1)]
                        I("dve", "scalar_tensor_tensor", mk + [("rst", j), ("xr", xs)], mk, out=msb[0:nt, j, pcs], in0=msb[0:nt, j, pcs],
                          scalar=rst[0:nt, j:j + 1], in1=xr[xs][0:nt, :], op0=ALU.mult, op1=ALU.add)
                        DMA("sp", res_dst[r0:r0 + nt, pcs], msb[0:nt, j, pcs], mk, [])
                        if j + 2 < n:
                            load(j + 2)

                def epiB(tiles, pc):
                    if not first:
                        return
                    t0, nts, ng = geom(tiles)
                    for j, tt in enumerate(tiles):
                        nt = nts[j]
                        r0 = tt * 128
                        xs = c3["t"] % 3
                        c3["t"] += 1
                        pcs = slice(pc * 1024, (pc + 1) * 1024)
                        mk = [("msb", j, 2 * pc), ("msb", j, 2 * pc + 1)]
                        copy_on("act", x1b[xs][0:nt, :], msb[0:nt, j, pcs], mk, [("x1b", xs)])
                        bk = 6 + xs % 2
                        pvv = psb(bk).rearrange("p (j n) -> p j n", j=8)
                        for c in range(8):
                            I("pe", "transpose", [("x1b", xs), "ident_bf"], [("ps", bk)], out=pvv[:, c, 0:nt],
                              in_=x1b[xs][0:nt, c * 128:(c + 1) * 128], identity=ident_bf[0:nt, 0:nt])
                        copy_on("act", x1Ts[xs % 2][:, :, 0:nt], pvv[:, :, 0:nt], [("ps", bk)], [("x1Ts", xs % 2)])
                        DMA("sp", x1T_d[pc * 8:(pc + 1) * 8, :, r0:r0 + nt].rearrange("c p t -> p c t"), x1Ts[xs % 2][:, :, 0:nt],
                            [("x1Ts", xs % 2)], [])

                prev = None
                rstd_bcast(GROUPS[0])
                load_group(GROUPS[0])
                for gi, tiles in enumerate(GROUPS):
                    t0, nts, ng = geom(tiles)
                    for cb in range(8):
                        ns = c3["n"] % 2
                        c3["n"] += 1
                        DMA("sp", nrmr[ns], nrm[cb * 512:(cb + 1) * 512].partition_broadcast(128), [], [("nrm", ns)])
                        if not first:
                            DMA("pool", wp[ns], w_proj[:, cb * 512:(cb + 1) * 512].rearrange("(c p) n -> p c n", p=128), [], [("wp", ns)])
                        if prev is not None:
                            if cb % 2 == 0:
                                epiB(prev, cb // 2)
                            elif cb // 2 + 1 < 4:
                                epiA(prev, cb // 2 + 1)
                        for fq in range(8):
                            slot = c3["w"] % NW3
                            c3["w"] += 1
                            DMA("pool", w3[slot], W[fq * 512:(fq + 1) * 512, cb * 512:(cb + 1) * 512].rearrange("(c p) n -> p c n", p=128),
                                [], [("w3", slot)])
                            order = [(f4, j) for f4 in range(4) for j in range(len(tiles))]
                            if fq == 0:
                                order = [(f4, j) for j in range(len(tiles)) for f4 in range(4)]
                            for f4, j in order:
                                fc = fq * 4 + f4
                                nt = nts[j]
                                I("pe", "matmul", [("w3", slot), ("oT", fc // 8)], [("ps", j)], PS[j][0:nt, :],
                                  oT[:, fc, j * 128:j * 128 + nt], w3[slot][:, f4, :], start=(fc == 0), stop=(fc == 31))
                            if (not first) and fq == 5:
                                e_matmuls(tiles, ns)
                        evac(tiles, cb, ns)
                        if cb == 5 and gi + 1 < len(GROUPS):
                            rstd_bcast(GROUPS[gi + 1])
                    if gi + 1 < len(GROUPS):
                        load_group(GROUPS[gi + 1])
                    stats(tiles)
                    epiA(tiles, 0)
                    prev = tiles
                for pc in range(4):
                    epiB(prev, pc)
                    if pc + 1 < 4:
                        epiA(prev, pc + 1)
                S.barrier()

            stage3(True)
            stage3(False)

        S.barrier()
        nsig = S.emit()
    return nc, nsig


_CACHE = {}


def _shard_inputs(inputs, c):
    f = np.float32
    g = lambda k: np.asarray(inputs[k])
    m = {
        "x": np.concatenate([g("x_prompt")[c], g("x_sample")[c]], axis=0).astype(f, copy=False),
        "p": np.concatenate([g("p_prompt")[0, c], g("p_sample")[0, c]], axis=0).astype(f, copy=False),
        "ck": np.ascontiguousarray(g("cache_k")[0, c].reshape(PAST, 2048)),
        "cv": np.ascontiguousarray(g("cache_v")[0, c].reshape(PAST, 2048)),
        "sconv": np.ascontiguousarray(g("state_conv")[0, c]),
        "sssd": np.ascontiguousarray(g("state_ssd")[0, c].reshape(2048, 128)),
        "w_in": np.ascontiguousarray(g("w_in")[0]),
        "w_out": np.ascontiguousarray(g("w_out")[0]),
        "w_gate": np.ascontiguousarray(g("w_ple_gate")[0]),
        "w_proj": np.ascontiguousarray(g("w_ple_proj")[0]),
    }
    for k in ("norm_pre", "norm_post", "ple_norm", "conv_b", "sb_norm", "ssd_norm", "dt_bias", "a_log", "d_skip"):
        m[k] = np.ascontiguousarray(g(k)[0].reshape(-1))
    m["conv_w"] = np.ascontiguousarray(g("conv_w")[0])
    return m


def kernel(**inputs):
    if "nc" not in _CACHE:
        _CACHE["nc"] = build_program()[0]
    nc = _CACHE["nc"]
    in_maps = [_shard_inputs(inputs, c) for c in range(N_CORES)]
    res = run_bass_kernel_spmd(nc, in_maps, core_ids=list(range(N_CORES)))
    R = res.results
    f = np.float32
    y = np.stack([r["y"] for r in R])
    ko = np.stack([r["k_out"] for r in R])
    vo = np.stack([r["v_out"] for r in R])
    co = np.stack([r["conv_out"] for r in R])
    so = np.stack([r["ssd_out"] for r in R])
    y_prompt = np.ascontiguousarray(y[:, :TP]).astype(f, copy=False)
    y_sample = np.ascontiguousarray(y[:, TP:]).astype(f, copy=False)
    k_prompt = np.ascontiguousarray(ko[:, :TP]).reshape(1, N_CORES, TP, 16, 128)
    v_prompt = np.ascontiguousarray(vo[:, :TP]).reshape(1, N_CORES, TP, 16, 128)
    k_sample = np.ascontiguousarray(ko[:, TP:]).reshape(1, N_CORES, TS, 16, 128)
    v_sample = np.ascontiguousarray(vo[:, TP:]).reshape(1, N_CORES, TS, 16, 128)
    conv_prompt = np.ascontiguousarray(co[:, 0:3]).reshape(1, N_CORES, 3, D)
    conv_sample = np.ascontiguousarray(co[:, 3:6]).reshape(1, N_CORES, 3, D)
    ssd_prompt = np.ascontiguousarray(so[:, 0:2048]).reshape(1, N_CORES, 32, 64, 128)
    ssd_sample = np.ascontiguousarray(so[:, 2048:4096]).reshape(1, N_CORES, 32, 64, 128)
    return (y_prompt, y_sample, k_prompt, v_prompt, conv_prompt, ssd_prompt, k_sample, v_sample, conv_sample, ssd_sample)
```

```python
import contextlib
import numpy as np
import concourse.bass as bass
import concourse.mybir as mybir
from concourse.bass_utils import run_bass_kernel_spmd

F32 = mybir.dt.float32
BF16 = mybir.dt.bfloat16
AF = mybir.ActivationFunctionType
ALU = mybir.AluOpType

N_CORES = 8
TP, TS = 2048, 64
T = TP + TS
NT = 17
D = 4096
NIN = 14368
PAST = 1024
EPS = 1e-6
Q0, K0, V0, G0, XBC0, Z0, DT0 = 0, 2048, 4096, 6144, 8192, 12288, 14336
TBS = [(0, 512), (512, 512), (1024, 512), (1536, 512), (2048, 64)]
NEG = -30000.0

ENGINES = ["pe", "act", "dve", "pool", "sp"]
SEM_CHUNK = 30000


class _Op:
    __slots__ = ("eng", "seq", "fn", "waits", "slot", "use", "sig")

    def __init__(self):
        self.sig = None


class Sched:
    def __init__(self, nc, slots_per_queue=8):
        self.nc = nc
        self.streams = {e: [] for e in ENGINES}
        self.nseq = {e: 0 for e in ENGINES}
        self.known = {e: {} for e in ENGINES}
        self.lastw = {}
        self.readers = {}
        self.clocks = {}
        self.waited = set()
        self.nslots = slots_per_queue
        self.slot_rr = {q: 0 for q in ("sp", "act", "pool")}
        self.slot_use = {}
        self.opmap = {}

    def _deps(self, eng, reads, writes):
        deps = set()
        for k in reads:
            ev = self.lastw.get(k)
            if ev is not None:
                deps.add(ev)
            if isinstance(k, tuple) and k[0] == "ps":
                for src, seq in self.readers.get(k, {}).items():
                    if src != eng:
                        deps.add((src, seq))
        for k in writes:
            ev = self.lastw.get(k)
            if ev is not None:
                deps.add(ev)
            for src, seq in self.readers.get(k, {}).items():
                deps.add((src, seq))
        if eng == "pe":
            deps = {d for d in deps if d[0] != "pe"}
        return deps

    def _wait_list(self, eng, deps):
        kn = self.known[eng]
        waits = []
        for ev in sorted(deps, key=lambda d: (str(d[0]), d[1])):
            src, seq = ev
            if kn.get(src, 0) >= seq:
                continue
            waits.append(ev)
            kn[src] = seq
            for s2, v2 in self.clocks.get(ev, {}).items():
                if kn.get(s2, 0) < v2:
                    kn[s2] = v2
            if not isinstance(src, tuple):
                self.waited.add(ev)
        return waits

    def _mark(self, ev, reads, writes):
        src, seq = ev
        for k in reads:
            self.readers.setdefault(k, {})[src] = seq
        for k in writes:
            self.lastw[k] = ev
            self.readers[k] = {}

    def op(self, eng, fn, reads=(), writes=()):
        deps = self._deps(eng, reads, writes)
        o = _Op()
        o.eng = eng
        o.waits = self._wait_list(eng, deps)
        self.nseq[eng] += 1
        o.seq = self.nseq[eng]
        o.fn = fn
        o.slot = None
        ev = (eng, o.seq)
        self.clocks[ev] = dict(self.known[eng])
        self.streams[eng].append(o)
        self.opmap[ev] = o
        self._mark(ev, reads, writes)
        return ev

    def dma(self, q, fn, reads=(), writes=()):
        i = self.slot_rr[q]
        self.slot_rr[q] = (i + 1) % self.nslots
        slot = (q, i)
        use = self.slot_use.get(slot, 0)
        deps = self._deps(slot, reads, writes)
        if use > 0:
            deps.add((slot, use))
        o = _Op()
        o.eng = q
        o.waits = self._wait_list(q, deps)
        o.seq = None
        o.fn = fn
        o.slot = slot
        o.use = use + 1
        self.slot_use[slot] = use + 1
        ev = (slot, use + 1)
        self.clocks[ev] = dict(self.known[q])
        self.streams[q].append(o)
        self._mark(ev, reads, writes)
        return ev

    def fence(self, src_keys, dst_keys):
        evs = {}
        for k in src_keys:
            ev = self.lastw.get(k)
            if ev is not None:
                evs[ev[0]] = max(evs.get(ev[0], 0), ev[1])
            for s_, q_ in self.readers.get(k, {}).items():
                evs[s_] = max(evs.get(s_, 0), q_)
        for k in dst_keys:
            r = self.readers.setdefault(k, {})
            for s_, q_ in evs.items():
                r[s_] = max(r.get(s_, 0), q_)

    def barrier(self):
        deps = {(slot, use) for slot, use in self.slot_use.items()}
        for e in ENGINES:
            if self.nseq[e] > 0:
                deps.add((e, self.nseq[e]))
        for e in ENGINES:
            o = _Op()
            o.eng = e
            o.waits = self._wait_list(e, {d for d in deps if d[0] != e})
            o.seq = None
            o.fn = None
            o.slot = None
            self.streams[e].append(o)
        self.lastw = {}
        self.readers = {}

    def emit(self):
        nc = self.nc
        nsig = {}
        for e in ENGINES:
            c = 0
            for o in self.streams[e]:
                if o.seq is not None and (e, o.seq) in self.waited:
                    c += 1
                    o.sig = c
            nsig[e] = c
        with contextlib.ExitStack() as st:
            csem = {}
            for e in ENGINES:
                n = (nsig[e] + SEM_CHUNK - 1) // SEM_CHUNK
                csem[e] = [st.enter_context(nc.semaphore(f"c_{e}_{j}")) for j in range(n)]
            dsem = {}
            for slot in self.slot_use:
                dsem[slot] = st.enter_context(nc.semaphore(f"d_{slot[0]}_{slot[1]}"))

            def resolve(ev):
                src, seq = ev
                if isinstance(src, tuple):
                    return dsem[src], 16 * seq
                r = self.opmap[ev].sig
                return csem[src][(r - 1) // SEM_CHUNK], (r - 1) % SEM_CHUNK + 1

            def run(e, h):
                for o in self.streams[e]:
                    for ev in o.waits:
                        s, v = resolve(ev)
                        h.wait_ge(s, v)
                    if o.fn is None:
                        continue
                    ins = o.fn(h)
                    if o.slot is not None:
                        ins.then_inc(dsem[o.slot], 16)
                    elif o.sig is not None:
                        ins.then_inc(csem[e][(o.sig - 1) // SEM_CHUNK], 1)

            with nc.Block() as block:
                @block.tensor
                def _(h):
                    run("pe", h)

                @block.scalar
                def _(h):
                    run("act", h)

                @block.vector
                def _(h):
                    run("dve", h)

                @block.gpsimd
                def _(h):
                    run("pool", h)

                @block.sync
                def _(h):
                    run("sp", h)
        return nsig


class Arena:
    def __init__(self, tile_f32, nwords):
        self.t = tile_f32
        self.n = nwords * 4
        self.off = 0

    def mark(self):
        return self.off

    def reset(self, m):
        self.off = m

    def alloc(self, shape_free, dtype):
        esz = 4 if dtype == F32 else 2
        n = 1
        for s in shape_free:
            n *= s
        nbytes = (n * esz + 31) // 32 * 32
        assert self.off + nbytes <= self.n, f"arena overflow: {self.off}+{nbytes} > {self.n}"
        w0 = self.off // 4
        v = self.t[:, w0:w0 + nbytes // 4]
        self.off += nbytes
        if dtype != F32:
            v = v.bitcast(dtype)
        v = v[:, 0:n]
        if len(shape_free) == 2:
            v = v.rearrange("p (a b) -> p a b", a=shape_free[0])
        elif len(shape_free) == 3:
            v = v.rearrange("p (a b c) -> p a b c", a=shape_free[0], b=shape_free[1])
        return v


def build_program(phases=("p1", "p2a", "p2b", "p3"), dbg=False):
    nc = bass.Bass("TRN2", target_bir_lowering=False)

    def din(name, shape):
        return nc.dram_tensor(name, shape, F32, kind="ExternalInput").ap()

    def dout(name, shape):
        return nc.dram_tensor(name, shape, F32, kind="ExternalOutput").ap()

    x = din("x", [T, D])
    p_in = din("p", [T, 256])
    ck = din("ck", [PAST, 2048])
    cv = din("cv", [PAST, 2048])
    sconv = din("sconv", [3, D])
    sssd = din("sssd", [2048, 128])
    w_in = din("w_in", [D, NIN])
    w_out = din("w_out", [D, D])
    w_gate = din("w_gate", [D, D])
    w_proj = din("w_proj", [256, D])
    norm_pre = din("norm_pre", [D])
    norm_post = din("norm_post", [D])
    ple_norm = din("ple_norm", [D])
    conv_w = din("conv_w", [4, D])
    conv_b = din("conv_b", [D])
    sb_norm = din("sb_norm", [2048])
    ssd_norm = din("ssd_norm", [2048])
    dt_bias = din("dt_bias", [32])
    a_log = din("a_log", [32])
    d_skip = din("d_skip", [32])

    y = dout("y", [T, D])
    k_out = dout("k_out", [T, 2048])
    v_out = dout("v_out", [T, 2048])
    conv_out = dout("conv_out", [6, D])
    ssd_out = dout("ssd_out", [4096, 128])

    p3only = bool(dbg) and dbg.get("p3only", False)
    omix_d = nc.dram_tensor("omix_scr", [32, 128, T], BF16,
                            kind=("ExternalInput" if p3only else ("ExternalOutput" if dbg else "Internal"))).ap()
    ssq_in = din("ssq_in", [128, 2 * NT]) if p3only else None
    x1_d = nc.dram_tensor("x1_scr", [T, D], F32, kind="Internal").ap()
    x1T_d = nc.dram_tensor("x1T_scr", [32, 128, T], BF16, kind="Internal").ap()
    dbg_d = dout("dbg", [128, 4096]) if dbg else None

    S = Sched(nc)

    def row(ap):
        return ap.rearrange("(o n) -> o n", o=1)

    def I(eng, method, reads, writes, *args, **kw):
        return S.op(eng, lambda e: getattr(e, method)(*args, **kw), reads=reads, writes=writes)

    def DMA(q, out, in_, reads, writes):
        return S.dma(q, lambda e: e.dma_start(out=out, in_=in_), reads=reads, writes=writes)

    rr = {"ev": 0}

    def evac_eng():
        rr["ev"] += 1
        return "act" if rr["ev"] % 2 else "dve"

    def copy_on(eng, out, in_, reads, writes):
        if eng == "act":
            return I("act", "activation", reads, writes, out=out, in_=in_, func=AF.Copy)
        return I(eng, "tensor_copy", reads, writes, out=out, in_=in_)

    with contextlib.ExitStack() as st:
        NW = 52224
        arena_t = st.enter_context(nc.sbuf_tensor("arena", [128, NW], F32))
        A = Arena(arena_t, NW)
        PS = [st.enter_context(nc.psum_tensor(f"ps{i}", [128, 512], F32)) for i in range(8)]

        def psb(i):
            return PS[i][:].bitcast(BF16)

        ident_bf = A.alloc([128], BF16)
        ident_f = A.alloc([128], F32)
        negtri = A.alloc([128], BF16)
        negones = A.alloc([128], BF16)
        ones_bf = A.alloc([128], BF16)
        ones_f = A.alloc([128], F32)
        maskM = A.alloc([896], BF16)
        tri_f = A.alloc([128], F32)
        negmask2 = A.alloc([128], BF16)
        FV = A.alloc([32, 16], F32)
        ssq_sb = A.alloc([NT], F32)
        ssq_ssd = A.alloc([NT], F32)
        dtb_b = A.alloc([32], F32)
        alog_b = A.alloc([32], F32)
        dskp = A.alloc([16], F32)

        m0 = A.mark()
        zero_bf = A.alloc([896], BF16)
        I("pool", "memset", [], ["negones"], negones, -1.0)
        I("pool", "memset", [], ["ones_bf"], ones_bf, 1.0)
        I("pool", "memset", [], ["ones_f"], ones_f, 1.0)
        I("pool", "memset", [], ["zero_bf"], zero_bf, 0.0)
        I("pool", "memset", [], ["ssq_sb"], ssq_sb, 0.0)
        I("pool", "memset", [], ["ssq_ssd"], ssq_ssd, 0.0)
        I("pool", "affine_select", ["ones_bf"], ["ident_bf"], out=ident_bf, in_=ones_bf, pattern=[[-1, 128]],
          compare_op=ALU.is_equal, fill=0.0, base=0, channel_multiplier=1)
        I("pool", "affine_select", ["ones_f"], ["ident_f"], out=ident_f, in_=ones_f, pattern=[[-1, 128]],
          compare_op=ALU.is_equal, fill=0.0, base=0, channel_multiplier=1)
        I("pool", "affine_select", ["negones"], ["negtri"], out=negtri, in_=negones, pattern=[[-1, 128]],
          compare_op=ALU.is_ge, fill=0.0, base=0, channel_multiplier=1)
        I("pool", "affine_select", ["zero_bf"], ["maskM"], out=maskM, in_=zero_bf, pattern=[[1, 896]],
          compare_op=ALU.is_gt, fill=NEG, base=-384, channel_multiplier=-1)
        I("pool", "affine_select", ["ones_f"], ["tri_f"], out=tri_f, in_=ones_f, pattern=[[1, 128]],
          compare_op=ALU.is_ge, fill=0.0, base=0, channel_multiplier=-1)
        I("pool", "affine_select", ["zero_bf"], ["negmask2"], out=negmask2, in_=zero_bf[:, 0:128], pattern=[[1, 128]],
          compare_op=ALU.is_ge, fill=NEG, base=0, channel_multiplier=-1)
        CONSTS = ["ident_bf", "ident_f", "negtri", "negones", "ones_bf", "ones_f", "maskM", "tri_f", "negmask2"]

        fvs = A.alloc([D], F32)
        I("pool", "memset", [], ["fvs"], fvs[0:32, :], 0.0)
        DMA("sp", fvs[0:4, :], conv_w, [], ["fvs"])
        DMA("sp", fvs[4:5, :], row(conv_b), [], ["fvs"])
        DMA("sp", fvs[5:6, 0:2048], row(sb_norm), [], ["fvs"])
        DMA("sp", fvs[5:6, 2048:4096], row(ssd_norm), [], ["fvs"])
        DMA("sp", fvs[6:9, :], sconv, [], ["fvs"])
        DMA("sp", dtb_b, dt_bias.partition_broadcast(128), [], ["dtb_b"])
        DMA("sp", alog_b, a_log.partition_broadcast(128), [], ["alog_b"])
        for c in range(32):
            I("pe", "transpose", ["fvs", "ident_f"], [("ps", 7)], out=PS[7][:, c * 16:c * 16 + 16],
              in_=fvs[0:16, c * 128:(c + 1) * 128], identity=ident_f[0:16, 0:16])
        I("dve", "tensor_copy", [("ps", 7)], ["FV"], out=FV, in_=PS[7][:].rearrange("p (c r) -> p c r", c=32))
        S.barrier()
        A.reset(m0)

        mC = A.mark()
        hT = A.alloc([32, T], BF16)
        mP = A.mark()

        if "p1" in phases:
            npre_b = A.alloc([D], F32)
            xt = [A.alloc([D], F32) for _ in range(2)]
            hns = [A.alloc([D], BF16) for _ in range(2)]
            st1 = A.alloc([NT, 4], F32)
            DMA("sp", npre_b, norm_pre.partition_broadcast(128), [], ["npre_b"])
            def p1_load(tt):
                nt = 128 if tt < 16 else 64
                b = tt % 2
                DMA("sp", xt[b][0:nt, :], x[tt * 128:tt * 128 + nt, :], [], [("xt", b)])

            def p1_square(tt):
                nt = 128 if tt < 16 else 64
                b = tt % 2
                I("act", "activation", [("xt", b)], [("hn", b), ("st1", tt)], out=hns[b][0:nt, :], in_=xt[b][0:nt, :],
                  func=AF.Square, accum_out=st1[0:nt, tt, 0:1])

            def p1_stats(tt):
                nt = 128 if tt < 16 else 64
                I("dve", "tensor_scalar", [("st1", tt)], [("st1b", tt)], out=st1[0:nt, tt, 1:2], in0=st1[0:nt, tt, 0:1],
                  scalar1=1.0 / D, scalar2=EPS, op0=ALU.mult, op1=ALU.add)
                I("act", "activation", [("st1b", tt)], [("st1c", tt)], out=st1[0:nt, tt, 2:3], in_=st1[0:nt, tt, 1:2], func=AF.Ln)
                I("act", "activation", [("st1c", tt)], [("st1d", tt)], out=st1[0:nt, tt, 3:4], in_=st1[0:nt, tt, 2:3],
                  func=AF.Exp, scale=-0.5)

            def p1_main(tt):
                nt = 128 if tt < 16 else 64
                r0 = tt * 128
                b = tt % 2
                hn = hns[b]
                hk = ("hn", b)
                I("dve", "scalar_tensor_tensor", [("xt", b), ("st1d", tt), "npre_b"], [hk], out=hn[0:nt, :],
                  in0=xt[b][0:nt, :], scalar=st1[0:nt, tt, 3:4], in1=npre_b[0:nt, :], op0=ALU.mult, op1=ALU.mult)
                for c4 in range(4):
                    bk = (tt * 4 + c4) % 4
                    pv = psb(bk).rearrange("p (j n) -> p j n", j=8)
                    for j in range(8):
                        fc = c4 * 8 + j
                        I("pe", "transpose", [hk, "ident_bf"], [("ps", bk)], out=pv[:, j, 0:nt],
                          in_=hn[0:nt, fc * 128:(fc + 1) * 128], identity=ident_bf[0:nt, 0:nt])
                    copy_on(evac_eng(), hT[:, c4 * 8:(c4 + 1) * 8, r0:r0 + nt], pv[:, :, 0:nt], [("ps", bk)], [("hT", tt)])

            p1_load(0)
            p1_square(0)
            for tt in range(NT):
                if tt + 1 < NT:
                    p1_load(tt + 1)
                p1_stats(tt)
                if tt + 1 < NT:
                    p1_square(tt + 1)
                p1_main(tt)
            S.barrier()
            A.reset(mP)

        def hkeys(t0, n):
            return [("hT", tt) for tt in range(t0 // 128, (t0 + n + 127) // 128)]

        if "p2a" in phases:
            WSL = 2
            wt = [A.alloc([32, 128], BF16) for _ in range(WSL)]
            qTs = [A.alloc([T], BF16) for _ in range(2)]
            kT = A.alloc([T], BF16)
            gs = A.alloc([T], BF16)
            v_tok = A.alloc([NT, 128], BF16)
            kst = [A.alloc([4, 128], F32) for _ in range(2)]
            kb16 = [A.alloc([4, 128], BF16) for _ in range(2)]
            e_t = A.alloc([512], F32)
            og_t = e_t
            sp_t = [A.alloc([512], BF16) for _ in range(3)]
            racc2 = A.alloc([2, 512], BF16)
            racc = [racc2[:, 0, :], racc2[:, 1, :]]
            ckb = racc2.rearrange("p a b -> p (a b)").rearrange("p (j d) -> p j d", j=8)
            att_t = [A.alloc([512], BF16) for _ in range(2)]
            sq_t = A.alloc([512], BF16)
            om_t = [A.alloc([512], BF16) for _ in range(2)]
            vcb = A.alloc([8, 128], BF16)
            kTc = A.alloc([PAST], BF16)
            sp_s = A.alloc([64], BF16)
            I("pool", "memset", [], ["sp_s"], sp_s, 0.0)
            SCALE = 128.0 ** -0.5
            cnt = {"w": 0, "bank": 0, "st": 0}

            def load_w(col0):
                slot = cnt["w"] % WSL
                cnt["w"] += 1
                for part in range(4):
                    src = w_in[part * 1024:(part + 1) * 1024, col0:col0 + 128].rearrange("(c p) n -> p c n", p=128)
                    DMA("pool", wt[slot][:, part * 8:(part + 1) * 8, :], src, [], [("wt", slot, part)])
                return slot

            def proj_fm(col0, evac):
                slot = load_w(col0)
                for tbi, (t0, n) in enumerate(TBS):
                    bk = cnt["bank"] % 4
                    cnt["bank"] += 1
                    for fc in range(32):
                        I("pe", "matmul", [("wt", slot, fc // 8)] + hkeys(t0, n), [("ps", bk)], PS[bk][:, 0:n],
                          wt[slot][:, fc, :], hT[:, fc, t0:t0 + n], start=(fc == 0), stop=(fc == 31))
                    evac(bk, tbi, t0, n)

            def proj_tm(col0, hd, dst_out, is_k):
                slot = load_w(col0)
                kpend = []
                for j4 in range(5):
                    tiles = list(range(j4 * 4, min(j4 * 4 + 4, NT)))
                    bk = cnt["bank"] % 4
                    cnt["bank"] += 1
                    for j, tt in enumerate(tiles):
                        nt = 128 if tt < 16 else 64
                        for fc in range(32):
                            I("pe", "matmul", [("wt", slot, fc // 8), ("hT", tt)], [("ps", bk)],
                              PS[bk][0:nt, j * 128:(j + 1) * 128], hT[:, fc, tt * 128:tt * 128 + nt], wt[slot][:, fc, :],
                              start=(fc == 0), stop=(fc == 31))
                    while kpend:
                        kpend.pop(0)()
                    s = cnt["st"] % 2
                    cnt["st"] += 1
                    nj = len(tiles)
                    npart = 128 if tiles[0] < 16 else 64
                    copy_on(evac_eng(), kst[s][0:npart, 0:nj, :], PS[bk][0:npart, 0:nj * 128].rearrange("p (j d) -> p j d", j=nj),
                            [("ps", bk)], [("kst", s)])
                    r0 = tiles[0] * 128
                    if npart == 128:
                        dst = dst_out[r0:r0 + nj * 128, hd * 128:(hd + 1) * 128].rearrange("(j p) d -> p j d", p=128)
                        DMA("sp", dst, kst[s][:, 0:nj, :], [("kst", s)], [])
                    else:
                        DMA("sp", dst_out[r0:r0 + 64, hd * 128:(hd + 1) * 128], kst[s][0:64, 0, :], [("kst", s)], [])
                    if is_k:
                        I("dve", "tensor_copy", [("kst", s)], [("kb16", s)], out=kb16[s][0:npart, 0:nj, :], in_=kst[s][0:npart, 0:nj, :])

                        def tr(s=s, tiles=tiles, npart=npart, nj=nj, r0=r0, j4=j4):
                            pv = psb(4 + s).rearrange("p (j n) -> p j n", j=8)
                            for j, tt in enumerate(tiles):
                                I("pe", "transpose", [("kb16", s), "ident_bf"], [("ps", 4 + s)], out=pv[:, j, 0:npart],
                                  in_=kb16[s][0:npart, j, :], identity=ident_bf[0:npart, 0:npart])
                            if npart == 128:
                                copy_on(evac_eng(), kT[:, r0:r0 + nj * 128].rearrange("p (j n) -> p j n", j=nj), pv[:, 0:nj, :],
                                        [("ps", 4 + s)], [("kT", j4)])
                            else:
                                copy_on(evac_eng(), kT[:, r0:r0 + 64], pv[:, 0, 0:64], [("ps", 4 + s)], [("kT", j4)])
                        kpend.append(tr)
                    else:
                        I("dve", "tensor_copy", [("kst", s)], [("v_tok", j4)], out=v_tok[0:npart, tiles[0]:tiles[0] + nj, :],
                          in_=kst[s][0:npart, 0:nj, :])
                while kpend:
                    kpend.pop(0)()

            def attention(hd, qT, qp, bg=None):
                blocks = []
                for qb in range(4):
                    kbs = list(range(4 * qb + 3, -1, -1))
                    for i, kb in enumerate(kbs):
                        m = kb - 4 * qb
                        blocks.append(dict(q0=qb * 512, nq=512, k_ap=kT[:, kb * 128:(kb + 1) * 128], nk=128,
                                           v_ap=v_tok[:, kb, :], mask=(maskM[:, 384 - 128 * m:384 - 128 * m + 512] if m >= 0 else None),
                                           first=(i == 0), last=(i == len(kbs) - 1), qkey=("qT", qp, qb), kkey=("kT", kb // 4),
                                           vkey=("v_tok", kb // 4), ckeys=[], sps=None, obank=6, qbi=qb))
                blocks.append(dict(q0=TP, nq=64, k_ap=kT[:, TP:T], nk=64, v_ap=v_tok[0:64, 16, :], mask=maskM[0:64, 384:448],
                                   first=True, last=False, qkey=("qT", qp, 4), kkey=("kT", 4), vkey=("v_tok", 4), ckeys=[],
                                   sps=sp_s, obank=6, qbi=4))
                for i, kb in enumerate(range(7, -1, -1)):
                    blocks.append(dict(q0=TP, nq=64, k_ap=kTc[:, kb * 128:(kb + 1) * 128], nk=128, v_ap=vcb[:, kb, :], mask=None,
                                       first=False, last=(i == 7), qkey=("qT", qp, 4), kkey="kTc", vkey="vcb", ckeys=[], sps=None,
                                       obank=6, qbi=4))
                n = len(blocks)
                state = {"R": None, "Rkey": None, "ra": 0}

                def Zm(i):
                    b = blocks[i]
                    zb = i % 3
                    nk, nq = b["nk"], b["nq"]
                    I("pe", "matmul", [b["kkey"], b["qkey"]], [("ps", zb)], PS[zb][0:nk, 0:nq], b["k_ap"], qT[:, b["q0"]:b["q0"] + nq],
                      start=True, stop=False)
                    if b["mask"] is not None:
                        I("pe", "matmul", ["ident_bf", "maskM"], [("ps", zb)], PS[zb][0:nk, 0:nq], ident_bf[0:nk, 0:nk], b["mask"],
                          start=False, stop=False)
                    I("act", "activation", [("ps", zb)], ["e_t"], out=e_t[0:nk, 0:nq], in_=PS[zb][0:nk, 0:nq], func=AF.Exp)
                    if b["sps"] is not None:
                        sp_ap, spk = b["sps"], "sp_s"
                        b["sp_full"] = sp_ap[:, 0:nq]
                    else:
                        si = i % 3
                        sp_ap, spk = sp_t[si], ("sp", si)
                        b["sp_full"] = sp_ap[:, 0:nq]
                    b["sp"], b["spk"] = sp_ap, spk
                    I("act", "activation", ["e_t"], [spk], out=sp_ap[0:nk, 0:nq], in_=e_t[0:nk, 0:nq], func=AF.Ln, bias=1.0)
                    if b["first"]:
                        b["R"], b["Rk"] = None, None
                    else:
                        pb = blocks[i - 1]
                        if pb["first"]:
                            b["R"], b["Rk"] = pb["sp_full"], pb["spk"]
                        else:
                            ra = state["ra"] % 2
                            state["ra"] += 1
                            I("dve", "tensor_tensor", [pb["Rk"], pb["spk"]], [("racc", ra)], out=racc[ra][:, 0:nq],
                              in0=pb["R"], in1=pb["sp_full"], op=ALU.add)
                            b["R"], b["Rk"] = racc[ra][:, 0:nq], ("racc", ra)

                def Am(i):
                    b = blocks[i]
                    ab = i % 3
                    nk, nq = b["nk"], b["nq"]
                    I("pe", "matmul", ["negtri", b["spk"]], [("ps", ab)], PS[ab][0:nk, 0:nq], negtri[0:nk, 0:nk], b["sp"][0:nk, 0:nq],
                      start=False, stop=(b["R"] is None))
                    if b["R"] is not None:
                        I("pe", "matmul", ["negones", b["Rk"]], [("ps", ab)], PS[ab][0:nk, 0:nq], negones[:, 0:nk], b["R"],
                          start=False, stop=True)
                    ai = i % 2
                    I("act", "activation", [("ps", ab)], [("att", ai)], out=att_t[ai][0:nk, 0:nq], in_=PS[ab][0:nk, 0:nq], func=AF.Exp)

                def AVm(i):
                    b = blocks[i]
                    nk, nq = b["nk"], b["nq"]
                    ob = b["obank"]
                    ai = i % 2
                    I("pe", "matmul", [b["vkey"], ("att", ai)], [("ps", ob)], PS[ob][:, 0:nq], b["v_ap"][0:nk, :], att_t[ai][0:nk, 0:nq],
                      start=b["first"], stop=b["last"])
                    if b["last"]:
                        q0, qbi = b["q0"], b["qbi"]
                        I("dve", "tensor_tensor", [("ps", ob), ("gs", qbi)], ["e_t"], out=og_t[:, 0:nq], in0=PS[ob][:, 0:nq],
                          in1=gs[:, q0:q0 + nq], op=ALU.mult)
                        oi = qbi % 2
                        I("dve", "tensor_scalar", ["e_t", "FV"], [("om", oi)], out=om_t[oi][:, 0:nq], in0=og_t[:, 0:nq],
                          scalar1=FV[:, hd, 5:6], scalar2=None, op0=ALU.mult)
                        DMA("sp", omix_d[hd, :, q0:q0 + nq], om_t[oi][:, 0:nq], [("om", oi)], [("omix", hd, qbi)])
                        I("act", "activation", ["e_t"], ["sq_t"], out=sq_t[:, 0:nq], in_=og_t[:, 0:nq], func=AF.Square)
                        def ssq_mm(q0=q0, nq=nq):
                            for j in range((nq + 127) // 128):
                                tt = q0 // 128 + j
                                nt = min(128, nq - j * 128)
                                I("pe", "matmul", ["sq_t", "ones_bf"], [("ps", 7)], PS[7][0:nt, tt:tt + 1], sq_t[:, j * 128:j * 128 + nt],
                                  ones_bf[:, 0:1], start=True, stop=True)
                        pending.append(ssq_mm)

                pending = []
                for s in range(n + 2):
                    if pending and s % 2 == 0:
                        pending.pop(0)()
                    if s < n:
                        Zm(s)
                    if 0 <= s - 1 < n:
                        Am(s - 1)
                    if 0 <= s - 2 < n:
                        AVm(s - 2)
                    if bg is not None:
                        for _ in range(4):
                            next(bg, None)
                if bg is not None:
                    for _ in bg:
                        pass
                while pending:
                    pending.pop(0)()
                I("dve", "tensor_tensor", [("ps", 7), "ssq_sb"], ["ssq_sb"], out=ssq_sb[:, 0:16], in0=PS[7][:, 0:16], in1=ssq_sb[:, 0:16], op=ALU.add)
                I("dve", "tensor_tensor", [("ps", 7), "ssq_sb"], ["ssq_sb"], out=ssq_sb[0:64, 16:17], in0=PS[7][0:64, 16:17],
                  in1=ssq_sb[0:64, 16:17], op=ALU.add)

            nheads = 16 if not dbg else dbg.get("nheads", 16)

            def q_proj_gen(hd):
                qp = hd % 2
                slot = load_w(Q0 + hd * 128)
                for tbi, (t0, n) in enumerate(TBS):
                    bk = 4 + tbi % 2
                    for fc in range(32):
                        I("pe", "matmul", [("wt", slot, fc // 8)] + hkeys(t0, n), [("ps", bk)], PS[bk][:, 0:n],
                          wt[slot][:, fc, :], hT[:, fc, t0:t0 + n], start=(fc == 0), stop=(fc == 31))
                        yield
                    I("dve", "tensor_scalar", [("ps", bk)], [("qT", qp, tbi)], out=qTs[qp][:, t0:t0 + n], in0=PS[bk][:, 0:n],
                      scalar1=SCALE, scalar2=None, op0=ALU.mult)

            for _ in q_proj_gen(0):
                pass
            for hd in range(nheads):
                proj_tm(K0 + hd * 128, hd, k_out, True)
                proj_tm(V0 + hd * 128, hd, v_out, False)
                DMA("pool", ckb, ck[:, hd * 128:(hd + 1) * 128].rearrange("(j p) d -> p j d", p=128), [], [("racc", 0), ("racc", 1)])
                DMA("pool", vcb, cv[:, hd * 128:(hd + 1) * 128].rearrange("(j p) d -> p j d", p=128), [], ["vcb"])
                proj_fm(G0 + hd * 128, lambda bk, tbi, t0, n: I("act", "activation", [("ps", bk)], [("gs", tbi)], out=gs[:, t0:t0 + n],
                                                               in_=PS[bk][:, 0:n], func=AF.Silu))
                pv = psb(7).rearrange("p (j n) -> p j n", j=8)
                for j in range(8):
                    I("pe", "transpose", [("racc", 0), ("racc", 1), "ident_bf"], [("ps", 7)], out=pv[:, j, :], in_=ckb[:, j, :], identity=ident_bf)
                copy_on(evac_eng(), kTc.rearrange("p (j n) -> p j n", j=8), pv, [("ps", 7)], ["kTc"])
                attention(hd, qTs[hd % 2], hd % 2, q_proj_gen(hd + 1) if hd + 1 < nheads else None)
            if dbg:
                I("dve", "tensor_copy", ["ssq_sb"], ["dbgt"], out=og_t[:, 0:NT], in_=ssq_sb)
                DMA("sp", dbg_d[:, 0:NT], og_t[:, 0:NT], ["dbgt"], [])
            S.barrier()
            A.reset(mP)


        if "p2b" in phases:
            A.reset(mP)
            wt = [A.alloc([32, 128], BF16) for _ in range(2)]
            rA = A.alloc([2120], F32)
            wdt = rA[:, 0:512].bitcast(BF16).rearrange("p (c n) -> p c n", c=32)
            rB = A.alloc([2120], F32)
            xact = [A.alloc([T], BF16) for _ in range(2)]
            Bact = A.alloc([T], BF16)
            Cact = A.alloc([T], BF16)
            dt_tok = A.alloc([NT, 32], F32)
            a_b = A.alloc([32], F32)
            dsk_b = A.alloc([32], F32)
            hlast = A.alloc([32, 8], BF16)
            cst = [A.alloc([128], F32) for _ in range(1)]
            dAb4 = A.alloc([4, 128], F32)
            D4 = A.alloc([4, 128], F32)
            Wt4 = A.alloc([4, 128], BF16)
            xB_tok = A.alloc([3, 128], BF16)
            w4 = A.alloc([4], F32)
            xw = A.alloc([4, 64], BF16)
            dAx = A.alloc([2, 128], F32)
            t1 = A.alloc([128], F32)
            Hs = A.alloc([256], F32)
            Hbf = A.alloc([256], BF16)
            hst = [A.alloc([128], F32) for _ in range(1)] * 2
            zs_t = [A.alloc([512], BF16) for _ in range(1)]
            om_b = [A.alloc([512], BF16) for _ in range(1)]
            cntb = {"w": 0, "bank": 0, "z": 0, "c": 0, "h": 0}
            zpend = []

            def load_wb(col0):
                slot = cntb["w"] % 2
                cntb["w"] += 1
                for part in range(4):
                    src = w_in[part * 1024:(part + 1) * 1024, col0:col0 + 128].rearrange("(c p) n -> p c n", p=128)
                    DMA("pool", wt[slot][:, part * 8:(part + 1) * 8, :], src, [], [("wt", slot, part)])
                return slot

            def proj_fmb(col0, evac, extra=None):
                slot = load_wb(col0)
                for tbi, (t0, n) in enumerate(TBS):
                    bk = cntb["bank"] % 3
                    cntb["bank"] += 1
                    for fc in range(32):
                        I("pe", "matmul", [("wt", slot, fc // 8)] + hkeys(t0, n), [("ps", bk)], PS[bk][:, 0:n],
                          wt[slot][:, fc, :], hT[:, fc, t0:t0 + n], start=(fc == 0), stop=(fc == 31))
                    evac(bk, tbi, t0, n)
                if extra is not None:
                    extra(slot)

            DMA("pool", wdt, w_in[:, DT0:DT0 + 32].rearrange("(c p) n -> p c n", p=128), [], ["wdt", "rA"])
            DMA("sp", dsk_b, d_skip.partition_broadcast(128), [], ["dsk_b"])
            I("dve", "tensor_copy", ["dsk_b"], ["dskp"], out=dskp[0:64, :], in_=dsk_b[0:64, :].rearrange("p (j t) -> p j t", t=2)[:, :, 0])
            I("dve", "tensor_copy", ["dsk_b"], ["dskp"], out=dskp[64:128, :], in_=dsk_b[64:128, :].rearrange("p (j t) -> p j t", t=2)[:, :, 1])
            I("act", "activation", ["alog_b"], ["a_e"], out=a_b, in_=alog_b, func=AF.Exp)
            I("dve", "tensor_scalar", ["a_e"], ["a_b"], out=a_b, in0=a_b, scalar1=-1.0, scalar2=None, op0=ALU.mult)
            for tt in range(NT):
                nt = 128 if tt < 16 else 64
                bk, cc = (3, tt * 32) if tt < 16 else (4, 0)
                for fc in range(32):
                    I("pe", "matmul", ["wdt", "rA", ("hT", tt)], [("ps", bk)], PS[bk][0:nt, cc:cc + 32], hT[:, fc, tt * 128:tt * 128 + nt],
                      wdt[:, fc, :], start=(fc == 0), stop=(fc == 31))
            I("dve", "tensor_tensor", [("ps", 3), "dtb_b"], ["dt_tok"], out=dt_tok[:, 0:16, :],
              in0=PS[3][:].rearrange("p (t h) -> p t h", t=16),
              in1=dtb_b.rearrange("p (o h) -> p o h", o=1).broadcast_to([128, 16, 32]), op=ALU.add)
            I("dve", "tensor_tensor", [("ps", 4), "dtb_b"], ["dt_tok"], out=dt_tok[0:64, 16, :], in0=PS[4][0:64, 0:32], in1=dtb_b[0:64, :], op=ALU.add)
            I("act", "activation", ["dt_tok"], ["dt_e"], out=dt_tok[:, 0:16, :], in_=dt_tok[:, 0:16, :], func=AF.Exp)
            I("act", "activation", ["dt_tok", "dt_e"], ["dt_e"], out=dt_tok[0:64, 16, :], in_=dt_tok[0:64, 16, :], func=AF.Exp)
            I("act", "activation", ["dt_e"], ["dt_f"], out=dt_tok[:, 0:16, :], in_=dt_tok[:, 0:16, :], func=AF.Ln, bias=1.0)
            I("act", "activation", ["dt_e", "dt_f"], ["dt_f"], out=dt_tok[0:64, 16, :], in_=dt_tok[0:64, 16, :], func=AF.Ln, bias=1.0)
            I("dve", "tensor_copy", [("hT", 15)], ["hlast"], out=hlast[:, :, 0:3], in_=hT[:, :, TP - 3:TP])
            I("dve", "tensor_copy", [("hT", 16), "hlast"], ["hlast"], out=hlast[:, :, 3:6], in_=hT[:, :, T - 3:T])

            def proj_conv(col0, dst, dkey):
                ch = (col0 - XBC0) // 128
                raw, cvb = rA, rB

                def ev(bk, tbi, t0, n):
                    o0 = 3 + t0 if tbi < 4 else 2054
                    copy_on(evac_eng(), raw[:, o0:o0 + n], PS[bk][:, 0:n], [("ps", bk)], ["rA"])

                def extra(slot):
                    cs = 0
                    cntb["c"] += 1
                    for fc in range(32):
                        I("pe", "matmul", [("wt", slot, fc // 8), "hlast"], [("ps", 4)], PS[4][0:6, 128:256], hlast[:, fc, 0:6], wt[slot][:, fc, :],
                          start=(fc == 0), stop=(fc == 31))
                    copy_on("act", cst[cs][0:6, :], PS[4][0:6, 128:256], [("ps", 4)], [("cst", cs)])
                    DMA("sp", conv_out[0:6, ch * 128:(ch + 1) * 128], cst[cs][0:6, :], [("cst", cs)], [])

                proj_fmb(col0, ev, extra)
                I("dve", "memset", [], ["rA"], raw[:, 0:3], 0.0)
                I("dve", "tensor_copy", ["FV"], ["rA"], out=raw[:, 2051:2054], in_=FV[:, ch, 6:9])
                L = 2115
                I("dve", "tensor_scalar", ["rA", "FV"], ["rB"], out=cvb[:, 0:L], in0=raw[:, 0:L], scalar1=FV[:, ch, 0:1], scalar2=FV[:, ch, 4:5],
                  op0=ALU.mult, op1=ALU.add)
                for j in range(1, 4):
                    I("dve", "scalar_tensor_tensor", ["rA", "rB", "FV"], ["rB"], out=cvb[:, 0:L], in0=raw[:, j:j + L], scalar=FV[:, ch, j:j + 1],
                      in1=cvb[:, 0:L], op0=ALU.mult, op1=ALU.add)
                I("act", "activation", ["rB"], [dkey], out=dst[:, 0:TP], in_=cvb[:, 0:TP], func=AF.Silu)
                I("act", "activation", ["rB", dkey], [dkey], out=dst[:, TP:T], in_=cvb[:, 2051:2115], func=AF.Silu)

            w1f = wt[1].rearrange("p a b -> p (a b)")

            def carve(off, n, dtype):
                if dtype == F32:
                    return w1f[:, off // 2:off // 2 + 2 * n].bitcast(F32)
                return w1f[:, off // 2:off // 2 + n]
            TS2 = [dict(dAb4=dAb4, D4=D4, dAx=dAx, Wt4=Wt4, xB_tok=xB_tok, xw=xw, w4=w4),
                   dict(dAb4=carve(0, 512, F32).rearrange("p (a b) -> p a b", a=4), D4=carve(2048, 512, F32).rearrange("p (a b) -> p a b", a=4),
                        dAx=carve(4096, 256, F32).rearrange("p (a b) -> p a b", a=2), Wt4=carve(5120, 512, BF16).rearrange("p (a b) -> p a b", a=4),
                        xB_tok=carve(6144, 384, BF16).rearrange("p (a b) -> p a b", a=3), xw=carve(6912, 256, BF16).rearrange("p (a b) -> p a b", a=4),
                        w4=carve(7424, 4, F32))]
            dA_g = carve(7456, 68, F32).rearrange("p (c h) -> p c h", c=NT)
            negcum_g = carve(7744, 68, F32).rearrange("p (c h) -> p c h", c=NT)
            etot_g = A.alloc([NT, 4], F32)
            T1KEYS = [("dAb4", 1), ("dAx", 1), ("xB_tok", 1), ("w4", 1), ("xw", 1), "dA_g", "negcum_g"] + \
                     [("D4", 1, hq) for hq in range(4)] + [("Wt4", 1, hq) for hq in range(4)]
            W1KEYS = [("wt", 1, part) for part in range(4)]
            zflat = zs_t[0]
            oflat = om_b[0]
            ecxs = [[zflat[:, 0:256].bitcast(F32), zflat[:, 256:512].bitcast(F32)], [oflat[:, 0:256].bitcast(F32), oflat[:, 256:512].bitcast(F32)]]
            EKEYS = [("ecx", p_, r_) for p_ in range(2) for r_ in range(2)]
            ZKEYS = [("zs", 0), ("omb", 0)]

            def group_pre(g):
                hs = slice(4 * g, 4 * g + 4)
                I("dve", "tensor_tensor", ["dt_f", "a_b"], ["dA_g"], out=dA_g[:, 0:16, :], in0=dt_tok[:, 0:16, hs],
                  in1=a_b[:, hs].rearrange("p (o h) -> p o h", o=1).broadcast_to([128, 16, 4]), op=ALU.mult)
                I("dve", "tensor_tensor", ["dt_f", "a_b", "dA_g"], ["dA_g"], out=dA_g[0:64, 16, :], in0=dt_tok[0:64, 16, hs], in1=a_b[0:64, hs], op=ALU.mult)
                for c in range(NT):
                    nt = 128 if c < 16 else 64
                    I("pe", "matmul", ["dA_g", "tri_f"], [("ps", 3)], PS[3][0:nt, c * 4:c * 4 + 4], tri_f[0:nt, 0:nt], dA_g[0:nt, c, :], start=True, stop=True)
                    I("pe", "matmul", ["dA_g", "ones_f"], [("ps", 3)], PS[3][:, 128 + c * 4:128 + c * 4 + 4], ones_f[0:nt, :], dA_g[0:nt, c, :],
                      start=True, stop=True)
                I("dve", "tensor_scalar", [("ps", 3)], ["negcum_g"], out=negcum_g[:, 0:16, :], in0=PS[3][:, 0:64].rearrange("p (c h) -> p c h", c=16),
                  scalar1=-1.0, scalar2=None, op0=ALU.mult)
                I("dve", "tensor_scalar", [("ps", 3), "negcum_g"], ["negcum_g"], out=negcum_g[0:64, 16, :], in0=PS[3][0:64, 64:68], scalar1=-1.0, scalar2=None,
                  op0=ALU.mult)
                I("act", "activation", [("ps", 3)], ["etot_g"], out=etot_g, in_=PS[3][:, 128:196].rearrange("p (c h) -> p c h", c=NT), func=AF.Exp)

            def geo(c):
                nt = 128 if c < 16 else 64
                par = c % 2
                banks = (5, 4, 7, 3) if par == 0 else (6, 0, 1, 2)
                return nt, c * 128, par, TS2[par], banks

            def u1a(g, c, inter):
                nt, t0, par, tt_, _ = geo(c)
                dA4 = dA_g[0:nt, c, :]
                I("dve", "tensor_copy", ["dA_g"], [("dAb4", par)], out=tt_["dAb4"][0:nt, :, 0:nt],
                  in_=dA4.rearrange("p (h o) -> p h o", o=1).broadcast_to([nt, 4, nt]))
                if inter:
                    I("dve", "tensor_copy", ["dA_g"], [("dAx", par)], out=tt_["dAx"][0:nt, :, :].rearrange("p a (b q) -> p (a b) q", q=64),
                      in_=dA4.rearrange("p (h o) -> p h o", o=1).broadcast_to([nt, 4, 64]))

            def u1b(g, c, inter):
                nt, t0, par, tt_, (pb, b1, yb, tb) = geo(c)
                dAb4, D4, dAx, Wt4, xB_tok, xw, w4 = tt_["dAb4"], tt_["D4"], tt_["dAx"], tt_["Wt4"], tt_["xB_tok"], tt_["xw"], tt_["w4"]
                kdAb, kdAx, kxB, kw4, kxw = ("dAb4", par), ("dAx", par), ("xB_tok", par), ("w4", par), ("xw", par)
                hs = slice(4 * g, 4 * g + 4)
                I("pe", "matmul", ["Bact", "Cact"], [("ps", b1)], PS[b1][0:nt, 0:nt], Bact[:, t0:t0 + nt], Cact[:, t0:t0 + nt], start=True, stop=True)
                for hq in range(4):
                    I("pe", "matmul", [kdAb, "tri_f"], [("ps", pb)], PS[pb][0:nt, hq * 128:hq * 128 + nt], dAb4[0:nt, hq, 0:nt], tri_f[0:nt, 0:nt],
                      start=True, stop=False)
                    I("pe", "matmul", ["ident_bf", "negmask2"], [("ps", pb)], PS[pb][0:nt, hq * 128:hq * 128 + nt], ident_bf[0:nt, 0:nt],
                      negmask2[0:nt, 0:nt], start=False, stop=True)
                pv2 = psb(tb).rearrange("p (j n) -> p j n", j=8)
                for pr in range(2):
                    I("pe", "transpose", [("xact", pr), "ident_bf"], [("ps", tb)], out=pv2[0:nt, pr, :], in_=xact[pr][:, t0:t0 + nt], identity=ident_bf)
                I("pe", "transpose", ["Bact", "ident_bf"], [("ps", tb)], out=pv2[0:nt, 2, :], in_=Bact[:, t0:t0 + nt], identity=ident_bf)
                if inter:
                    for pr in range(2):
                        I("pe", "matmul", [kdAx, "tri_f"], [("ps", b1)], PS[b1][:, 128 + pr * 128:128 + pr * 128 + nt],
                          dAx[0:nt, pr, :], tri_f[0:nt, 0:nt], start=True, stop=True)
                for hq in range(4):
                    I("act", "activation", [("ps", pb), "negcum_g"], [("D4", par, hq)], out=D4[0:nt, hq, 0:nt], in_=PS[pb][0:nt, hq * 128:hq * 128 + nt],
                      func=AF.Exp, bias=negcum_g[0:nt, c, hq:hq + 1], scale=1.0)
                    I("dve", "scalar_tensor_tensor", [("D4", par, hq), "dt_f", ("ps", b1)], [("Wt4", par, hq)], out=Wt4[0:nt, hq, 0:nt],
                      in0=D4[0:nt, hq, 0:nt], scalar=dt_tok[0:nt, c, 4 * g + hq:4 * g + hq + 1], in1=PS[b1][0:nt, 0:nt], op0=ALU.mult, op1=ALU.mult)
                copy_on("act", xB_tok[0:nt, :, :], pv2[0:nt, 0:3, :], [("ps", tb)], [kxB])
                if inter:
                    for pr in range(2):
                        I("act", "activation", [("ps", b1)], [("ecx", par, pr)], out=ecxs[par][pr][:, 0:nt],
                          in_=PS[b1][:, 128 + pr * 128:128 + pr * 128 + nt], func=AF.Exp)
                D4k = [("D4", par, hq) for hq in range(4)]
                I("dve", "tensor_tensor", D4k + ["dt_f"], [kw4], out=w4[0:nt, :], in0=D4[0:nt, :, nt - 1], in1=dt_tok[0:nt, c, hs], op=ALU.mult)
                I("dve", "tensor_tensor", [kxB, kw4], [kxw], out=xw[0:nt, :, :],
                  in0=xB_tok[0:nt, 0:2, :].rearrange("p a (b q) -> p (a b) q", q=64),
                  in1=w4[0:nt, :].rearrange("p (h o) -> p h o", o=1).broadcast_to([nt, 4, 64]), op=ALU.mult)

            def u2(g, c, first, inter):
                nt, t0, par, tt_, (pb, b1, yb, tb) = geo(c)
                Wt4, xB_tok, xw = tt_["Wt4"], tt_["xB_tok"], tt_["xw"]
                kxB, kxw = ("xB_tok", par), ("xw", par)
                yall = [rA, rB]
                ykey = ["rA", "rB"]
                for pr in range(2):
                    for hh in range(2):
                        I("pe", "matmul", [kxB, ("Wt4", par, 2 * pr + hh)], [("ps", yb)], PS[yb][hh * 64:(hh + 1) * 64, pr * 128:pr * 128 + nt],
                          xB_tok[0:nt, pr, hh * 64:(hh + 1) * 64], Wt4[0:nt, 2 * pr + hh, 0:nt], start=True, stop=True)
                    if inter:
                        I("pe", "matmul", ["Hbf", "Cact"], [("ps", yb)], PS[yb][:, 256 + pr * 128:256 + pr * 128 + nt], Hbf[:, pr * 128:(pr + 1) * 128],
                          Cact[:, t0:t0 + nt], start=True, stop=True)
                I("pe", "matmul", [kxB, kxw], [("ps", tb)], PS[tb][:, 256:512], xB_tok[0:nt, 2, :], xw[0:nt, :, :].rearrange("p h q -> p (h q)"),
                  start=True, stop=True)
                if first:
                    I("dve", "tensor_copy", [("ps", tb)], ["Hs"], out=Hs, in_=PS[tb][:, 256:512])
                else:
                    I("dve", "tensor_tensor", ["Hs", "etot_g"], ["Hs"], out=Hs.rearrange("p (h q) -> p h q", q=64), in0=Hs.rearrange("p (h q) -> p h q", q=64),
                      in1=etot_g[:, c, :].rearrange("p (h o) -> p h o", o=1).broadcast_to([128, 4, 64]), op=ALU.mult)
                    I("dve", "tensor_tensor", ["Hs", ("ps", tb)], ["Hs"], out=Hs, in0=PS[tb][:, 256:512], in1=Hs, op=ALU.add)
                for pr in range(2):
                    if inter:
                        I("dve", "tensor_tensor", [("ps", yb), ("ecx", par, pr)], ["t1"], out=t1[:, 0:nt], in0=PS[yb][:, 256 + pr * 128:256 + pr * 128 + nt],
                          in1=ecxs[par][pr][:, 0:nt], op=ALU.mult)
                        I("dve", "tensor_tensor", [("ps", yb), "t1"], [ykey[pr]], out=yall[pr][:, t0:t0 + nt], in0=PS[yb][:, pr * 128:pr * 128 + nt],
                          in1=t1[:, 0:nt], op=ALU.add)
                    else:
                        I("dve", "tensor_copy", [("ps", yb)], [ykey[pr]], out=yall[pr][:, t0:t0 + nt], in_=PS[yb][:, pr * 128:pr * 128 + nt])

            def hbf_update():
                I("act", "activation", ["Hs"], ["Hbf"], out=Hbf, in_=Hs, func=AF.Copy)

            def state_out(g, row0):
                for pr in range(2):
                    hsx = cntb["h"] % 2
                    cntb["h"] += 1
                    I("pe", "transpose", ["Hs", "ident_f"], [("ps", 3)], out=PS[3][:, 0:128], in_=Hs[:, pr * 128:(pr + 1) * 128], identity=ident_f)
                    copy_on("act", hst[hsx], PS[3][:, 0:128], [("ps", 3)], ["hst"])
                    DMA("sp", ssd_out[row0 + g * 256 + pr * 128:row0 + g * 256 + (pr + 1) * 128, :], hst[hsx], ["hst"], [])

            def state_in(g):
                for pr in range(2):
                    hsx = cntb["h"] % 2
                    cntb["h"] += 1
                    DMA("sp", hst[hsx], sssd[g * 256 + pr * 128:g * 256 + (pr + 1) * 128, :], [], ["hst"])
                    I("pe", "transpose", ["hst", "ident_f"], [("ps", 3)], out=PS[3][:, 0:128], in_=hst[hsx], identity=ident_f)
                    I("dve", "tensor_copy", [("ps", 3)], ["Hs"], out=Hs[:, pr * 128:(pr + 1) * 128], in_=PS[3][:, 0:128])
                I("act", "activation", ["Hs"], ["Hbf"], out=Hbf, in_=Hs, func=AF.Copy)

            ngroups = 8 if not dbg else dbg.get("ngroups", 8)
            for g in range(ngroups):
                proj_conv(XBC0 + g * 256, xact[0], ("xact", 0))
                proj_conv(XBC0 + g * 256 + 128, xact[1], ("xact", 1))
                proj_conv(XBC0 + 2048 + g * 128, Bact, "Bact")
                proj_conv(XBC0 + 3072 + g * 128, Cact, "Cact")
                S.fence(W1KEYS, T1KEYS)
                S.fence(ZKEYS, EKEYS)
                group_pre(g)
                u1a(g, 0, False)
                for st_ in range(18):
                    if st_ + 1 < 17:
                        u1a(g, st_ + 1, True)
                    if st_ < 17:
                        u1b(g, st_, inter=(st_ > 0))
                    if st_ >= 1:
                        c = st_ - 1
                        if c == 16:
                            state_out(g, 0)
                            state_in(g)
                        u2(g, c, first=(c == 0), inter=(c > 0))
                        hbf_update()
                state_out(g, 2048)
                S.fence(T1KEYS, W1KEYS)
                S.fence(EKEYS, ZKEYS)
                yall = [rA, rB]
                ykey = ["rA", "rB"]
                for pr in range(2):
                    fcx = 16 + 2 * g + pr
                    I("dve", "scalar_tensor_tensor", [("xact", pr), "dskp", ykey[pr]], [ykey[pr]], out=yall[pr][:, 0:T], in0=xact[pr][:, 0:T],
                      scalar=dskp[:, 2 * g + pr:2 * g + pr + 1], in1=yall[pr][:, 0:T], op0=ALU.mult, op1=ALU.add)

                    def evz(bk, tbi, t0, n, pr=pr, fcx=fcx):
                        zi = 0
                        while zpend:
                            zpend.pop(0)()
                        cntb["z"] += 1
                        I("act", "activation", [("ps", bk)], [("zs", zi)], out=zs_t[zi][:, 0:n], in_=PS[bk][:, 0:n], func=AF.Silu)
                        I("dve", "tensor_tensor", [ykey[pr], ("zs", zi)], [ykey[pr]], out=yall[pr][:, t0:t0 + n], in0=yall[pr][:, t0:t0 + n],
                          in1=zs_t[zi][:, 0:n], op=ALU.mult)
                        I("dve", "tensor_scalar", [ykey[pr], "FV"], [("omb", zi)], out=om_b[zi][:, 0:n], in0=yall[pr][:, t0:t0 + n],
                          scalar1=FV[:, fcx, 5:6], scalar2=None, op0=ALU.mult)
                        DMA("sp", omix_d[fcx, :, t0:t0 + n], om_b[zi][:, 0:n], [("omb", zi)], [])
                        I("act", "activation", [ykey[pr]], [("zs", zi)], out=zs_t[zi][:, 0:n], in_=yall[pr][:, t0:t0 + n], func=AF.Square)
                        def ssq_mm(t0=t0, n=n, zi=zi):
                            for j in range((n + 127) // 128):
                                tt = t0 // 128 + j
                                nt = min(128, n - j * 128)
                                I("pe", "matmul", [("zs", zi), "ones_bf"], [("ps", 5)], PS[5][0:nt, tt:tt + 1], zs_t[zi][:, j * 128:j * 128 + nt],
                                  ones_bf[:, 0:1], start=True, stop=True)
                        zpend.append(ssq_mm)

                    proj_fmb(Z0 + g * 256 + pr * 128, evz)
                    while zpend:
                        zpend.pop(0)()
                    I("dve", "tensor_tensor", [("ps", 5), "ssq_ssd"], ["ssq_ssd"], out=ssq_ssd[:, 0:16], in0=PS[5][:, 0:16], in1=ssq_ssd[:, 0:16], op=ALU.add)
                    I("dve", "tensor_tensor", [("ps", 5), "ssq_ssd"], ["ssq_ssd"], out=ssq_ssd[0:64, 16:17], in0=PS[5][0:64, 16:17],
                      in1=ssq_ssd[0:64, 16:17], op=ALU.add)
            if dbg:
                I("dve", "tensor_copy", ["ssq_ssd"], ["dbgt"], out=t1[:, 0:NT], in_=ssq_ssd)
                DMA("sp", dbg_d[:, 32:32 + NT], t1[:, 0:NT], ["dbgt"], [])
            S.barrier()
            A.reset(mP)

        if "p3" in phases:
            A.reset(mC)
            if p3only:
                DMA("sp", ssq_sb, ssq_in[:, 0:NT], [], ["ssq_sb"])
                DMA("sp", ssq_ssd, ssq_in[:, NT:2 * NT], [], ["ssq_ssd"])
            GROUPS = [list(range(0, 6)), list(range(6, 12)), list(range(12, 17))]
            rs = A.alloc([4, NT], F32)
            for (src, key, a, b) in ((ssq_sb, "ssq_sb", 0, 1), (ssq_ssd, "ssq_ssd", 2, 3)):
                I("dve", "tensor_scalar", [key], [("rs", a)], out=rs[:, a, :], in0=src, scalar1=1.0 / 2048, scalar2=EPS,
                  op0=ALU.mult, op1=ALU.add)
                I("act", "activation", [("rs", a)], [("rsl", a)], out=rs[:, a, :], in_=rs[:, a, :], func=AF.Ln)
                I("act", "activation", [("rsl", a)], [("rs", b)], out=rs[:, b, :], in_=rs[:, a, :], func=AF.Exp, scale=-0.5)
            m3 = A.mark()

            def stage3(first):
                A.reset(m3)
                src_d = omix_d if first else x1T_d
                W = w_out if first else w_gate
                nrm = norm_post if first else ple_norm
                res_src = x if first else x1_d
                res_dst = x1_d if first else y
                oT = A.alloc([32, 768], BF16)
                msb = A.alloc([6, D], F32)
                NW3 = 4 if first else 3
                w3 = [A.alloc([4, 512], BF16) for _ in range(NW3)]
                nrmr = [A.alloc([512], F32) for _ in range(2)]
                xr = [A.alloc([1024], F32) for _ in range(2)]
                junk = A.alloc([512], F32) if first else None
                stt = A.alloc([6, 12], F32)
                rst = A.alloc([6], F32)
                if first:
                    rB = [A.alloc([768], F32) for _ in range(2)]
                    dg = A.alloc([2, 6, 128], F32)
                    x1b = [A.alloc([1024], BF16) for _ in range(3)]
                    x1Ts = [A.alloc([8, 128], BF16) for _ in range(2)]
                else:
                    pT = A.alloc([2, 768], BF16)
                    pst = A.alloc([256], F32)
                    pbf = A.alloc([256], BF16)
                    wp = [A.alloc([2, 512], BF16) for _ in range(2)]
                    sg = [A.alloc([512], F32) for _ in range(2)]
                    tmp = [A.alloc([512], F32) for _ in range(1)] * 2
                    e_sb = A.alloc([6, 512], F32)
                c3 = {"w": 0, "n": 0, "x": 0, "e": 0, "t": 0}

                def geom(tiles):
                    nts = [128 if t < 16 else 64 for t in tiles]
                    return tiles[0] * 128, nts, sum(nts)

                def rstd_bcast(tiles):
                    t0, nts, ng = geom(tiles)
                    if first:
                        nj = len(tiles)
                        for which, col in ((0, 1), (1, 3)):
                            I("dve", "tensor_tensor", ["ident_f", ("rs", col)], [("dg", which)], out=dg[:, which, 0:nj, :],
                              in0=ident_f.rearrange("p (o l) -> p o l", o=1).broadcast_to([128, nj, 128]),
                              in1=rs[:, col, tiles[0]:tiles[0] + nj].rearrange("p (j o) -> p j o", o=1).broadcast_to([128, nj, 128]), op=ALU.mult)
                        for which, col in ((0, 1), (1, 3)):
                            for j, tt in enumerate(tiles):
                                nt = nts[j]
                                bank = 6 + (j * 128) // 512
                                cc = (j * 128) % 512
                                I("pe", "matmul", [("dg", which), "ones_f"], [("ps", bank)], PS[bank][:, cc:cc + nt], ones_f[0:nt, :],
                                  dg[0:nt, which, j, 0:nt], start=True, stop=True)
                            n6 = min(ng, 512)
                            copy_on("act", rB[which][:, 0:n6], PS[6][:, 0:n6], [("ps", 6)], [("rB", which)])
                            if ng > 512:
                                copy_on("act", rB[which][:, 512:ng], PS[7][:, 0:ng - 512], [("ps", 7)], [("rB", which)])

                def load_group(tiles):
                    t0, nts, ng = geom(tiles)
                    for c8 in range(4):
                        DMA("sp", oT[:, c8 * 8:(c8 + 1) * 8, 0:ng],
                            src_d[c8 * 8:(c8 + 1) * 8, :, t0:t0 + ng].rearrange("c p t -> p c t"), [], [("oT", c8)])
                    if first:
                        for c8 in range(4):
                            which = 0 if c8 < 2 else 1
                            bc = rB[which][:, 0:ng].rearrange("p (o t) -> p o t", o=1).broadcast_to([128, 8, ng])
                            I("dve", "tensor_tensor", [("oT", c8), ("rB", which)], [("oT", c8)], out=oT[:, c8 * 8:(c8 + 1) * 8, 0:ng],
                              in0=oT[:, c8 * 8:(c8 + 1) * 8, 0:ng], in1=bc, op=ALU.mult)
                    else:
                        for j, tt in enumerate(tiles):
                            nt = nts[j]
                            r0 = tt * 128
                            DMA("sp", pst[0:nt, :], p_in[r0:r0 + nt, :], [], ["pst"])
                            I("dve", "tensor_copy", ["pst"], ["pbf"], out=pbf[0:nt, :], in_=pst[0:nt, :])
                            bk = 6 + j % 2
                            pvv = psb(bk).rearrange("p (j n) -> p j n", j=8)
                            for c in range(2):
                                I("pe", "transpose", ["pbf", "ident_bf"], [("ps", bk)], out=pvv[:, c, 0:nt],
                                  in_=pbf[0:nt, c * 128:(c + 1) * 128], identity=ident_bf[0:nt, 0:nt])
                            copy_on(evac_eng(), pT[:, :, j * 128:j * 128 + nt], pvv[:, 0:2, 0:nt], [("ps", bk)], [("pT", j)])

                def e_matmuls(tiles, ns):
                    t0, nts, ng = geom(tiles)
                    for j, tt in enumerate(tiles):
                        nt = nts[j]
                        eb = 6 + c3["e"] % 2
                        c3["e"] += 1
                        for c in range(2):
                            I("pe", "matmul", [("pT", j), ("wp", ns)], [("ps", eb)], PS[eb][0:nt, :], pT[:, c, j * 128:j * 128 + nt],
                              wp[ns][:, c, :], start=(c == 0), stop=(c == 1))
                        copy_on("act", e_sb[0:nt, j, :], PS[eb][0:nt, :], [("ps", eb)], [("e_sb", j)])

                def evac(tiles, cb, ns):
                    t0, nts, ng = geom(tiles)
                    if first:
                        for j, tt in enumerate(tiles):
                            copy_on("act" if j % 2 == 0 else "dve", msb[0:nts[j], j, cb * 512:(cb + 1) * 512], PS[j][0:nts[j], :],
                                    [("ps", j)], [("msb", j, cb)])
                    for j, tt in enumerate(tiles):
                        nt = nts[j]
                        cs = slice(cb * 512, (cb + 1) * 512)
                        if first:
                            I("dve", "scalar_tensor_tensor", [("msb", j, cb)], ["junk", ("stt", j, cb)], out=junk[0:nt, :], in0=msb[0:nt, j, cs],
                              scalar=1.0, in1=msb[0:nt, j, cs], op0=ALU.mult, op1=ALU.mult, accum_out=stt[0:nt, j, cb:cb + 1])
                            I("dve", "tensor_tensor", [("msb", j, cb), ("nrm", ns)], [("msb", j, cb)], out=msb[0:nt, j, cs],
                              in0=msb[0:nt, j, cs], in1=nrmr[ns][0:nt, :], op=ALU.mult)
                        else:
                            es = c3["t"] % 2
                            c3["t"] += 1
                            I("act", "activation", [("ps", j)], [("sg", es)], out=sg[es][0:nt, :], in_=PS[j][0:nt, :], func=AF.Sigmoid)
                            I("dve", "tensor_tensor", [("e_sb", j), ("sg", es)], ["tmp"], out=tmp[es][0:nt, :], in0=e_sb[0:nt, j, :],
                              in1=sg[es][0:nt, :], op=ALU.mult)
                            I("dve", "scalar_tensor_tensor", ["tmp"], [("sg", es), ("stt", j, cb)], out=sg[es][0:nt, :], in0=tmp[es][0:nt, :],
                              scalar=1.0, in1=tmp[es][0:nt, :], op0=ALU.mult, op1=ALU.mult, accum_out=stt[0:nt, j, cb:cb + 1])
                            I("dve", "tensor_tensor", ["tmp", ("nrm", ns)], [("msb", j, cb)], out=msb[0:nt, j, cs],
                              in0=tmp[es][0:nt, :], in1=nrmr[ns][0:nt, :], op=ALU.mult)

                def stats(tiles):
                    t0, nts, ng = geom(tiles)
                    for j, tt in enumerate(tiles):
                        nt = nts[j]
                        sk = [("stt", j, cb) for cb in range(8)]
                        I("dve", "tensor_reduce", sk, [("stt8", j)], out=stt[0:nt, j, 8:9], in_=stt[0:nt, j, 0:8], axis=mybir.AxisListType.X,
                          op=ALU.add)
                        I("dve", "tensor_scalar", [("stt8", j)], [("stt9", j)], out=stt[0:nt, j, 9:10], in0=stt[0:nt, j, 8:9],
                          scalar1=1.0 / D, scalar2=EPS, op0=ALU.mult, op1=ALU.add)
                        I("act", "activation", [("stt9", j)], [("stt10", j)], out=stt[0:nt, j, 10:11], in_=stt[0:nt, j, 9:10], func=AF.Ln)
                        I("act", "activation", [("stt10", j)], [("rst", j)], out=rst[0:nt, j:j + 1], in_=stt[0:nt, j, 10:11],
                          func=AF.Exp, scale=-0.5)

                def epiA(tiles, pc):
                    t0, nts, ng = geom(tiles)
                    n = len(tiles)
                    base = c3["x"]
                    c3["x"] += n
                    pcs = slice(pc * 1024, (pc + 1) * 1024)

                    def load(j):
                        xs = (base + j) % 2
                        r0 = tiles[j] * 128
                        DMA("sp", xr[xs][0:nts[j], :], res_src[r0:r0 + nts[j], pcs], [], [("xr", xs)])

                    load(0)
                    if n > 1:
                        load(1)
                    for j, tt in enumerate(tiles):
                        nt = nts[j]
                        r0 = tt * 128
                        xs = (base + j) % 2
                        mk = [("msb", j, 2 * pc), ("msb", j, 2 * pc + 1)]
                        I("dve", "scalar_tensor_tensor", mk + [("rst", j), ("xr", xs)], mk, out=msb[0:nt, j, pcs], in0=msb[0:nt, j, pcs],
                          scalar=rst[0:nt, j:j + 1], in1=xr[xs][0:nt, :], op0=ALU.mult, op1=ALU.add)
                        DMA("sp", res_dst[r0:r0 + nt, pcs], msb[0:nt, j, pcs], mk, [])
                        if j + 2 < n:
                            load(j + 2)

                def epiB(tiles, pc):
                    if not first:
                        return
                    t0, nts, ng = geom(tiles)
                    for j, tt in enumerate(tiles):
                        nt = nts[j]
                        r0 = tt * 128
                        xs = c3["t"] % 3
                        c3["t"] += 1
                        pcs = slice(pc * 1024, (pc + 1) * 1024)
                        mk = [("msb", j, 2 * pc), ("msb", j, 2 * pc + 1)]
                        copy_on("act", x1b[xs][0:nt, :], msb[0:nt, j, pcs], mk, [("x1b", xs)])
                        bk = 6 + xs % 2
                        pvv = psb(bk).rearrange("p (j n) -> p j n", j=8)
                        for c in range(8):
                            I("pe", "transpose", [("x1b", xs), "ident_bf"], [("ps", bk)], out=pvv[:, c, 0:nt],
                              in_=x1b[xs][0:nt, c * 128:(c + 1) * 128], identity=ident_bf[0:nt, 0:nt])
                        copy_on("act", x1Ts[xs % 2][:, :, 0:nt], pvv[:, :, 0:nt], [("ps", bk)], [("x1Ts", xs % 2)])
                        DMA("sp", x1T_d[pc * 8:(pc + 1) * 8, :, r0:r0 + nt].rearrange("c p t -> p c t"), x1Ts[xs % 2][:, :, 0:nt],
                            [("x1Ts", xs % 2)], [])

                prev = None
                rstd_bcast(GROUPS[0])
                load_group(GROUPS[0])
                for gi, tiles in enumerate(GROUPS):
                    t0, nts, ng = geom(tiles)
                    for cb in range(8):
                        ns = c3["n"] % 2
                        c3["n"] += 1
                        DMA("sp", nrmr[ns], nrm[cb * 512:(cb + 1) * 512].partition_broadcast(128), [], [("nrm", ns)])
                        if not first:
                            DMA("pool", wp[ns], w_proj[:, cb * 512:(cb + 1) * 512].rearrange("(c p) n -> p c n", p=128), [], [("wp", ns)])
                        if prev is not None:
                            if cb % 2 == 0:
                                epiB(prev, cb // 2)
                            elif cb // 2 + 1 < 4:
                                epiA(prev, cb // 2 + 1)
                        for fq in range(8):
                            slot = c3["w"] % NW3
                            c3["w"] += 1
                            DMA("pool", w3[slot], W[fq * 512:(fq + 1) * 512, cb * 512:(cb + 1) * 512].rearrange("(c p) n -> p c n", p=128),
                                [], [("w3", slot)])
                            order = [(f4, j) for f4 in range(4) for j in range(len(tiles))]
                            if fq == 0:
                                order = [(f4, j) for j in range(len(tiles)) for f4 in range(4)]
                            for f4, j in order:
                                fc = fq * 4 + f4
                                nt = nts[j]
                                I("pe", "matmul", [("w3", slot), ("oT", fc // 8)], [("ps", j)], PS[j][0:nt, :],
                                  oT[:, fc, j * 128:j * 128 + nt], w3[slot][:, f4, :], start=(fc == 0), stop=(fc == 31))
                            if (not first) and fq == 5:
                                e_matmuls(tiles, ns)
                        evac(tiles, cb, ns)
                        if cb == 5 and gi + 1 < len(GROUPS):
                            rstd_bcast(GROUPS[gi + 1])
                    if gi + 1 < len(GROUPS):
                        load_group(GROUPS[gi + 1])
                    stats(tiles)
                    epiA(tiles, 0)
                    prev = tiles
                for pc in range(4):
                    epiB(prev, pc)
                    if pc + 1 < 4:
                        epiA(prev, pc + 1)
                S.barrier()

            stage3(True)
            stage3(False)

        S.barrier()
        nsig = S.emit()
    return nc, nsig


_CACHE = {}


def _shard_inputs(inputs, c):
    f = np.float32
    g = lambda k: np.asarray(inputs[k])
    m = {
        "x": np.concatenate([g("x_prompt")[c], g("x_sample")[c]], axis=0).astype(f, copy=False),
        "p": np.concatenate([g("p_prompt")[0, c], g("p_sample")[0, c]], axis=0).astype(f, copy=False),
        "ck": np.ascontiguousarray(g("cache_k")[0, c].reshape(PAST, 2048)),
        "cv": np.ascontiguousarray(g("cache_v")[0, c].reshape(PAST, 2048)),
        "sconv": np.ascontiguousarray(g("state_conv")[0, c]),
        "sssd": np.ascontiguousarray(g("state_ssd")[0, c].reshape(2048, 128)),
        "w_in": np.ascontiguousarray(g("w_in")[0]),
        "w_out": np.ascontiguousarray(g("w_out")[0]),
        "w_gate": np.ascontiguousarray(g("w_ple_gate")[0]),
        "w_proj": np.ascontiguousarray(g("w_ple_proj")[0]),
    }
    for k in ("norm_pre", "norm_post", "ple_norm", "conv_b", "sb_norm", "ssd_norm", "dt_bias", "a_log", "d_skip"):
        m[k] = np.ascontiguousarray(g(k)[0].reshape(-1))
    m["conv_w"] = np.ascontiguousarray(g("conv_w")[0])
    return m


def kernel(**inputs):
    if "nc" not in _CACHE:
        _CACHE["nc"] = build_program()[0]
    nc = _CACHE["nc"]
    in_maps = [_shard_inputs(inputs, c) for c in range(N_CORES)]
    res = run_bass_kernel_spmd(nc, in_maps, core_ids=list(range(N_CORES)))
    R = res.results
    f = np.float32
    y = np.stack([r["y"] for r in R])
    ko = np.stack([r["k_out"] for r in R])
    vo = np.stack([r["v_out"] for r in R])
    co = np.stack([r["conv_out"] for r in R])
    so = np.stack([r["ssd_out"] for r in R])
    y_prompt = np.ascontiguousarray(y[:, :TP]).astype(f, copy=False)
    y_sample = np.ascontiguousarray(y[:, TP:]).astype(f, copy=False)
    k_prompt = np.ascontiguousarray(ko[:, :TP]).reshape(1, N_CORES, TP, 16, 128)
    v_prompt = np.ascontiguousarray(vo[:, :TP]).reshape(1, N_CORES, TP, 16, 128)
    k_sample = np.ascontiguousarray(ko[:, TP:]).reshape(1, N_CORES, TS, 16, 128)
    v_sample = np.ascontiguousarray(vo[:, TP:]).reshape(1, N_CORES, TS, 16, 128)
    conv_prompt = np.ascontiguousarray(co[:, 0:3]).reshape(1, N_CORES, 3, D)
    conv_sample = np.ascontiguousarray(co[:, 3:6]).reshape(1, N_CORES, 3, D)
    ssd_prompt = np.ascontiguousarray(so[:, 0:2048]).reshape(1, N_CORES, 32, 64, 128)
    ssd_sample = np.ascontiguousarray(so[:, 2048:4096]).reshape(1, N_CORES, 32, 64, 128)
    return (y_prompt, y_sample, k_prompt, v_prompt, conv_prompt, ssd_prompt, k_sample, v_sample, conv_sample, ssd_sample)
```
